# Optimizing a Trainium2 kernel written in Bass

```python
import math
import jax
import jax.numpy as jnp
from jax import lax
import numpy as np

D_MODEL = 1024
BATCH = 8
SEQ = 2048
DEPTH = 4
DEC_BATCH = 128
DEC_SEQ = 1
PAST_LEN = 16384
PAGE_SIZE = 128

D_INNER = D_MODEL
N_HEADS = 4
HEAD_DIM = D_INNER // N_HEADS
MCONV_W = 4
CCONV_W = 31
CHUNK = 64
N_SPLITS = 7
EPS = 1e-6

kernel_name = "hybrid_mlstm_conformer_conv_decode_step"


def rmsnorm(x, w):
    xf = x.astype(jnp.float32)
    r = lax.rsqrt(jnp.mean(xf * xf, axis=-1, keepdims=True) + EPS)
    return (xf * r).astype(x.dtype) * w


def layernorm(x, w, b):
    xf = x.astype(jnp.float32)
    mu = jnp.mean(xf, axis=-1, keepdims=True)
    var = jnp.mean(jnp.square(xf - mu), axis=-1, keepdims=True)
    y = ((xf - mu) * lax.rsqrt(var + EPS)).astype(x.dtype)
    return y * w + b


def causal_dwconv(x, buf, w, b):
    width = w.shape[0]
    xp = jnp.concatenate([buf.astype(x.dtype), x], axis=1)
    y = lax.conv_general_dilated(
        xp, w[:, None, :].astype(x.dtype), window_strides=(1,), padding='VALID',
        dimension_numbers=('NWC', 'WIO', 'NWC'), feature_group_count=x.shape[-1])
    new_buf = xp[:, xp.shape[1] - (width - 1):, :]
    return y + b, new_buf


def mlstm_chunkwise(q, k, v, i_pre, logf, C0, n0, m0):
    B, H, T, D = q.shape
    L = math.gcd(T, CHUNK)
    nc = T // L

    def to_chunks(a):
        a = a.astype(jnp.float32)
        return jnp.moveaxis(a.reshape(a.shape[:2] + (nc, L) + a.shape[3:]), 2, 0)

    qs, ks, vs, is_, fs = (to_chunks(a) for a in (q, k, v, i_pre, logf))
    causal = jnp.tril(jnp.ones((L, L), dtype=bool))

    def step(carry, inp):
        C, n, m = carry
        qc, kc, vc, ic, fc = inp
        F = jnp.cumsum(fc, axis=-1)
        Dlog = jnp.where(causal, F[..., :, None] - F[..., None, :] + ic[..., None, :], -jnp.inf)
        m_inter = F + m[..., None]
        m_t = jnp.maximum(m_inter, jnp.max(Dlog, axis=-1))
        S = jnp.einsum('bhld,bhsd->bhls', qc, kc) * jnp.exp(Dlog - m_t[..., None])
        decay = jnp.exp(m_inter - m_t)
        num = jnp.einsum('bhls,bhsd->bhld', S, vc) + decay[..., None] * jnp.einsum('bhld,bhde->bhle', qc, C)
        den = jnp.sum(S, axis=-1) + decay * jnp.einsum('bhld,bhd->bhl', qc, n)
        h = num / jnp.maximum(jnp.abs(den), jnp.exp(-m_t))[..., None]
        FL = F[..., -1]
        logw = FL[..., None] - F + ic
        m_new = jnp.maximum(FL + m, jnp.max(logw, axis=-1))
        a = jnp.exp(FL + m - m_new)
        ws = jnp.exp(logw - m_new[..., None])
        C_new = a[..., None, None] * C + jnp.einsum('bhl,bhld,bhle->bhde', ws, kc, vc)
        n_new = a[..., None] * n + jnp.einsum('bhl,bhld->bhd', ws, kc)
        return (C_new, n_new, m_new), h

    carry0 = (C0.astype(jnp.float32), n0.astype(jnp.float32), m0.astype(jnp.float32))
    (C, n, m), hs = lax.scan(step, carry0, (qs, ks, vs, is_, fs))
    h = jnp.moveaxis(hs, 0, 2).reshape(B, H, T, D)
    return h, C, n, m


def block(x, c, C0, n0, m0, mbuf0, cbuf0,
          w_ada, b_ada, norm_w, w_in, mconv_w, mconv_b, w_q, w_k, w_v, w_gate, b_gate,
          gn_w, skip, w_br_m, cconv_w, cconv_b, cln_w, cln_b, w_br_c, w_out):
    B, T, _ = x.shape
    mod = jax.nn.silu(c) @ w_ada + b_ada
    shift, scale, gate = jnp.split(mod[:, None, :], 3, axis=-1)
    h = rmsnorm(x, norm_w) * (1.0 + scale) + shift
    proj = h @ w_in
    xm, zm, ga, gb, zc, gm, gc = jnp.split(proj, N_SPLITS, axis=-1)

    xm_c, mbuf = causal_dwconv(xm, mbuf0, mconv_w, mconv_b)
    xm_c = jax.nn.silu(xm_c)
    xc_h = xm_c.reshape(B, T, N_HEADS, HEAD_DIM)
    xm_h = xm.reshape(B, T, N_HEADS, HEAD_DIM)
    q = jnp.einsum('bthd,hde->bthe', xc_h, w_q)
    k = jnp.einsum('bthd,hde->bthe', xc_h, w_k) * (HEAD_DIM ** -0.5)
    v = jnp.einsum('bthd,hde->bthe', xm_h, w_v)
    qkv = jnp.concatenate([q.reshape(B, T, D_INNER), k.reshape(B, T, D_INNER), v.reshape(B, T, D_INNER)], axis=-1)
    g = (qkv @ w_gate + b_gate).astype(jnp.float32)
    i_pre = jnp.swapaxes(g[..., :N_HEADS], 1, 2)
    logf = jnp.swapaxes(jax.nn.log_sigmoid(g[..., N_HEADS:]), 1, 2)
    hm, C, n, m = mlstm_chunkwise(q.transpose(0, 2, 1, 3), k.transpose(0, 2, 1, 3), v.transpose(0, 2, 1, 3),
                                  i_pre, logf, C0, n0, m0)
    hm = hm.transpose(0, 2, 1, 3)
    hf = hm - jnp.mean(hm, axis=-1, keepdims=True)
    hf = hf * lax.rsqrt(jnp.mean(hf * hf, axis=-1, keepdims=True) + EPS)
    hm = hf.reshape(B, T, D_INNER).astype(x.dtype) * gn_w
    hm = (hm + skip * xm_c) * jax.nn.silu(zm)
    y_m = hm @ w_br_m

    u = ga * jax.nn.sigmoid(gb)
    u, cbuf = causal_dwconv(u, cbuf0, cconv_w, cconv_b)
    u = jax.nn.silu(layernorm(u, cln_w, cln_b))
    u = u * jax.nn.silu(zc)
    y_c = u @ w_br_c

    merged = jax.nn.sigmoid(gm) * y_m + jax.nn.sigmoid(gc) * y_c
    x = x + gate * (merged @ w_out)
    return x, C.astype(C0.dtype), n.astype(n0.dtype), m.astype(m0.dtype), mbuf, cbuf


def setup_inputs(seed: int = 0) -> dict:
    key = jax.random.key(seed)
    ks = jax.random.split(key, 40)
    f32 = jnp.float32

    def nrm(k, shape, scale):
        return jax.random.normal(k, shape, f32) * scale

    L = DEPTH
    b_gate = jnp.concatenate([nrm(ks[30], (L, N_HEADS), 0.1),
                              3.0 + nrm(ks[31], (L, N_HEADS), 0.1)], axis=-1)
    return {
        "x_prompt": nrm(ks[0], (BATCH, SEQ, D_MODEL), 1.0),
        "x_sample": nrm(ks[1], (DEC_BATCH, DEC_SEQ, D_MODEL), 1.0),
        "c_prompt": nrm(ks[2], (BATCH, D_MODEL), 1.0),
        "c_sample": nrm(ks[3], (DEC_BATCH, D_MODEL), 1.0),
        "state_mlstm_C": nrm(ks[4], (L, DEC_BATCH, N_HEADS, HEAD_DIM, HEAD_DIM), 0.05),
        "state_mlstm_n": nrm(ks[5], (L, DEC_BATCH, N_HEADS, HEAD_DIM), 0.5),
        "state_mlstm_m": nrm(ks[6], (L, DEC_BATCH, N_HEADS), 1.0),
        "state_mlstm_conv": nrm(ks[7], (L, DEC_BATCH, MCONV_W - 1, D_INNER), 1.0),
        "state_conv": nrm(ks[8], (L, DEC_BATCH, CCONV_W - 1, D_INNER), 1.0),
        "w_ada": nrm(ks[9], (L, D_MODEL, 3 * D_MODEL), 0.5 * D_MODEL ** -0.5),
        "b_ada": nrm(ks[10], (L, 3 * D_MODEL), 0.02),
        "norm_w": 1.0 + nrm(ks[11], (L, D_MODEL), 0.02),
        "w_in": nrm(ks[12], (L, D_MODEL, N_SPLITS * D_INNER), D_MODEL ** -0.5),
        "mconv_w": nrm(ks[13], (L, MCONV_W, D_INNER), MCONV_W ** -0.5),
        "mconv_b": nrm(ks[14], (L, D_INNER), 0.02),
        "w_q": nrm(ks[15], (L, N_HEADS, HEAD_DIM, HEAD_DIM), HEAD_DIM ** -0.5),
        "w_k": nrm(ks[16], (L, N_HEADS, HEAD_DIM, HEAD_DIM), HEAD_DIM ** -0.5),
        "w_v": nrm(ks[17], (L, N_HEADS, HEAD_DIM, HEAD_DIM), HEAD_DIM ** -0.5),
        "w_gate": nrm(ks[18], (L, 3 * D_INNER, 2 * N_HEADS), (3 * D_INNER) ** -0.5),
        "b_gate": b_gate,
        "gn_w": 1.0 + nrm(ks[19], (L, D_INNER), 0.02),
        "skip": 1.0 + nrm(ks[20], (L, D_INNER), 0.02),
        "w_br_m": nrm(ks[21], (L, D_INNER, D_MODEL), D_INNER ** -0.5),
        "cconv_w": nrm(ks[22], (L, CCONV_W, D_INNER), CCONV_W ** -0.5),
        "cconv_b": nrm(ks[23], (L, D_INNER), 0.02),
        "cln_w": 1.0 + nrm(ks[24], (L, D_INNER), 0.02),
        "cln_b": nrm(ks[25], (L, D_INNER), 0.02),
        "w_br_c": nrm(ks[26], (L, D_INNER, D_MODEL), D_INNER ** -0.5),
        "w_out": nrm(ks[27], (L, D_MODEL, D_MODEL), D_MODEL ** -0.5),
        "final_norm_w": 1.0 + nrm(ks[28], (D_MODEL,), 0.02),
    }


def reference(x_prompt, x_sample, c_prompt, c_sample,
              state_mlstm_C, state_mlstm_n, state_mlstm_m, state_mlstm_conv, state_conv,
              w_ada, b_ada, norm_w, w_in, mconv_w, mconv_b, w_q, w_k, w_v, w_gate, b_gate,
              gn_w, skip, w_br_m, cconv_w, cconv_b, cln_w, cln_b, w_br_c, w_out, final_norm_w):
    Bp = x_prompt.shape[0]
    dt = x_prompt.dtype
    C0p = jnp.zeros((Bp, N_HEADS, HEAD_DIM, HEAD_DIM), jnp.float32)
    n0p = jnp.zeros((Bp, N_HEADS, HEAD_DIM), jnp.float32)
    m0p = jnp.zeros((Bp, N_HEADS), jnp.float32)
    mb0p = jnp.zeros((Bp, MCONV_W - 1, D_INNER), dt)
    cb0p = jnp.zeros((Bp, CCONV_W - 1, D_INNER), dt)

    xp, xs = x_prompt, x_sample
    Cp, np_, mp, mbp, cbp = [], [], [], [], []
    Cs, ns, ms, mbs, cbs = [], [], [], [], []
    for l in range(DEPTH):
        lw = (w_ada[l], b_ada[l], norm_w[l], w_in[l], mconv_w[l], mconv_b[l], w_q[l], w_k[l], w_v[l],
              w_gate[l], b_gate[l], gn_w[l], skip[l], w_br_m[l], cconv_w[l], cconv_b[l],
              cln_w[l], cln_b[l], w_br_c[l], w_out[l])
        xp, C1, n1, m1, mb1, cb1 = block(xp, c_prompt, C0p, n0p, m0p, mb0p, cb0p, *lw)
        Cp.append(C1); np_.append(n1); mp.append(m1); mbp.append(mb1); cbp.append(cb1)
        xs, C2, n2, m2, mb2, cb2 = block(xs, c_sample, state_mlstm_C[l], state_mlstm_n[l], state_mlstm_m[l],
                                         state_mlstm_conv[l], state_conv[l], *lw)
        Cs.append(C2); ns.append(n2); ms.append(m2); mbs.append(mb2); cbs.append(cb2)

    y_prompt = rmsnorm(xp, final_norm_w)
    y_sample = rmsnorm(xs, final_norm_w)
    return (y_prompt, y_sample,
            jnp.stack(Cp), jnp.stack(np_), jnp.stack(mp), jnp.stack(mbp), jnp.stack(cbp),
            jnp.stack(Cs), jnp.stack(ns), jnp.stack(ms), jnp.stack(mbs), jnp.stack(cbs))
```

```python
import contextlib
import numpy as np
import concourse.bass as bass
import concourse.mybir as mybir
from concourse.bass_utils import run_bass_kernel_spmd

F32 = mybir.dt.float32
BF16 = mybir.dt.bfloat16
AF = mybir.ActivationFunctionType
ALU = mybir.AluOpType
AX = mybir.AxisListType

P = 128
D = 1024
KC = 8
H = 4
DH = 256
T = 512
LC = 128
MW = 4
CW = 31
EPS = 1e-6
NV = 46
V_NORM, V_MCB, V_GN, V_SKIP, V_CCB, V_CLNW, V_CLNB, V_BADA, V_MCW, V_CCW, V_FIN = 0, 1, 2, 3, 4, 5, 6, 7, 10, 14, 45


class Buf:
    __slots__ = ("w", "r")

    def __init__(self):
        self.w = None
        self.r = {}


class TB:
    def __init__(self, t, nb=1):
        self.t = t
        self.b = [Buf() for _ in range(nb)]

    @property
    def all(self):
        return self.b


class Rec:
    def __getattr__(self, name):
        def f(*a, **kw):
            self.call = (name, a, kw)
            return self
        return f


def _bind(fn):
    r = Rec()
    fn(r)
    name, a, kw = r.call
    return lambda eng: getattr(eng, name)(*a, **kw)


class Sched:
    def __init__(self, nc, stack):
        self.nc = nc
        self.engs = ["pe", "act", "dve", "pool", "sp"]
        self.prog = {e: [] for e in self.engs}
        self.semh = {}
        self.cnt = {}
        for e in ["pe", "act", "dve", "pool"]:
            self.semh[e] = stack.enter_context(nc.semaphore("sem_" + e))
            self.cnt[e] = 0
        self.KD = 8
        self.dq = {}
        for q in ["sp", "pool", "act"]:
            lst = []
            for i in range(self.KD):
                key = f"d_{q}{i}"
                self.semh[key] = stack.enter_context(nc.semaphore(key))
                self.cnt[key] = 0
                lst.append(key)
            self.dq[q] = [lst, 0]
        self.known = {e: {} for e in self.engs}
        self.pe_pending = False

    def _wait(self, e, key, val):
        if self.known[e].get(key, 0) >= val:
            return
        self.known[e][key] = val
        h = self.semh[key]
        self.prog[e].append(lambda eng, h=h, val=val: eng.wait_ge(h, val))

    def _deps(self, e, reads, writes):
        deps = {}
        for b in reads:
            if b.w:
                deps[b.w[0]] = max(deps.get(b.w[0], 0), b.w[1])
        for b in writes:
            if b.w:
                deps[b.w[0]] = max(deps.get(b.w[0], 0), b.w[1])
            for k, v in b.r.items():
                deps[k] = max(deps.get(k, 0), v)
        for k, v in deps.items():
            if k == "pe" and e == "pe":
                continue
            self._wait(e, k, v)

    def _mark(self, tok, reads, writes):
        for b in writes:
            b.w = tok
            b.r = {}
        for b in reads:
            b.r[tok[0]] = max(b.r.get(tok[0], 0), tok[1])

    def op(self, e, fn, reads=(), writes=(), sig=True):
        fn = _bind(fn)
        self._deps(e, reads, writes)
        if e == "pe" and not sig:
            tick = self.cnt[e] + 1
            self.pe_pending = True
            self.prog[e].append(lambda eng, fn=fn: fn(eng))
        else:
            self.cnt[e] += 1
            tick = self.cnt[e]
            h = self.semh[e]
            if e == "pe":
                self.pe_pending = False
            self.prog[e].append(lambda eng, fn=fn, h=h: fn(eng).then_inc(h, 1))
        self._mark((e, tick), reads, writes)

    def dma(self, q, fn, reads=(), writes=()):
        fn = _bind(fn)
        lst, i = self.dq[q]
        key = lst[i % self.KD]
        self.dq[q][1] = i + 1
        if self.cnt[key] > 0:
            self._wait(q, key, self.cnt[key])
        self._deps(q, reads, writes)
        self.cnt[key] += 16
        val = self.cnt[key]
        h = self.semh[key]
        self.prog[q].append(lambda eng, fn=fn, h=h: fn(eng).then_inc(h, 16))
        self._mark((key, val), reads, writes)

    def finish(self):
        assert not self.pe_pending
        for key, c in self.cnt.items():
            if c > 0:
                self._wait("sp", key, c)

    def emit(self):
        nc = self.nc
        prog = self.prog
        with nc.Block() as block:

            @block.tensor
            def _(e):
                for f in prog["pe"]:
                    f(e)

            @block.scalar
            def _(e):
                for f in prog["act"]:
                    f(e)

            @block.vector
            def _(e):
                for f in prog["dve"]:
                    f(e)

            @block.gpsimd
            def _(e):
                for f in prog["pool"]:
                    f(e)

            @block.sync
            def _(e):
                for f in prog["sp"]:
                    f(e)


class Ring:
    def __init__(self, items):
        self.items = items
        self.i = 0

    def next(self):
        it = self.items[self.i % len(self.items)]
        self.i += 1
        return it


def bcast(ap, shape):
    return ap.broadcast_to(list(shape))


def build(SEQ, DEPTH, NS):
    nc = bass.Bass("TRN2", target_bir_lowering=False)
    NT = SEQ // T
    NTOK = 1 + NS

    def din(name, shape):
        return nc.dram_tensor(name, list(shape), F32, kind="ExternalInput").ap()

    def dout(name, shape):
        return nc.dram_tensor(name, list(shape), F32, kind="ExternalOutput").ap()

    xp = din("xp", [SEQ, D])
    xs = din("xs", [NS, D])
    call = din("call", [NTOK, D])
    sC = din("sC", [DEPTH, NS, H, DH, DH])
    sn = din("sn", [DEPTH, NS, H * DH])
    sm = din("sm", [DEPTH, NS, H])
    smc = din("smc", [DEPTH, NS * (MW - 1), D])
    scv = din("scv", [DEPTH, NS * (CW - 1), D])
    w_ada = din("w_ada", [DEPTH, D, 3 * D])
    w_in = din("w_in", [DEPTH, D, 7 * D])
    w_q = din("w_q", [DEPTH, H, DH, DH])
    w_k = din("w_k", [DEPTH, H, DH, DH])
    w_v = din("w_v", [DEPTH, H, DH, DH])
    w_gate = din("w_gate", [DEPTH, 3 * D, 8])
    b_gate = din("b_gate", [DEPTH, 8])
    w_br_m = din("w_br_m", [DEPTH, D, D])
    w_br_c = din("w_br_c", [DEPTH, D, D])
    w_out = din("w_out", [DEPTH, D, D])
    vecs = din("vecs", [DEPTH, NV, D])
    consts = din("consts", [P, 4 * P + 256])

    yp = dout("yp", [SEQ, D])
    ys = dout("ys", [NS, D])
    oCp = dout("oCp", [DEPTH, H, DH, DH])
    onp = dout("onp", [DEPTH, H * DH])
    omp = dout("omp", [DEPTH, H])
    omcp = dout("omcp", [DEPTH, MW - 1, D])
    ocvp = dout("ocvp", [DEPTH, CW - 1, D])
    oCs = dout("oCs", [DEPTH, NS, H, DH, DH])
    ons = dout("ons", [DEPTH, NS, H * DH])
    oms = dout("oms", [DEPTH, NS, H])
    omcs = dout("omcs", [DEPTH, NS * (MW - 1), D])
    ocvs = dout("ocvs", [DEPTH, NS * (CW - 1), D])

    stack = contextlib.ExitStack()
    with stack:
        S = Sched(nc, stack)
        uid = [0]

        def barrier():
            snap = dict(S.cnt)
            for e in S.engs:
                for key, c in snap.items():
                    if c > 0 and key != e:
                        S._wait(e, key, c)


        pst = contextlib.ExitStack()

        def sb(shape, dt, nb=1, st=stack):
            uid[0] += 1
            t = st.enter_context(nc.sbuf_tensor(f"t{uid[0]}", list(shape), dt))
            return TB(t, nb)

        def ps(shape, dt):
            uid[0] += 1
            t = stack.enter_context(nc.psum_tensor(f"p{uid[0]}", list(shape), dt))
            return TB(t, 1)

        psA = Ring([ps([P, 512], F32) for _ in range(2)])
        _po = [ps([P, 1024], F32) for _ in range(2)]
        for _t in _po:
            _t.b = [Buf(), Buf()]
        psOr = Ring(_po)
        _halves = []
        for _t in _po:
            for _i in range(2):
                _h = TB(_t.t[:, _i * 512:(_i + 1) * 512], 1)
                _h.b = [_t.b[_i]]
                _halves.append(_h)
        psA6 = Ring(list(psA.items) + _halves)
        psT = ps([P, 1024], BF16)
        psS = ps([P, 512], F32)

        cst = sb([P, 4 * P + 256], F32)
        cbf = sb([P, 3 * P], BF16)
        epsc = sb([P, 1], F32)
        vcol = sb([P, DEPTH, KC, NV], F32)
        NWB = 4
        wring = Ring([sb([P, KC, 256], BF16) for _ in range(NWB)])
        modall = sb([P, DEPTH, 24, NTOK], F32)
        Acol = sb([P, DEPTH, KC, NTOK], F32)
        n32 = sb([P, DEPTH, H, 2], F32, nb=DEPTH)
        mst = sb([H, DEPTH], F32, nb=DEPTH)
        hxm = sb([P, DEPTH, KC, MW - 1], BF16, nb=DEPTH)
        hu = sb([P, DEPTH, KC, CW - 1], BF16, nb=DEPTH)
        vstg = sb([NV, D], F32, 1, pst)
        cstg = sb([NTOK, D], F32, 1, pst)
        csl = sb([NTOK, D], F32, 1, pst)
        scT = sb([P, KC, NTOK], BF16, KC, pst)
        S.dma("sp", lambda e: e.dma_start(out=cst.t[:], in_=consts[:, :]), writes=cst.all)
        ident = cst.t[:, 0:P]
        triu = cst.t[:, P:2 * P]
        ones = cst.t[:, 2 * P:3 * P]
        ntri = cst.t[:, 3 * P:4 * P]
        identb = cbf.t[:, 0:P]
        maskb = cbf.t[:, P:2 * P]
        S.op("dve", lambda e: e.tensor_copy(out=cbf.t[:], in_=cst.t[:, 0:3 * P]), reads=cst.all, writes=cbf.all)
        onesb = cbf.t[:, 2 * P:3 * P]
        CST = cst.all + cbf.all
        S.op("dve", lambda e: e.memset(epsc.t[:], EPS), writes=epsc.all)

        for l in range(DEPTH):
            S.dma("sp", lambda e, l=l: e.dma_start(out=vstg.t[:], in_=vecs[l, :, :]), writes=vstg.all)
            for half in range(2):
                pa = psA.next()
                for kk in range(4):
                    k = half * 4 + kk
                    S.op("pe", lambda e, pa=pa, k=k, kk=kk: e.transpose(out=pa.t[:, kk * NV:(kk + 1) * NV], in_=vstg.t[:, k * P:(k + 1) * P], identity=ident[0:NV, 0:NV]),
                         reads=vstg.all + CST, writes=pa.all, sig=(kk == 3))
                S.op("dve", lambda e, pa=pa, l=l, half=half: e.tensor_copy(out=vcol.t[:, l, half * 4:(half + 1) * 4, :], in_=pa.t[:, 0:4 * NV].rearrange("p (k v) -> p k v", v=NV)),
                     reads=pa.all, writes=vcol.all)

        def vc(l, k, v):
            return vcol.t[:, l, k, v:v + 1]


        def wload(src3, kk, cols):
            w = wring.next()
            S.dma("pool", lambda e, w=w: e.dma_start(out=w.t[:, 0:kk, 0:cols], in_=src3), writes=w.all)
            return w

        def wmat(wd, f0, cols):
            return wload(wd.rearrange("(k p) f -> p k f", p=P)[:, :, f0:f0 + cols], KC, cols)

        def proj(wd, f0, nfc, src, N, evac, src_cols=None):
            for _ in proj_gen(wd, f0, nfc, src, N, evac, src_cols, ring=psA6):
                pass

        def proj_gen(wd, f0, nfc, src, N, evac, src_cols=None, ring=None):
            ring = ring or psA
            done = 0
            while done < nfc:
                n = min(2, nfc - done)
                w = wmat(wd, f0 + done * P, n * P)
                for c in range(n):
                    pa = ring.next()
                    for k in range(KC):
                        rhs = src.t[:, k, 0:N] if src_cols is None else src.t[:, k, src_cols[0]:src_cols[1]]
                        S.op("pe", lambda e, pa=pa, w=w, c=c, k=k, rhs=rhs: e.matmul(out=pa.t[:, 0:N], lhsT=w.t[:, k, c * P:(c + 1) * P], rhs=rhs, start=(k == 0), stop=(k == KC - 1)),
                             reads=w.all + [src.b[k]], writes=pa.all, sig=(k == KC - 1))
                    evac(done + c, pa)
                done += n
                yield

        S.dma("sp", lambda e: e.dma_start(out=cstg.t[:], in_=call[:, :]), writes=cstg.all)
        S.op("act", lambda e: e.activation(out=csl.t[:], in_=cstg.t[:], func=AF.Silu), reads=cstg.all, writes=csl.all)
        for half in range(2):
            pa = psA.next()
            for kk in range(4):
                k = half * 4 + kk
                S.op("pe", lambda e, pa=pa, k=k, kk=kk: e.transpose(out=pa.t[:, kk * NTOK:(kk + 1) * NTOK], in_=csl.t[:, k * P:(k + 1) * P], identity=ident[0:NTOK, 0:NTOK]),
                     reads=csl.all + CST, writes=pa.all, sig=(kk == 3))
            S.op("dve", lambda e, pa=pa, half=half: e.tensor_copy(out=scT.t[:, half * 4:(half + 1) * 4, :], in_=pa.t[:, 0:4 * NTOK].rearrange("p (k v) -> p k v", v=NTOK)),
                 reads=pa.all, writes=scT.b[half * 4:(half + 1) * 4])
        for l in range(DEPTH):
            def ev(fc, pa, l=l):
                S.op("act", lambda e: e.activation(out=modall.t[:, l, fc, :], in_=pa.t[:, 0:NTOK], func=AF.Identity, bias=vc(l, fc % KC, V_BADA + fc // KC), scale=1.0),
                     reads=pa.all + vcol.all, writes=modall.all)
            proj(w_ada[l], 0, 24, scT, NTOK, ev)
            S.op("dve", lambda e, l=l: e.scalar_tensor_tensor(out=Acol.t[:, l, :, :], in0=modall.t[:, l, 8:16, :], scalar=1.0, in1=bcast(vcol.t[:, l, :, V_NORM:V_NORM + 1], [P, KC, NTOK]), op0=ALU.add, op1=ALU.mult),
                 reads=modall.all + vcol.all, writes=Acol.all)
        MOD = modall.all + Acol.all + vcol.all
        barrier()
        pst.close()
        dC = [Buf() for _ in range(DEPTH)]

        for tb in (n32, mst, hxm, hu):
            S.op("dve", lambda e, tb=tb: e.memset(tb.t[:], 0.0), writes=tb.all)

        def run_group(N, sample, tiles, st):
            def A(shape, dt, nb=1):
                return sb(shape, dt, nb, st)

            xT = A([P, KC, N], F32, KC)
            hT = A([P, KC, N], BF16, KC)
            XU = A([P, KC, CW - 1 + N], BF16, KC)
            xm = TB(XU.t, 1); xm.b = XU.b
            uu = XU
            XO = CW - MW
            szm = A([P, KC, N], BF16, KC)
            sgm = A([P, KC, N], BF16, KC)
            xc = A([P, KC, N], BF16, KC)
            qT = A([P, KC, N], BF16, KC)
            kT = A([P, KC, N], BF16, KC)
            vT = A([P, 2, N], BF16, 1)
            NB = (N + LC - 1) // LC
            RB = LC if not sample else NS
            ktok = A([LC, NB, D], BF16, NB)
            hftok = ktok
            szc = szm
            sgc = sgm
            ymg = xc
            if not sample:
                big = A([P, KC * N * 2], BF16, KC)
                vtok = TB(None, 1); vtok.b = big.b[0:NB]
                vtok_v = big.t[0:LC, 0:NB * D].rearrange("p (b d) -> p b d", d=D)
                uc32 = TB(None, 1); uc32.b = big.b
                uc32_v = big.t[:].bitcast(F32).rearrange("p (k n) -> p k n", n=N)
            else:
                vtok = A([LC, NB, D], BF16, NB)
                vtok_v = vtok.t[:]
                uc32 = A([P, KC, N], F32, KC)
                uc32_v = uc32.t[:]
            hmg = qT
            sgb = qT if sample else A([P, KC, N], BF16, KC)
            ucg = kT
            mrg = hT
            tmpf = Ring([A([P, N], F32) for _ in range(4)])
            sq = tmpf
            tmpb = Ring([A([P, N], BF16) for _ in range(2)])
            sqb = Ring([A([P, N], BF16) for _ in range(2)])
            rstd = A([P, N], F32)
            meanT = A([P, N], F32)
            dgm = Ring([A([P, MW, P], BF16) for _ in range(2)])
            dgc = Ring([A([P, CW, P], BF16) for _ in range(2)])
            stg = Ring([A([P, D], F32) for _ in range(1 if not sample else 2)])
            gsb = A([LC, NB, 8], F32)
            bgt = A([LC, 8], F32)
            gt = [A([LC, NB, 4], F32) for _ in range(4)]
            if not sample:
                C32 = A([P, H, 2, DH], F32, H)
                Cb = A([P, H, 2, DH], BF16, H)
                nb_ = A([P, H, 2], BF16)
                rowA = A([H, N], F32)
                rowF = A([H, N], F32)
                sm8 = [A([H, NB], F32) for _ in range(6)]
                adg = A([H, NB, H], F32)
                abc = A([P, NB, H], F32)
                ctk = A([LC, NB, 8], F32)
                cbk = A([LC, NB, H], BF16)
                Sm = Ring([A([LC, H, LC], BF16) for _ in range(NB)])
                st6 = A([LC, H, 6], F32)
                mv = A([LC, H, 2], F32)
                smls = [[A([LC, H], F32) for _ in range(5)] for _ in range(2)]
                st6s = [A([LC, H, 6], F32) for _ in range(2)]
                mvs = [A([LC, H, 2], F32) for _ in range(2)]
            else:
                hxmS = A([P, KC, MW - 1, NS], BF16, KC)
                huS = A([P, KC, CW - 1, NS], BF16, KC)
                qtok = A([NS, D], F32)
                qTf = A([P, KC, NS], F32, KC)
                qm = A([P, KC, NS, NS], BF16, KC)
                kexp = A([NS, 4, D], BF16)
                Cin = Ring([A([P, H, 2, DH], F32) for _ in range(3)])
                Cbf = Ring([A([P, H, 2, DH], BF16) for _ in range(2)])
                Cout = Ring([A([P, H, 2, DH], F32) for _ in range(2)])
                nst = A([NS, H, DH], F32)
                nnew = A([NS, H, DH], F32)
                mprev = A([NS, H], F32)
                s16 = [A([NS, H], F32) for _ in range(10)]
                dexp = A([NS, NS, H], F32)
                dbc = A([P, NS, H], F32)
                prod = A([NS, D], F32)
                numt = A([NS, D], F32)
                st6 = A([NS, H, 6], F32)
                mv = A([NS, H, 2], F32)

            for _nm, _val in list(locals().items()):
                if isinstance(_val, TB) and _val.t is not None:
                    DBG[(sample, _nm)] = _val.t.name

            def modv(l, which, k):
                if not sample:
                    return modall.t[:, l, which * 8 + k, 0:1]
                return modall.t[:, l, which * 8 + k, 1:NTOK]

            def Av(l, k):
                if not sample:
                    return Acol.t[:, l, k, 0:1]
                return Acol.t[:, l, k, 1:NTOK]

            rms_pending = []

            def rms_flush():
                while rms_pending:
                    k, s_ = rms_pending.pop(0)
                    S.op("pe", lambda e: e.matmul(out=psS.t[:, 0:N], lhsT=onesb, rhs=s_.t[:], start=(k == 0), stop=(k == KC - 1)),
                         reads=s_.all + CST, writes=psS.all, sig=True)

            def rms_chunk(k, defer=False):
                rms_flush()
                s_ = sqb.next()
                S.op("act", lambda e: e.activation(out=s_.t[:], in_=xT.t[:, k, :], func=AF.Square), reads=[xT.b[k]], writes=s_.all)
                rms_pending.append((k, s_))
                if not defer:
                    rms_flush()

            def rms_finish():
                S.op("act", lambda e: e.activation(out=rstd.t[:], in_=psS.t[:, 0:N], func=AF.Sqrt, bias=epsc.t[:, 0:1], scale=1.0 / D), reads=psS.all + epsc.all, writes=rstd.all)
                S.op("dve", lambda e: e.reciprocal(out=rstd.t[:], in_=rstd.t[:]), reads=rstd.all, writes=rstd.all)

            def build_diag(l, W, vbase, ring, k):
                dg = ring.next()
                S.op("dve", lambda e: e.tensor_tensor(out=dg.t[:], in0=bcast(identb.unsqueeze(1), [P, W, P]), in1=bcast(vcol.t[:, l, k, vbase:vbase + W].unsqueeze(2), [P, W, P]), op=ALU.mult),
                     reads=CST + vcol.all, writes=dg.all)
                return dg

            def dwconv(l, W, vbase, ring, src_of, evac, pre=()):
                for k in range(KC):
                    dg = pre[k] if k < len(pre) else build_diag(l, W, vbase, ring, k)
                    pa = psA6.next()
                    for j in range(W):
                        rhs, rb = src_of(k, j)
                        S.op("pe", lambda e, dg=dg, j=j, pa=pa, rhs=rhs: e.matmul(out=pa.t[:, 0:N], lhsT=dg.t[:, j, :], rhs=rhs, start=(j == 0), stop=(j == W - 1)),
                             reads=dg.all + rb, writes=pa.all, sig=(j == W - 1))
                    evac(k, pa)

            def store_rows(src_ap_of_k, src_bufs, R, dram_rows, bf):
                s_ = stg.next()
                for half in range(2):
                    if bf:
                        po = psT
                    else:
                        po = psA.next()
                    for kk in range(4):
                        k = half * 4 + kk
                        S.op("pe", lambda e, po=po, k=k, kk=kk: e.transpose(out=po.t[0:R, kk * P:(kk + 1) * P], in_=src_ap_of_k(k), identity=(identb if bf else ident)),
                             reads=src_bufs + CST, writes=po.all, sig=(kk == 3))
                    S.op("dve", lambda e, po=po, half=half, s_=s_: e.tensor_copy(out=s_.t[0:R, half * 512:(half + 1) * 512], in_=po.t[0:R, 0:512]), reads=po.all, writes=s_.all)
                S.dma("sp", lambda e, s_=s_: e.dma_start(out=dram_rows, in_=s_.t[0:R, :]), reads=s_.all)

            def load_rows(dram_rows, R, writer):
                s_ = stg.next()
                S.dma("sp", lambda e, s_=s_: e.dma_start(out=s_.t[0:R, :], in_=dram_rows), writes=s_.all)
                nper = max(1, min(4, 512 // R))
                k = 0
                while k < KC:
                    n = min(nper, KC - k)
                    pa = psA.next()
                    for kk in range(n):
                        S.op("pe", lambda e, pa=pa, k=k, kk=kk, s_=s_: e.transpose(out=pa.t[:, kk * R:(kk + 1) * R], in_=s_.t[0:R, (k + kk) * P:(k + kk + 1) * P], identity=ident[0:R, 0:R]),
                             reads=s_.all + CST, writes=pa.all, sig=(kk == n - 1))
                    writer(k, n, pa)
                    k += n

            def layer(l, ti):
                last_tile = (ti == NT - 1)
                rms_finish()
                for k in range(KC):
                    tf = tmpf.next()
                    if not sample:
                        S.op("dve", lambda e, tf=tf, k=k: e.scalar_tensor_tensor(out=tf.t[:], in0=xT.t[:, k, :], scalar=Av(l, k), in1=rstd.t[:], op0=ALU.mult, op1=ALU.mult),
                             reads=[xT.b[k]] + rstd.all + MOD, writes=tf.all)
                        S.op("act", lambda e, tf=tf, k=k: e.activation(out=hT.t[:, k, :], in_=tf.t[:], func=AF.Identity, bias=modv(l, 0, k), scale=1.0),
                             reads=tf.all + MOD, writes=[hT.b[k]])
                    else:
                        S.op("dve", lambda e, tf=tf, k=k: e.tensor_tensor(out=tf.t[:], in0=xT.t[:, k, :], in1=rstd.t[:], op=ALU.mult), reads=[xT.b[k]] + rstd.all, writes=tf.all)
                        S.op("dve", lambda e, tf=tf, k=k: e.tensor_tensor(out=tf.t[:], in0=tf.t[:], in1=Av(l, k), op=ALU.mult), reads=tf.all + MOD, writes=tf.all)
                        S.op("dve", lambda e, tf=tf, k=k: e.tensor_tensor(out=hT.t[:, k, :], in0=tf.t[:], in1=modv(l, 0, k), op=ALU.add), reads=tf.all + MOD, writes=[hT.b[k]])

                if not sample:
                    S.op("dve", lambda e: e.tensor_copy(out=XU.t[:, :, XO:XO + MW - 1], in_=hxm.t[:, l, :, :]), reads=[hxm.b[l]], writes=xm.all)
                else:
                    def wr_m(k, n, pa):
                        S.op("dve", lambda e: e.tensor_copy(out=hxmS.t[:, k:k + n, :, :].rearrange("p k j b -> p k b j"), in_=pa.t[:, 0:n * NS * (MW - 1)].rearrange("p (k b j) -> p k b j", k=n, b=NS)),
                             reads=pa.all, writes=hxmS.b[k:k + n])
                    load_rows(smc[l, :, :], NS * (MW - 1), wr_m)
                    SB4 = 4
                    for b0 in range(0, NS, SB4):
                        def wr_c(k, n, pa, b0=b0):
                            S.op("dve", lambda e: e.tensor_copy(out=huS.t[:, k:k + n, :, b0:b0 + SB4].rearrange("p k j b -> p k b j"), in_=pa.t[:, 0:n * SB4 * (CW - 1)].rearrange("p (k b j) -> p k b j", k=n, b=SB4)),
                                 reads=pa.all, writes=huS.b[k:k + n])
                        load_rows(scv[l, b0 * (CW - 1):(b0 + SB4) * (CW - 1), :], SB4 * (CW - 1), wr_c)
                    S.dma("sp", lambda e: e.dma_start(out=omcs[l].rearrange("(b j) d -> b j d", j=MW - 1)[:, 0:MW - 2, :], in_=smc[l].rearrange("(b j) d -> b j d", j=MW - 1)[:, 1:MW - 1, :]))
                    S.dma("sp", lambda e: e.dma_start(out=ocvs[l].rearrange("(b j) d -> b j d", j=CW - 1)[:, 0:CW - 2, :], in_=scv[l].rearrange("(b j) d -> b j d", j=CW - 1)[:, 1:CW - 1, :]))

                def ev_xm(fc, pa):
                    S.op("dve", lambda e: e.tensor_copy(out=XU.t[:, fc, CW - 1:CW - 1 + N], in_=pa.t[:, 0:N]), reads=pa.all, writes=[xm.b[fc]])
                proj(w_in[l], 0 * D, KC, hT, N, ev_xm)
                pre_c = [build_diag(l, CW, V_CCW, dgc, 0), build_diag(l, CW, V_CCW, dgc, 1)]

                def ev_zm(fc, pa):
                    S.op("act", lambda e: e.activation(out=szm.t[:, fc, :], in_=pa.t[:, 0:N], func=AF.Silu), reads=pa.all, writes=[szm.b[fc]])

                def src_m(k, j):
                    if not sample:
                        return XU.t[:, k, XO + j:XO + j + N], [xm.b[k]]
                    if j < MW - 1:
                        return hxmS.t[:, k, j, :], [hxmS.b[k]]
                    return XU.t[:, k, CW - 1:CW - 1 + N], [xm.b[k]]

                def ev_mc(k, pa):
                    S.op("act", lambda e: e.activation(out=xc.t[:, k, :], in_=pa.t[:, 0:N], func=AF.Silu, bias=vc(l, k, V_MCB), scale=1.0), reads=pa.all + vcol.all, writes=[xc.b[k]])
                dwconv(l, MW, V_MCW, dgm, src_m, ev_mc)

                def ev_gm(fc, pa):
                    S.op("act", lambda e: e.activation(out=sgm.t[:, fc, :], in_=pa.t[:, 0:N], func=AF.Sigmoid), reads=pa.all, writes=[sgm.b[fc]])

                if not sample:
                    S.op("dve", lambda e: e.tensor_copy(out=hxm.t[:, l, :, :], in_=XU.t[:, :, XO + N:XO + N + MW - 1]), reads=xm.all, writes=[hxm.b[l]])
                    if last_tile:
                        store_rows(lambda k: hxm.t[:, l, k, :], [hxm.b[l]], MW - 1, omcp[l, :, :], True)
                else:
                    store_rows(lambda k: XU.t[:, k, CW - 1:CW - 1 + N], xm.all, NS, omcs[l].rearrange("(b j) d -> b j d", j=MW - 1)[:, MW - 2, :], True)

                wq = wload(w_q[l].rearrange("h (k p) e -> p (h k) e", p=P), KC, DH)
                wk = wload(w_k[l].rearrange("h (k p) e -> p (h k) e", p=P), KC, DH)
                wv = wload(w_v[l].rearrange("h (k p) e -> p (h k) e", p=P), KC, DH)
                wg = sb_wg
                S.dma("pool", lambda e: e.dma_start(out=wg.t[:], in_=w_gate[l].rearrange("(c p) g -> p c g", p=P)), writes=wg.all)
                S.dma("sp", lambda e: e.dma_start(out=bgt.t[:], in_=b_gate[l:l + 1, :].partition_broadcast(LC)), writes=bgt.all)

                def fm_proj(w, src, src_off, dst, scale, h):
                    for ec in range(2):
                        pa = psA6.next()
                        for k in range(2):
                            S.op("pe", lambda e, pa=pa, k=k, ec=ec: e.matmul(out=pa.t[:, 0:N], lhsT=w.t[:, h * 2 + k, ec * P:(ec + 1) * P], rhs=src.t[:, h * 2 + k, src_off:src_off + N], start=(k == 0), stop=(k == 1)),
                                 reads=w.all + [src.b[h * 2 + k]], writes=pa.all, sig=(k == 1))
                        if dst is vT:
                            S.op("act", lambda e, pa=pa, ec=ec: e.activation(out=vT.t[:, ec, :], in_=pa.t[:, 0:N], func=AF.Copy, scale=scale), reads=pa.all, writes=vT.all)
                        elif dst is kT:
                            S.op("dve", lambda e, pa=pa, ec=ec: e.tensor_scalar(out=dst.t[:, h * 2 + ec, :], in0=pa.t[:, 0:N], scalar1=scale, scalar2=None, op0=ALU.mult), reads=pa.all, writes=[dst.b[h * 2 + ec]])
                        else:
                            S.op("act", lambda e, pa=pa, ec=ec: e.activation(out=dst.t[:, h * 2 + ec, :], in_=pa.t[:, 0:N], func=AF.Copy, scale=scale), reads=pa.all, writes=[dst.b[h * 2 + ec]])

                def gate_mm(src_ap, srcb, cidx, first, last):
                    for blk in range(NB):
                        S.op("pe", lambda e, blk=blk: e.matmul(out=psS.t[0:RB, blk * 8:(blk + 1) * 8], lhsT=src_ap(blk), rhs=wg.t[:, cidx, :], start=(first and blk == 0), stop=last, skip_group_check=True),
                             reads=srcb + wg.all, writes=psS.all, sig=(blk == NB - 1))

                for h in range(H):
                    fm_proj(wq, xc, 0, qT, 1.0, h)
                    fm_proj(wk, xc, 0, kT, DH ** -0.5, h)
                    fm_proj(wv, xm, CW - 1, vT, 1.0, h)
                    for blk in range(NB):
                        for (w, src, off, dst, scale, eng) in ((wk, xc, 0, ktok, DH ** -0.5, "act"), (wv, xm, CW - 1, vtok, 1.0, "dve")) + (((wq, xc, 0, None, 1.0, "dve"),) if sample else ()):
                            pa = psA6.next()
                            for k in range(2):
                                S.op("pe", lambda e, pa=pa, k=k, w=w, src=src, off=off, blk=blk: e.matmul(out=pa.t[0:RB, 0:DH], lhsT=src.t[:, h * 2 + k, off + blk * LC:off + blk * LC + RB], rhs=w.t[:, h * 2 + k, 0:DH], start=(k == 0), stop=(k == 1)),
                                     reads=w.all + [src.b[h * 2 + k]], writes=pa.all, sig=(k == 1))
                            if dst is None:
                                S.op("dve", lambda e, pa=pa: e.tensor_copy(out=qtok.t[:, h * DH:(h + 1) * DH], in_=pa.t[0:RB, 0:DH]), reads=pa.all, writes=qtok.all)
                            elif eng == "act":
                                S.op("act", lambda e, pa=pa, dst=dst, blk=blk, scale=scale: e.activation(out=dst.t[0:RB, blk, h * DH:(h + 1) * DH], in_=pa.t[0:RB, 0:DH], func=AF.Copy, scale=scale), reads=pa.all, writes=[dst.b[blk]])
                            else:
                                S.op("dve", lambda e, pa=pa, dst=dst, blk=blk: e.tensor_copy(out=vtok_v[0:RB, blk, h * DH:(h + 1) * DH], in_=pa.t[0:RB, 0:DH]), reads=pa.all, writes=[dst.b[blk]])
                    for ec in range(2):
                        gate_mm(lambda blk, ec=ec: qT.t[:, h * 2 + ec, blk * LC:blk * LC + RB], [qT.b[h * 2 + ec]], h * 2 + ec, (h == 0 and ec == 0), False)
                        gate_mm(lambda blk, ec=ec: kT.t[:, h * 2 + ec, blk * LC:blk * LC + RB], [kT.b[h * 2 + ec]], KC + h * 2 + ec, False, False)
                    for ec in range(2):
                        gate_mm(lambda blk, ec=ec: vT.t[:, ec, blk * LC:blk * LC + RB], vT.all, 2 * KC + h * 2 + ec, False, (h == H - 1 and ec == 1))

                gi, lf, t1, t2 = gt
                S.op("dve", lambda e: e.tensor_tensor(out=gsb.t[0:RB], in0=psS.t[0:RB, 0:NB * 8].rearrange("p (b g) -> p b g", g=8), in1=bcast(bgt.t[0:RB].unsqueeze(1), [RB, NB, 8]), op=ALU.add),
                     reads=psS.all + bgt.all, writes=gsb.all)
                S.op("act", lambda e: e.activation(out=t1.t[0:RB], in_=gsb.t[0:RB, :, 4:8], func=AF.Abs), reads=gsb.all, writes=t1.all)
                S.op("act", lambda e: e.activation(out=t1.t[0:RB], in_=t1.t[0:RB], func=AF.Exp, scale=-1.0), reads=t1.all, writes=t1.all)
                S.op("act", lambda e: e.activation(out=t1.t[0:RB], in_=t1.t[0:RB], func=AF.Ln, bias=1.0, scale=1.0), reads=t1.all, writes=t1.all)
                S.op("dve", lambda e: e.tensor_scalar_min(out=t2.t[0:RB], in0=gsb.t[0:RB, :, 4:8], scalar1=0.0), reads=gsb.all, writes=t2.all)
                S.op("dve", lambda e: e.tensor_sub(out=lf.t[0:RB], in0=t2.t[0:RB], in1=t1.t[0:RB]), reads=t1.all + t2.all, writes=lf.all)
                S.op("dve", lambda e: e.tensor_copy(out=gi.t[0:RB], in_=gsb.t[0:RB, :, 0:4]), reads=gsb.all, writes=gi.all)

                def ev_gb(fc, pa):
                    S.op("act", lambda e: e.activation(out=sgb.t[:, fc, :], in_=pa.t[:, 0:N], func=AF.Sigmoid), reads=pa.all, writes=[sgb.b[fc]])

                def ev_ga(fc, pa):
                    S.op("dve", lambda e: e.tensor_tensor(out=uu.t[:, fc, CW - 1:CW - 1 + N], in0=pa.t[:, 0:N], in1=sgb.t[:, fc, :], op=ALU.mult), reads=pa.all + [sgb.b[fc]], writes=[uu.b[fc]])

                if not sample:
                    S.op("dve", lambda e: e.tensor_copy(out=uu.t[:, :, 0:CW - 1], in_=hu.t[:, l, :, :]), reads=[hu.b[l]], writes=uu.all)

                    def fill():
                        yield from proj_gen(w_in[l], 3 * D, KC, hT, N, ev_gb)
                        yield from proj_gen(w_in[l], 2 * D, KC, hT, N, ev_ga)

                    def fill2():
                        yield from proj_gen(w_in[l], 1 * D, KC, hT, N, ev_zm, ring=psA6)
                        yield from proj_gen(w_in[l], 5 * D, KC, hT, N, ev_gm, ring=psA6)
                    filler = fill()
                    filler2 = fill2()
                    mlstm_prompt(l, ti, gi, lf, filler, filler2)
                    for _ in filler:
                        pass
                else:
                    proj(w_in[l], 1 * D, KC, hT, N, ev_zm)
                    proj(w_in[l], 5 * D, KC, hT, N, ev_gm)
                    mlstm_sample(l, gi, lf)

                if sample:
                    proj(w_in[l], 3 * D, KC, hT, N, ev_gb)
                    proj(w_in[l], 2 * D, KC, hT, N, ev_ga)
                def src_c(k, j):
                    if not sample:
                        return uu.t[:, k, j:j + N], [uu.b[k]]
                    if j < CW - 1:
                        return huS.t[:, k, j, :], [huS.b[k]]
                    return uu.t[:, k, CW - 1:CW - 1 + N], [uu.b[k]]

                def ev_cc(k, pa):
                    S.op("act", lambda e: e.activation(out=uc32_v[:, k, :], in_=pa.t[:, 0:N], func=AF.Identity, bias=vc(l, k, V_CCB), scale=1.0), reads=pa.all + vcol.all, writes=[uc32.b[k]])
                dwconv(l, CW, V_CCW, dgc, src_c, ev_cc, pre_c)

                for fc in range(KC):
                    for blk in range(NB):
                        S.op("pe", lambda e, fc=fc, blk=blk: e.transpose(out=psT.t[:, blk * LC:blk * LC + RB], in_=hftok.t[0:RB, blk, fc * P:(fc + 1) * P], identity=identb[0:RB, 0:RB]),
                             reads=[hftok.b[blk]] + CST, writes=psT.all, sig=(blk == NB - 1))
                    tf = tmpf.next()
                    tb_ = tmpb.next()
                    S.op("dve", lambda e, fc=fc, tf=tf: e.scalar_tensor_tensor(out=tf.t[:], in0=psT.t[:, 0:N], scalar=vc(l, fc, V_GN), in1=szm.t[:, fc, :], op0=ALU.mult, op1=ALU.mult),
                         reads=psT.all + [szm.b[fc]] + vcol.all, writes=tf.all)
                    S.op("dve", lambda e, fc=fc, tb_=tb_: e.scalar_tensor_tensor(out=tb_.t[:], in0=xc.t[:, fc, :], scalar=vc(l, fc, V_SKIP), in1=szm.t[:, fc, :], op0=ALU.mult, op1=ALU.mult),
                         reads=[xc.b[fc], szm.b[fc]] + vcol.all, writes=tb_.all)
                    S.op("pool", lambda e, fc=fc, tf=tf, tb_=tb_: e.tensor_tensor(out=hmg.t[:, fc, :], in0=tf.t[:], in1=tb_.t[:], op=ALU.add), reads=tf.all + tb_.all, writes=[hmg.b[fc]])

                def ev_ym(fc, pa):
                    S.op("dve", lambda e: e.tensor_tensor(out=ymg.t[:, fc, :], in0=pa.t[:, 0:N], in1=sgm.t[:, fc, :], op=ALU.mult), reads=pa.all + [sgm.b[fc]], writes=[ymg.b[fc]])
                proj(w_br_m[l], 0, KC, hmg, N, ev_ym)


                def ev_gc(fc, pa):
                    S.op("act", lambda e: e.activation(out=sgc.t[:, fc, :], in_=pa.t[:, 0:N], func=AF.Sigmoid), reads=pa.all, writes=[sgc.b[fc]])


                def ev_zc(fc, pa):
                    S.op("act", lambda e: e.activation(out=szc.t[:, fc, :], in_=pa.t[:, 0:N], func=AF.Silu), reads=pa.all, writes=[szc.b[fc]])

                if not sample:
                    S.op("dve", lambda e: e.tensor_copy(out=hu.t[:, l, :, :], in_=uu.t[:, :, N:N + CW - 1]), reads=uu.all, writes=[hu.b[l]])
                    if last_tile:
                        store_rows(lambda k: hu.t[:, l, k, :], [hu.b[l]], CW - 1, ocvp[l, :, :], True)
                else:
                    store_rows(lambda k: uu.t[:, k, CW - 1:CW - 1 + N], uu.all, NS, ocvs[l].rearrange("(b j) d -> b j d", j=CW - 1)[:, CW - 2, :], True)

                pm = psA.next()
                pq = psA.next()
                for k in range(KC):
                    S.op("pe", lambda e, k=k: e.matmul(out=pm.t[:, 0:N], lhsT=ones, rhs=uc32_v[:, k, :], start=(k == 0), stop=(k == KC - 1)), reads=[uc32.b[k]] + CST, writes=pm.all, sig=(k == KC - 1))
                for k in range(KC):
                    s_ = tmpb.next()
                    S.op("act", lambda e, s_=s_, k=k: e.activation(out=s_.t[:], in_=uc32_v[:, k, :], func=AF.Square), reads=[uc32.b[k]], writes=s_.all)
                    S.op("pe", lambda e, s_=s_, k=k: e.matmul(out=pq.t[:, 0:N], lhsT=onesb, rhs=s_.t[:], start=(k == 0), stop=(k == KC - 1)), reads=s_.all + CST, writes=pq.all, sig=True)
                mean = meanT
                var = rstd
                S.op("dve", lambda e: e.tensor_scalar(out=mean.t[:], in0=pm.t[:, 0:N], scalar1=1.0 / D, scalar2=None, op0=ALU.mult), reads=pm.all, writes=mean.all)
                S.op("dve", lambda e: e.tensor_tensor(out=var.t[:], in0=mean.t[:], in1=mean.t[:], op=ALU.mult), reads=mean.all, writes=var.all)
                S.op("dve", lambda e: e.scalar_tensor_tensor(out=var.t[:], in0=pq.t[:, 0:N], scalar=1.0 / D, in1=var.t[:], op0=ALU.mult, op1=ALU.subtract), reads=pq.all + var.all, writes=var.all)
                S.op("act", lambda e: e.activation(out=var.t[:], in_=var.t[:], func=AF.Sqrt, bias=epsc.t[:, 0:1], scale=1.0), reads=var.all + epsc.all, writes=var.all)
                S.op("dve", lambda e: e.reciprocal(out=var.t[:], in_=var.t[:]), reads=var.all, writes=var.all)
                for k in range(KC):
                    tf = tmpf.next()
                    S.op("pool", lambda e, k=k, tf=tf: e.tensor_tensor(out=tf.t[:], in0=uc32_v[:, k, :], in1=mean.t[:], op=ALU.subtract), reads=[uc32.b[k]] + mean.all, writes=tf.all)
                    S.op("dve", lambda e, k=k, tf=tf: e.tensor_tensor(out=tf.t[:], in0=tf.t[:], in1=var.t[:], op=ALU.mult), reads=tf.all + var.all, writes=tf.all)
                    S.op("act", lambda e, k=k, tf=tf: e.activation(out=ucg.t[:, k, :], in_=tf.t[:], func=AF.Silu, bias=vc(l, k, V_CLNB), scale=vc(l, k, V_CLNW)), reads=tf.all + vcol.all, writes=[ucg.b[k]])
                proj(w_in[l], 4 * D, KC, hT, N, ev_zc)
                for k in range(KC):
                    S.op("dve", lambda e, k=k: e.tensor_tensor(out=ucg.t[:, k, :], in0=ucg.t[:, k, :], in1=szc.t[:, k, :], op=ALU.mult), reads=[ucg.b[k], szc.b[k]], writes=[ucg.b[k]])

                proj(w_in[l], 6 * D, KC, hT, N, ev_gc)

                def ev_yc(fc, pa):
                    tf = tmpf.next()
                    S.op("dve", lambda e: e.tensor_tensor(out=tf.t[:], in0=pa.t[:, 0:N], in1=sgc.t[:, fc, :], op=ALU.mult), reads=pa.all + [sgc.b[fc]], writes=tf.all)
                    S.op("pool", lambda e: e.tensor_tensor(out=mrg.t[:, fc, :], in0=tf.t[:], in1=ymg.t[:, fc, :], op=ALU.add), reads=tf.all + [ymg.b[fc]], writes=[mrg.b[fc]])
                proj(w_br_c[l], 0, KC, ucg, N, ev_yc)

                def ev_out(fc, pa):
                    if not sample:
                        S.op("dve", lambda e: e.scalar_tensor_tensor(out=xT.t[:, fc, :], in0=pa.t[:, 0:N], scalar=modv(l, 2, fc), in1=xT.t[:, fc, :], op0=ALU.mult, op1=ALU.add), reads=pa.all + [xT.b[fc]] + MOD, writes=[xT.b[fc]])
                    else:
                        tf = tmpf.next()
                        S.op("dve", lambda e: e.tensor_tensor(out=tf.t[:], in0=pa.t[:, 0:N], in1=modv(l, 2, fc), op=ALU.mult), reads=pa.all + MOD, writes=tf.all)
                        S.op("dve", lambda e: e.tensor_tensor(out=xT.t[:, fc, :], in0=tf.t[:], in1=xT.t[:, fc, :], op=ALU.add), reads=tf.all + [xT.b[fc]], writes=[xT.b[fc]])
                    rms_chunk(fc, defer=True)
                proj(w_out[l], 0, KC, mrg, N, ev_out)
                rms_flush()

            def mlstm_prompt(l, ti, gi, lf, filler=None, filler2=None):
                aT, FT, cT, tT = rowA, rowF, rowA, rowF
                if ti == 0:
                    S.op("dve", lambda e: e.memset(C32.t[:], 0.0), writes=C32.all)
                else:
                    S.dma("sp", lambda e: e.dma_start(out=C32.t[:], in_=oCp[l].rearrange("h (k p) e -> p h k e", p=P)), reads=[dC[l]], writes=C32.all)
                Amax, FL, mall, mprv, Mx, alp = sm8
                def fill(n):
                    if filler is not None:
                        for _ in range(n):
                            next(filler, None)

                def fill_g(n):
                    if filler2 is not None:
                        for _ in range(n):
                            next(filler2, None)
                fill_g(4)
                pa = psA.next()
                pf = psA.next()
                for blk in range(NB):
                    S.op("pe", lambda e, blk=blk: e.matmul(out=pa.t[0:H, blk * LC:(blk + 1) * LC], lhsT=gi.t[:, blk, :], rhs=ident[0:LC, 0:LC], start=True, stop=False), reads=gi.all + CST, writes=pa.all, sig=False)
                    S.op("pe", lambda e, blk=blk: e.matmul(out=pa.t[0:H, blk * LC:(blk + 1) * LC], lhsT=lf.t[:, blk, :], rhs=ntri[0:LC, 0:LC], start=False, stop=True), reads=lf.all + CST, writes=pa.all, sig=False)
                    S.op("pe", lambda e, blk=blk: e.matmul(out=pf.t[0:H, blk * LC:(blk + 1) * LC], lhsT=lf.t[:, blk, :], rhs=triu[0:LC, 0:LC], start=True, stop=True), reads=lf.all + CST, writes=pf.all, sig=True)
                S.op("act", lambda e: e.activation(out=aT.t[:], in_=pa.t[0:H, 0:N], func=AF.Copy), reads=pa.all, writes=aT.all)
                S.op("act", lambda e: e.activation(out=FT.t[:], in_=pf.t[0:H, 0:N], func=AF.Copy), reads=pf.all, writes=FT.all)
                fill_g(4)
                S.op("dve", lambda e: e.tensor_reduce(out=Amax.t[:], in_=aT.t[:].rearrange("p (c s) -> p c s", s=LC), axis=AX.X, op=ALU.max), reads=aT.all, writes=Amax.all)
                S.op("dve", lambda e: e.tensor_copy(out=FL.t[:], in_=FT.t[:].rearrange("p (c s) -> p c s", s=LC)[:, :, LC - 1]), reads=FT.all, writes=FL.all)
                S.op("dve", lambda e: e.tensor_tensor_scan(out=mall.t[:], data0=Amax.t[:], data1=FL.t[:], initial=mst.t[:, l:l + 1], op0=ALU.max, op1=ALU.add), reads=Amax.all + FL.all + [mst.b[l]], writes=mall.all)
                S.op("dve", lambda e: e.tensor_copy(out=mprv.t[:, 0:1], in_=mst.t[:, l:l + 1]), reads=[mst.b[l]], writes=mprv.all)
                S.op("dve", lambda e: e.tensor_copy(out=mprv.t[:, 1:NB], in_=mall.t[:, 0:NB - 1]), reads=mall.all + mprv.all, writes=mprv.all)
                S.op("dve", lambda e: e.tensor_copy(out=mst.t[:, l:l + 1], in_=mall.t[:, NB - 1:NB]), reads=mall.all + mprv.all, writes=[mst.b[l]])
                S.op("dve", lambda e: e.tensor_tensor(out=Mx.t[:], in0=mprv.t[:], in1=Amax.t[:], op=ALU.max), reads=mprv.all + Amax.all, writes=Mx.all)
                S.op("dve", lambda e: e.tensor_sub(out=alp.t[:], in0=mprv.t[:], in1=Mx.t[:]), reads=mprv.all + Mx.all, writes=alp.all)
                S.op("act", lambda e: e.activation(out=alp.t[:], in_=alp.t[:], func=AF.Exp), reads=alp.all, writes=alp.all)
                Mb = bcast(Mx.t[:].unsqueeze(2), [H, NB, LC])
                S.op("dve", lambda e: e.tensor_tensor(out=cT.t[:].rearrange("p (c s) -> p c s", s=LC), in0=aT.t[:].rearrange("p (c s) -> p c s", s=LC), in1=Mb, op=ALU.subtract), reads=aT.all + Mx.all, writes=cT.all)
                S.op("act", lambda e: e.activation(out=cT.t[:], in_=cT.t[:], func=AF.Exp), reads=cT.all, writes=cT.all)
                S.op("dve", lambda e: e.tensor_tensor(out=tT.t[:].rearrange("p (c s) -> p c s", s=LC), in0=FT.t[:].rearrange("p (c s) -> p c s", s=LC), in1=Mb, op=ALU.add), reads=FT.all + Mx.all, writes=tT.all)
                S.op("act", lambda e: e.activation(out=tT.t[:], in_=tT.t[:], func=AF.Exp, scale=-1.0), reads=tT.all, writes=tT.all)
                fill_g(4)
                for blk in range(NB):
                    S.op("pe", lambda e, blk=blk: e.matmul(out=psS.t[0:LC, blk * 8:blk * 8 + 4], lhsT=cT.t[:, blk * LC:(blk + 1) * LC], rhs=ident[0:H, 0:H], start=True, stop=True), reads=cT.all + CST, writes=psS.all, sig=False)
                    S.op("pe", lambda e, blk=blk: e.matmul(out=psS.t[0:LC, blk * 8 + 4:blk * 8 + 8], lhsT=tT.t[:, blk * LC:(blk + 1) * LC], rhs=ident[0:H, 0:H], start=True, stop=True), reads=tT.all + CST, writes=psS.all, sig=(blk == NB - 1))
                S.op("dve", lambda e: e.tensor_copy(out=ctk.t[:], in_=psS.t[0:LC, 0:NB * 8].rearrange("p (b g) -> p b g", g=8)), reads=psS.all, writes=ctk.all)
                S.op("dve", lambda e: e.tensor_copy(out=cbk.t[:], in_=ctk.t[:, :, 0:4]), reads=ctk.all, writes=cbk.all)
                fill_g(4)
                S.op("dve", lambda e: e.tensor_tensor(out=adg.t[:], in0=bcast(alp.t[:].unsqueeze(2), [H, NB, H]), in1=bcast(ident[0:H, 0:H].unsqueeze(1), [H, NB, H]), op=ALU.mult), reads=alp.all + CST, writes=adg.all)
                S.op("pe", lambda e: e.matmul(out=psS.t[:, 0:NB * H], lhsT=ones[0:H, :], rhs=adg.t[:].rearrange("p c h -> p (c h)"), start=True, stop=True), reads=adg.all + CST, writes=psS.all, sig=True)
                S.op("dve", lambda e: e.tensor_copy(out=abc.t[:], in_=psS.t[:, 0:NB * H].rearrange("p (c h) -> p c h", h=H)), reads=psS.all, writes=abc.all)
                for blk in range(NB):
                    S.op("dve", lambda e, blk=blk: e.tensor_tensor(out=vtok_v[:, blk, :].rearrange("p (h e) -> p h e", h=H), in0=vtok_v[:, blk, :].rearrange("p (h e) -> p h e", h=H), in1=bcast(ctk.t[:, blk, 0:4].unsqueeze(2), [LC, H, DH]), op=ALU.mult),
                         reads=[vtok.b[blk]] + ctk.all, writes=[vtok.b[blk]])
                for h in range(H):
                    S.op("act", lambda e, h=h: e.activation(out=Cb.t[:, h].rearrange("p k e -> p (k e)"), in_=C32.t[:, h].rearrange("p k e -> p (k e)"), func=AF.Copy, scale=abc.t[:, 0, h:h + 1]), reads=[C32.b[h]] + abc.all, writes=[Cb.b[h]])
                S.op("dve", lambda e: e.tensor_tensor(out=nb_.t[:], in0=n32.t[:, l], in1=bcast(abc.t[:, 0, :].unsqueeze(2), [P, H, 2]), op=ALU.mult), reads=[n32.b[l]] + abc.all, writes=nb_.all)

                fill_g(16)
                sms = []
                pend = []
                for j in range(NB):
                    t0 = j * LC
                    pa = psA.next()
                    for h in range(H):
                        for k in range(2):
                            S.op("pe", lambda e, h=h, k=k, pa=pa: e.matmul(out=pa.t[0:LC, h * LC:(h + 1) * LC], lhsT=kT.t[:, h * 2 + k, t0:t0 + LC], rhs=qT.t[:, h * 2 + k, t0:t0 + LC], start=(k == 0), stop=(k == 1)),
                                 reads=[kT.b[h * 2 + k], qT.b[h * 2 + k]], writes=pa.all, sig=(h == H - 1 and k == 1))
                    sm_ = Sm.next()
                    S.op("dve", lambda e, pa=pa, sm_=sm_: e.tensor_tensor(out=sm_.t[:], in0=pa.t[0:LC, 0:H * LC].rearrange("p (h l) -> p h l", h=H), in1=bcast(maskb[0:LC, 0:LC].unsqueeze(1), [LC, H, LC]), op=ALU.mult), reads=pa.all + CST, writes=sm_.all)
                    sms.append(sm_)
                def evac_chunk(j, po):
                    dab, dmx, d2, rs, nbias = smls[j % 2]
                    S.op("dve", lambda e: e.tensor_tensor(out=dmx.t[:], in0=dab.t[:], in1=ctk.t[:, j, 4:8], op=ALU.max), reads=dab.all + ctk.all, writes=dmx.all)
                    S.op("dve", lambda e: e.tensor_tensor(out=d2.t[:], in0=dmx.t[:], in1=dmx.t[:], op=ALU.mult), reads=dmx.all, writes=d2.all)
                    for h in range(H):
                        S.op("dve", lambda e, h=h: e.bn_stats(out=st6.t[:, h, :], in_=po.t[0:LC, h * DH:(h + 1) * DH]), reads=po.all, writes=st6.all)
                    for h in range(H):
                        S.op("dve", lambda e, h=h: e.bn_aggr(out=mv.t[:, h, :], in_=st6.t[:, h, :]), reads=st6.all, writes=mv.all)
                    S.op("dve", lambda e: e.scalar_tensor_tensor(out=rs.t[:], in0=d2.t[:], scalar=EPS, in1=mv.t[:, :, 1], op0=ALU.mult, op1=ALU.add), reads=d2.all + mv.all, writes=rs.all)
                    S.op("act", lambda e: e.activation(out=rs.t[:], in_=rs.t[:], func=AF.Sqrt), reads=rs.all, writes=rs.all)
                    S.op("dve", lambda e: e.reciprocal(out=rs.t[:], in_=rs.t[:]), reads=rs.all, writes=rs.all)
                    S.op("dve", lambda e: e.scalar_tensor_tensor(out=nbias.t[:], in0=mv.t[:, :, 0], scalar=-1.0, in1=rs.t[:], op0=ALU.mult, op1=ALU.mult), reads=mv.all + rs.all, writes=nbias.all)
                    for h in range(H):
                        S.op("act", lambda e, h=h: e.activation(out=hftok.t[:, j, h * DH:(h + 1) * DH], in_=po.t[0:LC, h * DH:(h + 1) * DH], func=AF.Identity, bias=nbias.t[:, h:h + 1], scale=rs.t[:, h:h + 1]),
                             reads=po.all + rs.all + nbias.all, writes=[hftok.b[j]])
                for j in range(NB):
                    t0 = j * LC
                    po = psOr.next()
                    sm_ = sms[j]
                    for h in range(H):
                        pc = psA.next()
                        for k in range(2):
                            S.op("pe", lambda e, h=h, k=k, pc=pc: e.matmul(out=pc.t[:, k * DH:(k + 1) * DH], lhsT=ktok.t[:, j, h * DH + k * P:h * DH + (k + 1) * P], rhs=vtok_v[:, j, h * DH:(h + 1) * DH], start=True, stop=True), reads=[ktok.b[j], vtok.b[j]], writes=pc.all, sig=(k == 1))
                        S.op("dve", lambda e, h=h, pc=pc: e.scalar_tensor_tensor(out=C32.t[:, h].rearrange("p k e -> p (k e)"), in0=C32.t[:, h].rearrange("p k e -> p (k e)"), scalar=abc.t[:, j, h:h + 1], in1=pc.t[:, 0:2 * DH], op0=ALU.mult, op1=ALU.add),
                             reads=[C32.b[h]] + pc.all + abc.all, writes=[C32.b[h]])
                    pend_now = list(pend)
                    del pend[:]
                    for h in range(H):
                        S.op("pe", lambda e, h=h, sm_=sm_: e.matmul(out=po.t[0:LC, h * DH:(h + 1) * DH], lhsT=sm_.t[:, h, :], rhs=vtok_v[:, j, h * DH:(h + 1) * DH], start=True, stop=False), reads=sm_.all + [vtok.b[j]], writes=po.all, sig=False)
                        for k in range(2):
                            S.op("pe", lambda e, h=h, k=k: e.matmul(out=po.t[0:LC, h * DH:(h + 1) * DH], lhsT=qT.t[:, h * 2 + k, t0:t0 + LC], rhs=Cb.t[:, h, k, :], start=False, stop=(k == 1)), reads=[qT.b[h * 2 + k], Cb.b[h]], writes=po.all, sig=False)
                    for h in range(H):
                        S.op("pe", lambda e, h=h, sm_=sm_: e.matmul(out=psS.t[0:LC, h:h + 1], lhsT=sm_.t[:, h, :], rhs=cbk.t[:, j, h:h + 1], start=True, stop=False), reads=sm_.all + cbk.all, writes=psS.all, sig=False)
                        for k in range(2):
                            S.op("pe", lambda e, h=h, k=k: e.matmul(out=psS.t[0:LC, h:h + 1], lhsT=qT.t[:, h * 2 + k, t0:t0 + LC], rhs=nb_.t[:, h, k:k + 1], start=False, stop=(k == 1)), reads=[qT.b[h * 2 + k]] + nb_.all, writes=psS.all, sig=(h == H - 1 and k == 1))
                    dab, dmx, d2, rs, nbias = smls[j % 2]
                    S.op("act", lambda e: e.activation(out=dab.t[:], in_=psS.t[0:LC, 0:H], func=AF.Abs), reads=psS.all, writes=dab.all)
                    for pj, ppo in pend_now:
                        evac_chunk(pj, ppo)
                    if j + 1 < NB:
                        for h in range(H):
                            S.op("act", lambda e, h=h: e.activation(out=Cb.t[:, h].rearrange("p k e -> p (k e)"), in_=C32.t[:, h].rearrange("p k e -> p (k e)"), func=AF.Copy, scale=abc.t[:, j + 1, h:h + 1]), reads=[C32.b[h]] + abc.all, writes=[Cb.b[h]])
                    for h in range(H):
                        for k in range(2):
                            S.op("pe", lambda e, h=h, k=k: e.matmul(out=psS.t[:, 8 + h * 2 + k:9 + h * 2 + k], lhsT=ktok.t[:, j, h * DH + k * P:h * DH + (k + 1) * P], rhs=cbk.t[:, j, h:h + 1], start=True, stop=True), reads=[ktok.b[j]] + cbk.all, writes=psS.all, sig=(h == H - 1 and k == 1))
                    S.op("dve", lambda e: e.tensor_tensor(out=n32.t[:, l], in0=n32.t[:, l], in1=bcast(abc.t[:, j, :].unsqueeze(2), [P, H, 2]), op=ALU.mult), reads=[n32.b[l]] + abc.all, writes=[n32.b[l]])
                    S.op("dve", lambda e: e.tensor_tensor(out=n32.t[:, l], in0=n32.t[:, l], in1=psS.t[:, 8:16].rearrange("p (h k) -> p h k", k=2), op=ALU.add), reads=[n32.b[l]] + psS.all, writes=[n32.b[l]])
                    if j + 1 < NB:
                        S.op("dve", lambda e: e.tensor_tensor(out=nb_.t[:], in0=n32.t[:, l], in1=bcast(abc.t[:, j + 1, :].unsqueeze(2), [P, H, 2]), op=ALU.mult), reads=[n32.b[l]] + abc.all, writes=nb_.all)
                    pend.append((j, po))
                    fill(2)

                while pend:
                    evac_chunk(*pend.pop(0))
                S.dma("sp", lambda e: e.dma_start(out=oCp[l].rearrange("h (k p) e -> p h k e", p=P), in_=C32.t[:]), reads=C32.all, writes=[dC[l]])
                if ti == NT - 1:
                    S.dma("sp", lambda e: e.dma_start(out=onp[l].rearrange("(h k p) -> p h k", h=H, p=P), in_=n32.t[:, l], allow_slow_non_contiguous=True), reads=[n32.b[l]])
                    S.dma("sp", lambda e: e.dma_start(out=omp[l].rearrange("(h o) -> h o", o=1), in_=mst.t[:, l:l + 1], allow_slow_non_contiguous=True), reads=[mst.b[l]])

            def mlstm_sample(l, gi, lf):
                R = NS
                a_, mt, dec, wsc, thr, qk, qn, den, dmx, tmp = s16
                S.dma("sp", lambda e: e.dma_start(out=mprev.t[:], in_=sm[l, :, :]), writes=mprev.all)
                S.dma("sp", lambda e: e.dma_start(out=nst.t[:].rearrange("p h e -> p (h e)"), in_=sn[l, :, :]), writes=nst.all)
                g_i = gi.t[0:R, 0, :]
                g_f = lf.t[0:R, 0, :]
                S.op("dve", lambda e: e.tensor_add(out=a_.t[:], in0=g_f, in1=mprev.t[:]), reads=lf.all + mprev.all, writes=a_.all)
                S.op("dve", lambda e: e.tensor_max(out=mt.t[:], in0=a_.t[:], in1=g_i), reads=a_.all + gi.all, writes=mt.all)
                S.op("dve", lambda e: e.tensor_sub(out=dec.t[:], in0=a_.t[:], in1=mt.t[:]), reads=a_.all + mt.all, writes=dec.all)
                S.op("act", lambda e: e.activation(out=dec.t[:], in_=dec.t[:], func=AF.Exp), reads=dec.all, writes=dec.all)
                S.op("dve", lambda e: e.tensor_sub(out=wsc.t[:], in0=g_i, in1=mt.t[:]), reads=gi.all + mt.all, writes=wsc.all)
                S.op("act", lambda e: e.activation(out=wsc.t[:], in_=wsc.t[:], func=AF.Exp), reads=wsc.all, writes=wsc.all)
                S.op("act", lambda e: e.activation(out=thr.t[:], in_=mt.t[:], func=AF.Exp, scale=-1.0), reads=mt.all, writes=thr.all)
                S.dma("sp", lambda e: e.dma_start(out=oms[l, :, :], in_=mt.t[:]), reads=mt.all)
                ktf = tmpS_k
                S.op("act", lambda e: e.activation(out=ktf.t[:], in_=ktok.t[0:R, 0, :], func=AF.Copy), reads=ktok.all, writes=ktf.all)
                S.op("dve", lambda e: e.tensor_tensor(out=prod.t[:], in0=qtok.t[:], in1=ktf.t[:], op=ALU.mult), reads=qtok.all + ktf.all, writes=prod.all)
                S.op("dve", lambda e: e.tensor_reduce(out=qk.t[:], in_=prod.t[:].rearrange("p (h e) -> p h e", h=H), axis=AX.X, op=ALU.add), reads=prod.all, writes=qk.all)
                S.op("dve", lambda e: e.tensor_tensor(out=prod.t[:], in0=qtok.t[:], in1=nst.t[:].rearrange("p h e -> p (h e)"), op=ALU.mult), reads=qtok.all + nst.all, writes=prod.all)
                S.op("dve", lambda e: e.tensor_reduce(out=qn.t[:], in_=prod.t[:].rearrange("p (h e) -> p h e", h=H), axis=AX.X, op=ALU.add), reads=prod.all, writes=qn.all)
                S.op("dve", lambda e: e.tensor_mul(out=qk.t[:], in0=qk.t[:], in1=wsc.t[:]), reads=qk.all + wsc.all, writes=qk.all)
                S.op("dve", lambda e: e.tensor_mul(out=den.t[:], in0=dec.t[:], in1=qn.t[:]), reads=dec.all + qn.all, writes=den.all)
                S.op("dve", lambda e: e.tensor_add(out=den.t[:], in0=den.t[:], in1=qk.t[:]), reads=den.all + qk.all, writes=den.all)
                S.op("act", lambda e: e.activation(out=den.t[:], in_=den.t[:], func=AF.Abs), reads=den.all, writes=den.all)
                S.op("dve", lambda e: e.tensor_max(out=dmx.t[:], in0=den.t[:], in1=thr.t[:]), reads=den.all + thr.all, writes=dmx.all)
                S.op("dve", lambda e: e.tensor_tensor(out=nnew.t[:], in0=nst.t[:], in1=bcast(dec.t[:].unsqueeze(2), [R, H, DH]), op=ALU.mult), reads=nst.all + dec.all, writes=nnew.all)
                S.op("dve", lambda e: e.tensor_tensor(out=ktf.t[:].rearrange("p (h e) -> p h e", h=H), in0=ktf.t[:].rearrange("p (h e) -> p h e", h=H), in1=bcast(wsc.t[:].unsqueeze(2), [R, H, DH]), op=ALU.mult), reads=ktf.all + wsc.all, writes=ktf.all)
                S.op("dve", lambda e: e.tensor_add(out=nnew.t[:].rearrange("p h e -> p (h e)"), in0=nnew.t[:].rearrange("p h e -> p (h e)"), in1=ktf.t[:]), reads=nnew.all + ktf.all, writes=nnew.all)
                S.dma("sp", lambda e: e.dma_start(out=ons[l, :, :], in_=nnew.t[:].rearrange("p h e -> p (h e)")), reads=nnew.all)
                for half in range(2):
                    pa = psA.next()
                    for kk in range(4):
                        k = half * 4 + kk
                        S.op("pe", lambda e, pa=pa, k=k, kk=kk: e.transpose(out=pa.t[:, kk * R:(kk + 1) * R], in_=qtok.t[:, k * P:(k + 1) * P], identity=ident[0:R, 0:R]), reads=qtok.all + CST, writes=pa.all, sig=(kk == 3))
                    S.op("dve", lambda e, pa=pa, half=half: e.tensor_copy(out=qTf.t[:, half * 4:(half + 1) * 4, :], in_=pa.t[:, 0:4 * R].rearrange("p (k b) -> p k b", b=R)), reads=pa.all, writes=qTf.b[half * 4:(half + 1) * 4])
                for k in range(KC):
                    S.op("dve", lambda e, k=k: e.tensor_tensor(out=qm.t[:, k], in0=bcast(qTf.t[:, k, :].unsqueeze(1), [P, R, R]), in1=cst.t[:, 4 * P:4 * P + R * R].rearrange("p (a b) -> p a b", b=R), op=ALU.mult), reads=[qTf.b[k]] + CST, writes=[qm.b[k]])
                S.op("dve", lambda e: e.tensor_tensor(out=dexp.t[:], in0=bcast(dec.t[:].unsqueeze(1), [R, R, H]), in1=bcast(ident[0:R, 0:R].unsqueeze(2), [R, R, H]), op=ALU.mult), reads=dec.all + CST, writes=dexp.all)
                S.op("pe", lambda e: e.matmul(out=psS.t[:, 0:R * H], lhsT=ones[0:R, :], rhs=dexp.t[:].rearrange("p b h -> p (b h)"), start=True, stop=True), reads=dexp.all + CST, writes=psS.all, sig=True)
                S.op("dve", lambda e: e.tensor_copy(out=dbc.t[:], in_=psS.t[:, 0:R * H].rearrange("p (b h) -> p b h", h=H)), reads=psS.all, writes=dbc.all)
                cins = []
                psO = psOr.next()
                cis = {}

                def issue_in(bb):
                    ci_ = Cin.next()
                    S.dma("sp", lambda e: e.dma_start(out=ci_.t[:], in_=sC[l, bb].rearrange("h (k p) e -> p h k e", p=P)), writes=ci_.all)
                    cis[bb] = ci_
                issue_in(0)
                issue_in(1)
                for b in range(R):
                    if b + 2 < R:
                        issue_in(b + 2)
                    ci = cis[b]
                    if b % 4 == 0:
                        S.op("dve", lambda e, b=b: e.tensor_tensor(out=kexp.t[:], in0=bcast(ktf.t[:].unsqueeze(1), [R, 4, D]), in1=bcast(ident[0:R, b:b + 4].unsqueeze(2), [R, 4, D]), op=ALU.mult), reads=ktf.all + CST, writes=kexp.all)
                    cb16 = Cbf.next()
                    S.op("act", lambda e, ci=ci, cb16=cb16: e.activation(out=cb16.t[:].rearrange("p h k e -> p (h k e)"), in_=ci.t[:].rearrange("p h k e -> p (h k e)"), func=AF.Copy), reads=ci.all, writes=cb16.all)
                    for h in range(H):
                        for k in range(2):
                            S.op("pe", lambda e, ci=ci, b=b, h=h, k=k: e.matmul(out=psO.t[0:R, h * DH:(h + 1) * DH], lhsT=qm.t[:, h * 2 + k, b, :], rhs=cb16.t[:, h, k, :], start=(b == 0 and k == 0 and h % 2 == 0), stop=(b == R - 1 and k == 1), skip_group_check=True),
                                 reads=[qm.b[h * 2 + k]] + cb16.all, writes=psO.all, sig=(h == H - 1 and k == 1))
                    co = Cout.next()
                    for h in range(H):
                        pc = psA.next()
                        for k in range(2):
                            S.op("pe", lambda e, pc=pc, b=b, h=h, k=k: e.matmul(out=pc.t[:, k * DH:(k + 1) * DH], lhsT=kexp.t[:, b % 4, h * DH + k * P:h * DH + (k + 1) * P], rhs=vtok_v[0:R, 0, h * DH:(h + 1) * DH], start=True, stop=True),
                                 reads=kexp.all + vtok.all, writes=pc.all, sig=(k == 1))
                        S.op("dve", lambda e, pc=pc, ci=ci, co=co, b=b, h=h: e.scalar_tensor_tensor(out=co.t[:, h].rearrange("p k e -> p (k e)"), in0=ci.t[:, h].rearrange("p k e -> p (k e)"), scalar=dbc.t[:, b, h:h + 1], in1=pc.t[:, 0:2 * DH], op0=ALU.mult, op1=ALU.add),
                             reads=ci.all + pc.all + dbc.all, writes=co.all)
                    S.dma("sp", lambda e, co=co, b=b: e.dma_start(out=oCs[l, b].rearrange("h (k p) e -> p h k e", p=P), in_=co.t[:]), reads=co.all)
                S.op("dve", lambda e: e.tensor_tensor(out=numt.t[:].rearrange("p (h e) -> p h e", h=H), in0=psO.t[0:R, 0:D].rearrange("p (h e) -> p h e", h=H), in1=bcast(dec.t[:].unsqueeze(2), [R, H, DH]), op=ALU.mult), reads=psO.all + dec.all, writes=numt.all)
                S.op("dve", lambda e: e.tensor_tensor(out=prod.t[:].rearrange("p (h e) -> p h e", h=H), in0=vtok_v[0:R, 0, :].rearrange("p (h e) -> p h e", h=H), in1=bcast(qk.t[:].unsqueeze(2), [R, H, DH]), op=ALU.mult), reads=vtok.all + qk.all, writes=prod.all)
                S.op("dve", lambda e: e.tensor_add(out=numt.t[:], in0=numt.t[:], in1=prod.t[:]), reads=numt.all + prod.all, writes=numt.all)
                for h in range(H):
                    S.op("dve", lambda e, h=h: e.bn_stats(out=st6.t[:, h, :], in_=numt.t[:, h * DH:(h + 1) * DH]), reads=numt.all, writes=st6.all)
                for h in range(H):
                    S.op("dve", lambda e, h=h: e.bn_aggr(out=mv.t[:, h, :], in_=st6.t[:, h, :]), reads=st6.all, writes=mv.all)
                S.op("dve", lambda e: e.tensor_mul(out=tmp.t[:], in0=dmx.t[:], in1=dmx.t[:]), reads=dmx.all, writes=tmp.all)
                S.op("dve", lambda e: e.scalar_tensor_tensor(out=tmp.t[:], in0=tmp.t[:], scalar=EPS, in1=mv.t[:, :, 1], op0=ALU.mult, op1=ALU.add), reads=tmp.all + mv.all, writes=tmp.all)
                S.op("act", lambda e: e.activation(out=tmp.t[:], in_=tmp.t[:], func=AF.Sqrt), reads=tmp.all, writes=tmp.all)
                S.op("dve", lambda e: e.reciprocal(out=tmp.t[:], in_=tmp.t[:]), reads=tmp.all, writes=tmp.all)
                S.op("dve", lambda e: e.tensor_tensor(out=numt.t[:].rearrange("p (h e) -> p h e", h=H), in0=numt.t[:].rearrange("p (h e) -> p h e", h=H), in1=bcast(mv.t[:, :, 0:1], [R, H, DH]), op=ALU.subtract), reads=numt.all + mv.all, writes=numt.all)
                S.op("dve", lambda e: e.tensor_tensor(out=hftok.t[0:R, 0, :].rearrange("p (h e) -> p h e", h=H), in0=numt.t[:].rearrange("p (h e) -> p h e", h=H), in1=bcast(tmp.t[:].unsqueeze(2), [R, H, DH]), op=ALU.mult), reads=numt.all + tmp.all, writes=hftok.all)

            sb_wg = A([P, 3 * KC, 8], BF16)
            if sample:
                tmpS_k = A([NS, D], F32)

            for ti in tiles:
                if not sample:
                    for blk in range(N // P):
                        def wr_x(k, n, pa, blk=blk):
                            S.op("act", lambda e: e.activation(out=xT.t[:, k:k + n, blk * P:(blk + 1) * P], in_=pa.t[:, 0:n * P].rearrange("p (k t) -> p k t", t=P), func=AF.Copy), reads=pa.all, writes=xT.b[k:k + n])
                        load_rows(xp[ti * T + blk * P:ti * T + (blk + 1) * P, :], P, wr_x)
                else:
                    def wr_xs(k, n, pa):
                        S.op("act", lambda e: e.activation(out=xT.t[:, k:k + n, :], in_=pa.t[:, 0:n * NS].rearrange("p (k t) -> p k t", t=NS), func=AF.Copy), reads=pa.all, writes=xT.b[k:k + n])
                    load_rows(xs[:, :], NS, wr_xs)
                for k in range(KC):
                    rms_chunk(k)
                for l in range(DEPTH):
                    layer(l, ti)
                rms_finish()
                for k in range(KC):
                    S.op("dve", lambda e, k=k: e.scalar_tensor_tensor(out=uc32_v[:, k, :], in0=xT.t[:, k, :], scalar=vcol.t[:, 0, k, V_FIN:V_FIN + 1], in1=rstd.t[:], op0=ALU.mult, op1=ALU.mult), reads=[xT.b[k]] + rstd.all + vcol.all, writes=[uc32.b[k]])
                if not sample:
                    for blk in range(N // P):
                        store_rows(lambda k, blk=blk: uc32_v[:, k, blk * P:(blk + 1) * P], uc32.all, P, yp[ti * T + blk * P:ti * T + (blk + 1) * P, :], False)
                else:
                    store_rows(lambda k: uc32_v[:, k, :], uc32.all, NS, ys[:, :], False)

        if NS > 0:
            with contextlib.ExitStack() as st1:
                run_group(NS, True, [0], st1)
                barrier()
        with contextlib.ExitStack() as st2:
            run_group(T, False, list(range(NT)), st2)
            S.finish()
            S.emit()
    return nc


_CACHE = {}
DBG = {}


def make_consts():
    c = np.zeros((P, 4 * P + 256), np.float32)
    c[:, 4 * P:] = np.eye(16, dtype=np.float32).reshape(1, 256)
    c[:, 0:P] = np.eye(P, dtype=np.float32)
    tri = np.triu(np.ones((P, P), np.float32))
    c[:, P:2 * P] = tri
    c[:, 2 * P:3 * P] = 1.0
    c[:, 3 * P:4 * P] = -tri
    return c


def run(inputs, SEQ, DEPTH, NS, ncores=8):
    key = (SEQ, DEPTH, NS)
    if key not in _CACHE:
        _CACHE[key] = build(SEQ, DEPTH, NS)
    nc = _CACHE[key]
    f = lambda a: np.ascontiguousarray(np.asarray(a, dtype=np.float32))
    g = {k: f(v) for k, v in inputs.items()}
    vecs = np.zeros((DEPTH, NV, D), np.float32)
    vecs[:, V_NORM] = g["norm_w"]
    vecs[:, V_MCB] = g["mconv_b"]
    vecs[:, V_GN] = g["gn_w"]
    vecs[:, V_SKIP] = g["skip"]
    vecs[:, V_CCB] = g["cconv_b"]
    vecs[:, V_CLNW] = g["cln_w"]
    vecs[:, V_CLNB] = g["cln_b"]
    vecs[:, V_BADA:V_BADA + 3] = g["b_ada"].reshape(DEPTH, 3, D)
    vecs[:, V_MCW:V_MCW + MW] = g["mconv_w"]
    vecs[:, V_CCW:V_CCW + CW] = g["cconv_w"]
    vecs[:, V_FIN] = g["final_norm_w"][None, :]
    consts = make_consts()
    shared = {k: g[k] for k in ("w_ada", "w_in", "w_q", "w_k", "w_v", "w_gate", "b_gate", "w_br_m", "w_br_c", "w_out")}
    shared["vecs"] = vecs
    shared["consts"] = consts
    in_maps = []
    for i in range(ncores):
        sl = slice(i * NS, (i + 1) * NS)
        m = dict(shared)
        m["xp"] = g["x_prompt"][i]
        m["xs"] = g["x_sample"][sl, 0, :]
        m["call"] = np.concatenate([g["c_prompt"][i:i + 1], g["c_sample"][sl]], axis=0)
        m["sC"] = g["state_mlstm_C"][:, sl]
        m["sn"] = g["state_mlstm_n"][:, sl].reshape(DEPTH, NS, H * DH)
        m["sm"] = g["state_mlstm_m"][:, sl]
        m["smc"] = g["state_mlstm_conv"][:, sl].reshape(DEPTH, NS * (MW - 1), D)
        m["scv"] = g["state_conv"][:, sl].reshape(DEPTH, NS * (CW - 1), D)
        in_maps.append({k: np.ascontiguousarray(v) for k, v in m.items()})
    res = run_bass_kernel_spmd(nc, in_maps, core_ids=list(range(ncores)))
    R = res.results
    cat = lambda name, ax: np.concatenate([np.asarray(r[name]) for r in R], axis=ax)
    stk = lambda name: np.stack([np.asarray(r[name]) for r in R], axis=1)
    y_prompt = np.stack([np.asarray(r["yp"]) for r in R], axis=0)
    y_sample = cat("ys", 0).reshape(ncores * NS, 1, D)
    Cp = stk("oCp")
    np_ = stk("onp").reshape(DEPTH, ncores, H, DH)
    mp = stk("omp")
    mcp = stk("omcp")
    cvp = stk("ocvp")
    Cs = cat("oCs", 1)
    ns = cat("ons", 1).reshape(DEPTH, ncores * NS, H, DH)
    ms = cat("oms", 1)
    mcs = cat("omcs", 1).reshape(DEPTH, ncores * NS, MW - 1, D)
    cvs = cat("ocvs", 1).reshape(DEPTH, ncores * NS, CW - 1, D)
    outs = (y_prompt, y_sample, Cp, np_, mp, mcp, cvp, Cs, ns, ms, mcs, cvs)
    return tuple(np.ascontiguousarray(o, dtype=np.float32) for o in outs)


def kernel(**inputs):
    return run(inputs, SEQ=2048, DEPTH=4, NS=16)
```

```python
import contextlib
import numpy as np
import concourse.bass as bass
import concourse.mybir as mybir
from concourse.bass_utils import run_bass_kernel_spmd

F32 = mybir.dt.float32
BF16 = mybir.dt.bfloat16
AF = mybir.ActivationFunctionType
ALU = mybir.AluOpType
AX = mybir.AxisListType

P = 128
D = 1024
KC = 8
H = 4
DH = 256
T = 512
LC = 128
MW = 4
CW = 31
EPS = 1e-6
NV = 46
V_NORM, V_MCB, V_GN, V_SKIP, V_CCB, V_CLNW, V_CLNB, V_BADA, V_MCW, V_CCW, V_FIN = 0, 1, 2, 3, 4, 5, 6, 7, 10, 14, 45


class Buf:
    __slots__ = ("w", "r")

    def __init__(self):
        self.w = None
        self.r = {}


class TB:
    def __init__(self, t, nb=1):
        self.t = t
        self.b = [Buf() for _ in range(nb)]

    @property
    def all(self):
        return self.b


class Rec:
    def __getattr__(self, name):
        def f(*a, **kw):
            self.call = (name, a, kw)
            return self
        return f


def _bind(fn):
    r = Rec()
    fn(r)
    name, a, kw = r.call
    return lambda eng: getattr(eng, name)(*a, **kw)


class Sched:
    def __init__(self, nc, stack):
        self.nc = nc
        self.engs = ["pe", "act", "dve", "pool", "sp"]
        self.prog = {e: [] for e in self.engs}
        self.semh = {}
        self.cnt = {}
        for e in ["pe", "act", "dve", "pool"]:
            self.semh[e] = stack.enter_context(nc.semaphore("sem_" + e))
            self.cnt[e] = 0
        self.KD = 8
        self.dq = {}
        for q in ["sp", "pool", "act"]:
            lst = []
            for i in range(self.KD):
                key = f"d_{q}{i}"
                self.semh[key] = stack.enter_context(nc.semaphore(key))
                self.cnt[key] = 0
                lst.append(key)
            self.dq[q] = [lst, 0]
        self.known = {e: {} for e in self.engs}
        self.pe_pending = False

    def _wait(self, e, key, val):
        if self.known[e].get(key, 0) >= val:
            return
        self.known[e][key] = val
        h = self.semh[key]
        self.prog[e].append(lambda eng, h=h, val=val: eng.wait_ge(h, val))

    def _deps(self, e, reads, writes):
        deps = {}
        for b in reads:
            if b.w:
                deps[b.w[0]] = max(deps.get(b.w[0], 0), b.w[1])
        for b in writes:
            if b.w:
                deps[b.w[0]] = max(deps.get(b.w[0], 0), b.w[1])
            for k, v in b.r.items():
                deps[k] = max(deps.get(k, 0), v)
        for k, v in deps.items():
            if k == "pe" and e == "pe":
                continue
            self._wait(e, k, v)

    def _mark(self, tok, reads, writes):
        for b in writes:
            b.w = tok
            b.r = {}
        for b in reads:
            b.r[tok[0]] = max(b.r.get(tok[0], 0), tok[1])

    def op(self, e, fn, reads=(), writes=(), sig=True):
        fn = _bind(fn)
        self._deps(e, reads, writes)
        if e == "pe" and not sig:
            tick = self.cnt[e] + 1
            self.pe_pending = True
            self.prog[e].append(lambda eng, fn=fn: fn(eng))
        else:
            self.cnt[e] += 1
            tick = self.cnt[e]
            h = self.semh[e]
            if e == "pe":
                self.pe_pending = False
            self.prog[e].append(lambda eng, fn=fn, h=h: fn(eng).then_inc(h, 1))
        self._mark((e, tick), reads, writes)

    def dma(self, q, fn, reads=(), writes=()):
        fn = _bind(fn)
        lst, i = self.dq[q]
        key = lst[i % self.KD]
        self.dq[q][1] = i + 1
        if self.cnt[key] > 0:
            self._wait(q, key, self.cnt[key])
        self._deps(q, reads, writes)
        self.cnt[key] += 16
        val = self.cnt[key]
        h = self.semh[key]
        self.prog[q].append(lambda eng, fn=fn, h=h: fn(eng).then_inc(h, 16))
        self._mark((key, val), reads, writes)

    def finish(self):
        assert not self.pe_pending
        for key, c in self.cnt.items():
            if c > 0:
                self._wait("sp", key, c)

    def emit(self):
        nc = self.nc
        prog = self.prog
        with nc.Block() as block:

            @block.tensor
            def _(e):
                for f in prog["pe"]:
                    f(e)

            @block.scalar
            def _(e):
                for f in prog["act"]:
                    f(e)

            @block.vector
            def _(e):
                for f in prog["dve"]:
                    f(e)

            @block.gpsimd
            def _(e):
                for f in prog["pool"]:
                    f(e)

            @block.sync
            def _(e):
                for f in prog["sp"]:
                    f(e)


class Ring:
    def __init__(self, items):
        self.items = items
        self.i = 0

    def next(self):
        it = self.items[self.i % len(self.items)]
        self.i += 1
        return it


def bcast(ap, shape):
    return ap.broadcast_to(list(shape))


def build(SEQ, DEPTH, NS):
    nc = bass.Bass("TRN2", target_bir_lowering=False)
    NT = SEQ // T
    NTOK = 1 + NS

    def din(name, shape):
        return nc.dram_tensor(name, list(shape), F32, kind="ExternalInput").ap()

    def dout(name, shape):
        return nc.dram_tensor(name, list(shape), F32, kind="ExternalOutput").ap()

    xp = din("xp", [SEQ, D])
    xs = din("xs", [NS, D])
    call = din("call", [NTOK, D])
    sC = din("sC", [DEPTH, NS, H, DH, DH])
    sn = din("sn", [DEPTH, NS, H * DH])
    sm = din("sm", [DEPTH, NS, H])
    smc = din("smc", [DEPTH, NS * (MW - 1), D])
    scv = din("scv", [DEPTH, NS * (CW - 1), D])
    w_ada = din("w_ada", [DEPTH, D, 3 * D])
    w_in = din("w_in", [DEPTH, D, 7 * D])
    w_q = din("w_q", [DEPTH, H, DH, DH])
    w_k = din("w_k", [DEPTH, H, DH, DH])
    w_v = din("w_v", [DEPTH, H, DH, DH])
    w_gate = din("w_gate", [DEPTH, 3 * D, 8])
    b_gate = din("b_gate", [DEPTH, 8])
    w_br_m = din("w_br_m", [DEPTH, D, D])
    w_br_c = din("w_br_c", [DEPTH, D, D])
    w_out = din("w_out", [DEPTH, D, D])
    vecs = din("vecs", [DEPTH, NV, D])
    consts = din("consts", [P, 4 * P + 256])

    yp = dout("yp", [SEQ, D])
    ys = dout("ys", [NS, D])
    oCp = dout("oCp", [DEPTH, H, DH, DH])
    onp = dout("onp", [DEPTH, H * DH])
    omp = dout("omp", [DEPTH, H])
    omcp = dout("omcp", [DEPTH, MW - 1, D])
    ocvp = dout("ocvp", [DEPTH, CW - 1, D])
    oCs = dout("oCs", [DEPTH, NS, H, DH, DH])
    ons = dout("ons", [DEPTH, NS, H * DH])
    oms = dout("oms", [DEPTH, NS, H])
    omcs = dout("omcs", [DEPTH, NS * (MW - 1), D])
    ocvs = dout("ocvs", [DEPTH, NS * (CW - 1), D])

    dgscr = nc.dram_tensor("dgscr", [DEPTH, KC, P, CW * P], BF16).ap()
    stack = contextlib.ExitStack()
    with stack:
        S = Sched(nc, stack)
        uid = [0]

        def barrier():
            snap = dict(S.cnt)
            for e in S.engs:
                for key, c in snap.items():
                    if c > 0 and key != e:
                        S._wait(e, key, c)


        pst = contextlib.ExitStack()

        def sb(shape, dt, nb=1, st=stack):
            uid[0] += 1
            t = st.enter_context(nc.sbuf_tensor(f"t{uid[0]}", list(shape), dt))
            return TB(t, nb)

        def ps(shape, dt):
            uid[0] += 1
            t = stack.enter_context(nc.psum_tensor(f"p{uid[0]}", list(shape), dt))
            return TB(t, 1)

        psA = Ring([ps([P, 512], F32) for _ in range(2)])
        _po = [ps([P, 1024], F32) for _ in range(2)]
        for _t in _po:
            _t.b = [Buf(), Buf()]
        psOr = Ring(_po)
        _halves = []
        for _t in _po:
            for _i in range(2):
                _h = TB(_t.t[:, _i * 512:(_i + 1) * 512], 1)
                _h.b = [_t.b[_i]]
                _halves.append(_h)
        psA6 = Ring(list(psA.items) + _halves)
        psT = ps([P, 1024], BF16)
        psS = ps([P, 512], F32)

        cst = sb([P, 4 * P + 256], F32)
        cbf = sb([P, 3 * P], BF16)
        epsc = sb([P, 1], F32)
        vcol = sb([P, DEPTH, KC, NV], F32)
        NWB = 4
        wring = Ring([sb([P, KC, 256], BF16) for _ in range(NWB)])
        modall = sb([P, DEPTH, 24, NTOK], F32)
        Acol = sb([P, DEPTH, KC, NTOK], F32)
        n32 = sb([P, DEPTH, H, 2], F32, nb=DEPTH)
        mst = sb([H, DEPTH], F32, nb=DEPTH)
        hxm = sb([P, DEPTH, KC, MW - 1], BF16, nb=DEPTH)
        hu = sb([P, DEPTH, KC, CW - 1], BF16, nb=DEPTH)
        vstg = sb([NV, D], F32, 1, pst)
        cstg = sb([NTOK, D], F32, 1, pst)
        csl = sb([NTOK, D], F32, 1, pst)
        scT = sb([P, KC, NTOK], BF16, KC, pst)
        S.dma("sp", lambda e: e.dma_start(out=cst.t[:], in_=consts[:, :]), writes=cst.all)
        ident = cst.t[:, 0:P]
        triu = cst.t[:, P:2 * P]
        ones = cst.t[:, 2 * P:3 * P]
        ntri = cst.t[:, 3 * P:4 * P]
        identb = cbf.t[:, 0:P]
        maskb = cbf.t[:, P:2 * P]
        S.op("dve", lambda e: e.tensor_copy(out=cbf.t[:], in_=cst.t[:, 0:3 * P]), reads=cst.all, writes=cbf.all)
        onesb = cbf.t[:, 2 * P:3 * P]
        CST = cst.all + cbf.all
        S.op("dve", lambda e: e.memset(epsc.t[:], EPS), writes=epsc.all)

        for l in range(DEPTH):
            S.dma("sp", lambda e, l=l: e.dma_start(out=vstg.t[:], in_=vecs[l, :, :]), writes=vstg.all)
            for half in range(2):
                pa = psA.next()
                for kk in range(4):
                    k = half * 4 + kk
                    S.op("pe", lambda e, pa=pa, k=k, kk=kk: e.transpose(out=pa.t[:, kk * NV:(kk + 1) * NV], in_=vstg.t[:, k * P:(k + 1) * P], identity=ident[0:NV, 0:NV]),
                         reads=vstg.all + CST, writes=pa.all, sig=(kk == 3))
                S.op("dve", lambda e, pa=pa, l=l, half=half: e.tensor_copy(out=vcol.t[:, l, half * 4:(half + 1) * 4, :], in_=pa.t[:, 0:4 * NV].rearrange("p (k v) -> p k v", v=NV)),
                     reads=pa.all, writes=vcol.all)

        def vc(l, k, v):
            return vcol.t[:, l, k, v:v + 1]


        def wload(src3, kk, cols):
            w = wring.next()
            S.dma("pool", lambda e, w=w: e.dma_start(out=w.t[:, 0:kk, 0:cols], in_=src3), writes=w.all)
            return w

        def wmat(wd, f0, cols):
            return wload(wd.rearrange("(k p) f -> p k f", p=P)[:, :, f0:f0 + cols], KC, cols)

        def proj(wd, f0, nfc, src, N, evac, src_cols=None):
            for _ in proj_gen(wd, f0, nfc, src, N, evac, src_cols, ring=psA6):
                pass

        def proj_gen(wd, f0, nfc, src, N, evac, src_cols=None, ring=None):
            ring = ring or psA
            done = 0
            while done < nfc:
                n = min(2, nfc - done)
                w = wmat(wd, f0 + done * P, n * P)
                for c in range(n):
                    pa = ring.next()
                    for k in range(KC):
                        rhs = src.t[:, k, 0:N] if src_cols is None else src.t[:, k, src_cols[0]:src_cols[1]]
                        S.op("pe", lambda e, pa=pa, w=w, c=c, k=k, rhs=rhs: e.matmul(out=pa.t[:, 0:N], lhsT=w.t[:, k, c * P:(c + 1) * P], rhs=rhs, start=(k == 0), stop=(k == KC - 1)),
                             reads=w.all + [src.b[k]], writes=pa.all, sig=(k == KC - 1))
                    evac(done + c, pa)
                done += n
                yield

        S.dma("sp", lambda e: e.dma_start(out=cstg.t[:], in_=call[:, :]), writes=cstg.all)
        S.op("act", lambda e: e.activation(out=csl.t[:], in_=cstg.t[:], func=AF.Silu), reads=cstg.all, writes=csl.all)
        for half in range(2):
            pa = psA.next()
            for kk in range(4):
                k = half * 4 + kk
                S.op("pe", lambda e, pa=pa, k=k, kk=kk: e.transpose(out=pa.t[:, kk * NTOK:(kk + 1) * NTOK], in_=csl.t[:, k * P:(k + 1) * P], identity=ident[0:NTOK, 0:NTOK]),
                     reads=csl.all + CST, writes=pa.all, sig=(kk == 3))
            S.op("dve", lambda e, pa=pa, half=half: e.tensor_copy(out=scT.t[:, half * 4:(half + 1) * 4, :], in_=pa.t[:, 0:4 * NTOK].rearrange("p (k v) -> p k v", v=NTOK)),
                 reads=pa.all, writes=scT.b[half * 4:(half + 1) * 4])
        for l in range(DEPTH):
            def ev(fc, pa, l=l):
                S.op("act", lambda e: e.activation(out=modall.t[:, l, fc, :], in_=pa.t[:, 0:NTOK], func=AF.Identity, bias=vc(l, fc % KC, V_BADA + fc // KC), scale=1.0),
                     reads=pa.all + vcol.all, writes=modall.all)
            proj(w_ada[l], 0, 24, scT, NTOK, ev)
            S.op("dve", lambda e, l=l: e.scalar_tensor_tensor(out=Acol.t[:, l, :, :], in0=modall.t[:, l, 8:16, :], scalar=1.0, in1=bcast(vcol.t[:, l, :, V_NORM:V_NORM + 1], [P, KC, NTOK]), op0=ALU.add, op1=ALU.mult),
                 reads=modall.all + vcol.all, writes=Acol.all)
        MOD = modall.all + Acol.all + vcol.all
        barrier()
        pst.close()
        dC = [Buf() for _ in range(DEPTH)]
        dgB = [[Buf() for _ in range(KC)] for _ in range(DEPTH)]
        use_scr = NS > 0

        for tb in (n32, mst, hxm, hu):
            S.op("dve", lambda e, tb=tb: e.memset(tb.t[:], 0.0), writes=tb.all)

        def run_group(N, sample, tiles, st):
            def A(shape, dt, nb=1):
                return sb(shape, dt, nb, st)

            xT = A([P, KC, N], F32, KC)
            hT = A([P, KC, N], BF16, KC)
            XU = A([P, KC, CW - 1 + N], BF16, KC)
            xm = TB(XU.t, 1); xm.b = XU.b
            uu = XU
            XO = CW - MW
            szm = A([P, KC, N], BF16, KC)
            sgm = A([P, KC, N], BF16, KC)
            xc = A([P, KC, N], BF16, KC)
            qT = A([P, KC, N], BF16, KC)
            kT = A([P, KC, N], BF16, KC)
            vT = A([P, 2, N], BF16, 1)
            NB = (N + LC - 1) // LC
            RB = LC if not sample else NS
            ktok = A([LC, NB, D], BF16, NB)
            hftok = ktok
            szc = szm
            sgc = sgm
            ymg = xc
            if not sample:
                big = A([P, KC * N * 2], BF16, KC)
                vtok = TB(None, 1); vtok.b = big.b[0:NB]
                vtok_v = big.t[0:LC, 0:NB * D].rearrange("p (b d) -> p b d", d=D)
                uc32 = TB(None, 1); uc32.b = big.b
                uc32_v = big.t[:].bitcast(F32).rearrange("p (k n) -> p k n", n=N)
            else:
                vtok = A([LC, NB, D], BF16, NB)
                vtok_v = vtok.t[:]
                uc32 = A([P, KC, N], F32, KC)
                uc32_v = uc32.t[:]
            hmg = qT
            sgb = qT if sample else A([P, KC, N], BF16, KC)
            ucg = kT
            mrg = hT
            tmpf = Ring([A([P, N], F32) for _ in range(4)])
            sq = tmpf
            tmpb = Ring([A([P, N], BF16) for _ in range(2)])
            sqb = Ring([A([P, N], BF16) for _ in range(2)])
            rstd = A([P, N], F32)
            meanT = A([P, N], F32)
            dgm = Ring([A([P, MW, P], BF16) for _ in range(2)])
            dgc = Ring([A([P, CW, P], BF16) for _ in range(2)])
            stg = Ring([A([P, D], F32) for _ in range(1 if not sample else 2)])
            gsb = A([LC, NB, 8], F32)
            bgt = A([LC, 8], F32)
            gt = [A([LC, NB, 4], F32) for _ in range(4)]
            if not sample:
                C32 = A([P, H, 2, DH], F32, H)
                Cb = A([P, H, 2, DH], BF16, H)
                nb_ = A([P, H, 2], BF16)
                rowA = A([H, N], F32)
                rowF = A([H, N], F32)
                sm8 = [A([H, NB], F32) for _ in range(6)]
                adg = A([H, NB, H], F32)
                abc = A([P, NB, H], F32)
                ctk = A([LC, NB, 8], F32)
                cbk = A([LC, NB, H], BF16)
                Sm = Ring([A([LC, H, LC], BF16) for _ in range(NB)])
                st6 = A([LC, H, 6], F32)
                mv = A([LC, H, 2], F32)
                smls = [[A([LC, H], F32) for _ in range(5)] for _ in range(2)]
                st6s = [A([LC, H, 6], F32) for _ in range(2)]
                mvs = [A([LC, H, 2], F32) for _ in range(2)]
            else:
                hxmS = A([P, KC, MW - 1, NS], BF16, KC)
                huS = A([P, KC, CW - 1, NS], BF16, KC)
                qtok = A([NS, D], F32)
                qTf = A([P, KC, NS], F32, KC)
                qm = A([P, KC, NS, NS], BF16, KC)
                kexp = A([NS, 4, D], BF16)
                Cin = Ring([A([P, H, 2, DH], F32) for _ in range(3)])
                Cbf = Ring([A([P, H, 2, DH], BF16) for _ in range(2)])
                Cout = Ring([A([P, H, 2, DH], F32) for _ in range(2)])
                nst = A([NS, H, DH], F32)
                nnew = A([NS, H, DH], F32)
                mprev = A([NS, H], F32)
                s16 = [A([NS, H], F32) for _ in range(10)]
                dexp = A([NS, NS, H], F32)
                dbc = A([P, NS, H], F32)
                prod = A([NS, D], F32)
                numt = A([NS, D], F32)
                st6 = A([NS, H, 6], F32)
                mv = A([NS, H, 2], F32)

            for _nm, _val in list(locals().items()):
                if isinstance(_val, TB) and _val.t is not None:
                    DBG[(sample, _nm)] = _val.t.name

            def modv(l, which, k):
                if not sample:
                    return modall.t[:, l, which * 8 + k, 0:1]
                return modall.t[:, l, which * 8 + k, 1:NTOK]

            def Av(l, k):
                if not sample:
                    return Acol.t[:, l, k, 0:1]
                return Acol.t[:, l, k, 1:NTOK]

            rms_pending = []

            def rms_flush():
                while rms_pending:
                    k, s_ = rms_pending.pop(0)
                    S.op("pe", lambda e: e.matmul(out=psS.t[:, 0:N], lhsT=onesb, rhs=s_.t[:], start=(k == 0), stop=(k == KC - 1)),
                         reads=s_.all + CST, writes=psS.all, sig=True)

            def rms_chunk(k, defer=False):
                rms_flush()
                s_ = sqb.next()
                S.op("act", lambda e: e.activation(out=s_.t[:], in_=xT.t[:, k, :], func=AF.Square), reads=[xT.b[k]], writes=s_.all)
                rms_pending.append((k, s_))
                if not defer:
                    rms_flush()

            def rms_finish():
                S.op("act", lambda e: e.activation(out=rstd.t[:], in_=psS.t[:, 0:N], func=AF.Sqrt, bias=epsc.t[:, 0:1], scale=1.0 / D), reads=psS.all + epsc.all, writes=rstd.all)
                S.op("dve", lambda e: e.reciprocal(out=rstd.t[:], in_=rstd.t[:]), reads=rstd.all, writes=rstd.all)

            def build_diag(l, W, vbase, ring, k):
                dg = ring.next()
                if W == CW and use_scr and not sample:
                    S.dma("sp", lambda e: e.dma_start(out=dg.t[:].rearrange("p j m -> p (j m)"), in_=dgscr[l, k]), reads=[dgB[l][k]], writes=dg.all)
                    return dg
                S.op("dve", lambda e: e.tensor_tensor(out=dg.t[:], in0=bcast(identb.unsqueeze(1), [P, W, P]), in1=bcast(vcol.t[:, l, k, vbase:vbase + W].unsqueeze(2), [P, W, P]), op=ALU.mult),
                     reads=CST + vcol.all, writes=dg.all)
                if W == CW and use_scr and sample:
                    S.dma("sp", lambda e: e.dma_start(out=dgscr[l, k], in_=dg.t[:].rearrange("p j m -> p (j m)")), reads=dg.all, writes=[dgB[l][k]])
                return dg

            def dwconv(l, W, vbase, ring, src_of, evac, pre=()):
                for k in range(KC):
                    dg = pre[k] if k < len(pre) else build_diag(l, W, vbase, ring, k)
                    pa = psA6.next()
                    for j in range(W):
                        rhs, rb = src_of(k, j)
                        S.op("pe", lambda e, dg=dg, j=j, pa=pa, rhs=rhs: e.matmul(out=pa.t[:, 0:N], lhsT=dg.t[:, j, :], rhs=rhs, start=(j == 0), stop=(j == W - 1)),
                             reads=dg.all + rb, writes=pa.all, sig=(j == W - 1))
                    evac(k, pa)

            def store_rows(src_ap_of_k, src_bufs, R, dram_rows, bf):
                s_ = stg.next()
                for half in range(2):
                    if bf:
                        po = psT
                    else:
                        po = psA.next()
                    for kk in range(4):
                        k = half * 4 + kk
                        S.op("pe", lambda e, po=po, k=k, kk=kk: e.transpose(out=po.t[0:R, kk * P:(kk + 1) * P], in_=src_ap_of_k(k), identity=(identb if bf else ident)),
                             reads=src_bufs + CST, writes=po.all, sig=(kk == 3))
                    S.op("dve", lambda e, po=po, half=half, s_=s_: e.tensor_copy(out=s_.t[0:R, half * 512:(half + 1) * 512], in_=po.t[0:R, 0:512]), reads=po.all, writes=s_.all)
                S.dma("sp", lambda e, s_=s_: e.dma_start(out=dram_rows, in_=s_.t[0:R, :]), reads=s_.all)

            def load_rows(dram_rows, R, writer):
                s_ = stg.next()
                S.dma("sp", lambda e, s_=s_: e.dma_start(out=s_.t[0:R, :], in_=dram_rows), writes=s_.all)
                nper = max(1, min(4, 512 // R))
                k = 0
                while k < KC:
                    n = min(nper, KC - k)
                    pa = psA.next()
                    for kk in range(n):
                        S.op("pe", lambda e, pa=pa, k=k, kk=kk, s_=s_: e.transpose(out=pa.t[:, kk * R:(kk + 1) * R], in_=s_.t[0:R, (k + kk) * P:(k + kk + 1) * P], identity=ident[0:R, 0:R]),
                             reads=s_.all + CST, writes=pa.all, sig=(kk == n - 1))
                    writer(k, n, pa)
                    k += n

            def layer(l, ti):
                last_tile = (ti == NT - 1)
                rms_finish()
                for k in range(KC):
                    tf = tmpf.next()
                    if not sample:
                        S.op("dve", lambda e, tf=tf, k=k: e.scalar_tensor_tensor(out=tf.t[:], in0=xT.t[:, k, :], scalar=Av(l, k), in1=rstd.t[:], op0=ALU.mult, op1=ALU.mult),
                             reads=[xT.b[k]] + rstd.all + MOD, writes=tf.all)
                        S.op("act", lambda e, tf=tf, k=k: e.activation(out=hT.t[:, k, :], in_=tf.t[:], func=AF.Identity, bias=modv(l, 0, k), scale=1.0),
                             reads=tf.all + MOD, writes=[hT.b[k]])
                    else:
                        S.op("dve", lambda e, tf=tf, k=k: e.tensor_tensor(out=tf.t[:], in0=xT.t[:, k, :], in1=rstd.t[:], op=ALU.mult), reads=[xT.b[k]] + rstd.all, writes=tf.all)
                        S.op("dve", lambda e, tf=tf, k=k: e.tensor_tensor(out=tf.t[:], in0=tf.t[:], in1=Av(l, k), op=ALU.mult), reads=tf.all + MOD, writes=tf.all)
                        S.op("dve", lambda e, tf=tf, k=k: e.tensor_tensor(out=hT.t[:, k, :], in0=tf.t[:], in1=modv(l, 0, k), op=ALU.add), reads=tf.all + MOD, writes=[hT.b[k]])

                if not sample:
                    S.op("dve", lambda e: e.tensor_copy(out=XU.t[:, :, XO:XO + MW - 1], in_=hxm.t[:, l, :, :]), reads=[hxm.b[l]], writes=xm.all)
                else:
                    def wr_m(k, n, pa):
                        S.op("dve", lambda e: e.tensor_copy(out=hxmS.t[:, k:k + n, :, :].rearrange("p k j b -> p k b j"), in_=pa.t[:, 0:n * NS * (MW - 1)].rearrange("p (k b j) -> p k b j", k=n, b=NS)),
                             reads=pa.all, writes=hxmS.b[k:k + n])
                    load_rows(smc[l, :, :], NS * (MW - 1), wr_m)
                    SB4 = 4
                    for b0 in range(0, NS, SB4):
                        def wr_c(k, n, pa, b0=b0):
                            S.op("dve", lambda e: e.tensor_copy(out=huS.t[:, k:k + n, :, b0:b0 + SB4].rearrange("p k j b -> p k b j"), in_=pa.t[:, 0:n * SB4 * (CW - 1)].rearrange("p (k b j) -> p k b j", k=n, b=SB4)),
                                 reads=pa.all, writes=huS.b[k:k + n])
                        load_rows(scv[l, b0 * (CW - 1):(b0 + SB4) * (CW - 1), :], SB4 * (CW - 1), wr_c)
                    S.dma("sp", lambda e: e.dma_start(out=omcs[l].rearrange("(b j) d -> b j d", j=MW - 1)[:, 0:MW - 2, :], in_=smc[l].rearrange("(b j) d -> b j d", j=MW - 1)[:, 1:MW - 1, :]))
                    S.dma("sp", lambda e: e.dma_start(out=ocvs[l].rearrange("(b j) d -> b j d", j=CW - 1)[:, 0:CW - 2, :], in_=scv[l].rearrange("(b j) d -> b j d", j=CW - 1)[:, 1:CW - 1, :]))

                def ev_xm(fc, pa):
                    S.op("dve", lambda e: e.tensor_copy(out=XU.t[:, fc, CW - 1:CW - 1 + N], in_=pa.t[:, 0:N]), reads=pa.all, writes=[xm.b[fc]])
                proj(w_in[l], 0 * D, KC, hT, N, ev_xm)
                pre_c = [build_diag(l, CW, V_CCW, dgc, 0), build_diag(l, CW, V_CCW, dgc, 1)]

                def ev_zm(fc, pa):
                    S.op("act", lambda e: e.activation(out=szm.t[:, fc, :], in_=pa.t[:, 0:N], func=AF.Silu), reads=pa.all, writes=[szm.b[fc]])

                def src_m(k, j):
                    if not sample:
                        return XU.t[:, k, XO + j:XO + j + N], [xm.b[k]]
                    if j < MW - 1:
                        return hxmS.t[:, k, j, :], [hxmS.b[k]]
                    return XU.t[:, k, CW - 1:CW - 1 + N], [xm.b[k]]

                def ev_mc(k, pa):
                    S.op("act", lambda e: e.activation(out=xc.t[:, k, :], in_=pa.t[:, 0:N], func=AF.Silu, bias=vc(l, k, V_MCB), scale=1.0), reads=pa.all + vcol.all, writes=[xc.b[k]])
                dwconv(l, MW, V_MCW, dgm, src_m, ev_mc)

                def ev_gm(fc, pa):
                    S.op("act", lambda e: e.activation(out=sgm.t[:, fc, :], in_=pa.t[:, 0:N], func=AF.Sigmoid), reads=pa.all, writes=[sgm.b[fc]])

                if not sample:
                    S.op("dve", lambda e: e.tensor_copy(out=hxm.t[:, l, :, :], in_=XU.t[:, :, XO + N:XO + N + MW - 1]), reads=xm.all, writes=[hxm.b[l]])
                    if last_tile:
                        store_rows(lambda k: hxm.t[:, l, k, :], [hxm.b[l]], MW - 1, omcp[l, :, :], True)
                else:
                    store_rows(lambda k: XU.t[:, k, CW - 1:CW - 1 + N], xm.all, NS, omcs[l].rearrange("(b j) d -> b j d", j=MW - 1)[:, MW - 2, :], True)

                wq = wload(w_q[l].rearrange("h (k p) e -> p (h k) e", p=P), KC, DH)
                wk = wload(w_k[l].rearrange("h (k p) e -> p (h k) e", p=P), KC, DH)
                wv = wload(w_v[l].rearrange("h (k p) e -> p (h k) e", p=P), KC, DH)
                wg = sb_wg
                S.dma("pool", lambda e: e.dma_start(out=wg.t[:], in_=w_gate[l].rearrange("(c p) g -> p c g", p=P)), writes=wg.all)
                S.dma("sp", lambda e: e.dma_start(out=bgt.t[:], in_=b_gate[l:l + 1, :].partition_broadcast(LC)), writes=bgt.all)

                def fm_proj(w, src, src_off, dst, scale, h):
                    for ec in range(2):
                        pa = psA6.next()
                        for k in range(2):
                            S.op("pe", lambda e, pa=pa, k=k, ec=ec: e.matmul(out=pa.t[:, 0:N], lhsT=w.t[:, h * 2 + k, ec * P:(ec + 1) * P], rhs=src.t[:, h * 2 + k, src_off:src_off + N], start=(k == 0), stop=(k == 1)),
                                 reads=w.all + [src.b[h * 2 + k]], writes=pa.all, sig=(k == 1))
                        if dst is vT:
                            S.op("act", lambda e, pa=pa, ec=ec: e.activation(out=vT.t[:, ec, :], in_=pa.t[:, 0:N], func=AF.Copy, scale=scale), reads=pa.all, writes=vT.all)
                        elif dst is kT:
                            S.op("dve", lambda e, pa=pa, ec=ec: e.tensor_scalar(out=dst.t[:, h * 2 + ec, :], in0=pa.t[:, 0:N], scalar1=scale, scalar2=None, op0=ALU.mult), reads=pa.all, writes=[dst.b[h * 2 + ec]])
                        else:
                            S.op("act", lambda e, pa=pa, ec=ec: e.activation(out=dst.t[:, h * 2 + ec, :], in_=pa.t[:, 0:N], func=AF.Copy, scale=scale), reads=pa.all, writes=[dst.b[h * 2 + ec]])

                def gate_mm(src_ap, srcb, cidx, first, last):
                    for blk in range(NB):
                        S.op("pe", lambda e, blk=blk: e.matmul(out=psS.t[0:RB, blk * 8:(blk + 1) * 8], lhsT=src_ap(blk), rhs=wg.t[:, cidx, :], start=(first and blk == 0), stop=last, skip_group_check=True),
                             reads=srcb + wg.all, writes=psS.all, sig=(blk == NB - 1))

                for h in range(H):
                    fm_proj(wq, xc, 0, qT, 1.0, h)
                    fm_proj(wk, xc, 0, kT, DH ** -0.5, h)
                    fm_proj(wv, xm, CW - 1, vT, 1.0, h)
                    for blk in range(NB):
                        for (w, src, off, dst, scale, eng) in ((wk, xc, 0, ktok, DH ** -0.5, "act"), (wv, xm, CW - 1, vtok, 1.0, "dve")) + (((wq, xc, 0, None, 1.0, "dve"),) if sample else ()):
                            pa = psA6.next()
                            for k in range(2):
                                S.op("pe", lambda e, pa=pa, k=k, w=w, src=src, off=off, blk=blk: e.matmul(out=pa.t[0:RB, 0:DH], lhsT=src.t[:, h * 2 + k, off + blk * LC:off + blk * LC + RB], rhs=w.t[:, h * 2 + k, 0:DH], start=(k == 0), stop=(k == 1)),
                                     reads=w.all + [src.b[h * 2 + k]], writes=pa.all, sig=(k == 1))
                            if dst is None:
                                S.op("dve", lambda e, pa=pa: e.tensor_copy(out=qtok.t[:, h * DH:(h + 1) * DH], in_=pa.t[0:RB, 0:DH]), reads=pa.all, writes=qtok.all)
                            elif eng == "act":
                                S.op("act", lambda e, pa=pa, dst=dst, blk=blk, scale=scale: e.activation(out=dst.t[0:RB, blk, h * DH:(h + 1) * DH], in_=pa.t[0:RB, 0:DH], func=AF.Copy, scale=scale), reads=pa.all, writes=[dst.b[blk]])
                            else:
                                S.op("dve", lambda e, pa=pa, dst=dst, blk=blk: e.tensor_copy(out=vtok_v[0:RB, blk, h * DH:(h + 1) * DH], in_=pa.t[0:RB, 0:DH]), reads=pa.all, writes=[dst.b[blk]])
                    for ec in range(2):
                        gate_mm(lambda blk, ec=ec: qT.t[:, h * 2 + ec, blk * LC:blk * LC + RB], [qT.b[h * 2 + ec]], h * 2 + ec, (h == 0 and ec == 0), False)
                        gate_mm(lambda blk, ec=ec: kT.t[:, h * 2 + ec, blk * LC:blk * LC + RB], [kT.b[h * 2 + ec]], KC + h * 2 + ec, False, False)
                    for ec in range(2):
                        gate_mm(lambda blk, ec=ec: vT.t[:, ec, blk * LC:blk * LC + RB], vT.all, 2 * KC + h * 2 + ec, False, (h == H - 1 and ec == 1))

                gi, lf, t1, t2 = gt
                S.op("dve", lambda e: e.tensor_tensor(out=gsb.t[0:RB], in0=psS.t[0:RB, 0:NB * 8].rearrange("p (b g) -> p b g", g=8), in1=bcast(bgt.t[0:RB].unsqueeze(1), [RB, NB, 8]), op=ALU.add),
                     reads=psS.all + bgt.all, writes=gsb.all)
                S.op("act", lambda e: e.activation(out=t1.t[0:RB], in_=gsb.t[0:RB, :, 4:8], func=AF.Abs), reads=gsb.all, writes=t1.all)
                S.op("act", lambda e: e.activation(out=t1.t[0:RB], in_=t1.t[0:RB], func=AF.Exp, scale=-1.0), reads=t1.all, writes=t1.all)
                S.op("act", lambda e: e.activation(out=t1.t[0:RB], in_=t1.t[0:RB], func=AF.Ln, bias=1.0, scale=1.0), reads=t1.all, writes=t1.all)
                S.op("dve", lambda e: e.tensor_scalar_min(out=t2.t[0:RB], in0=gsb.t[0:RB, :, 4:8], scalar1=0.0), reads=gsb.all, writes=t2.all)
                S.op("dve", lambda e: e.tensor_sub(out=lf.t[0:RB], in0=t2.t[0:RB], in1=t1.t[0:RB]), reads=t1.all + t2.all, writes=lf.all)
                S.op("dve", lambda e: e.tensor_copy(out=gi.t[0:RB], in_=gsb.t[0:RB, :, 0:4]), reads=gsb.all, writes=gi.all)

                def ev_gb(fc, pa):
                    S.op("act", lambda e: e.activation(out=sgb.t[:, fc, :], in_=pa.t[:, 0:N], func=AF.Sigmoid), reads=pa.all, writes=[sgb.b[fc]])

                def ev_ga(fc, pa):
                    S.op("dve", lambda e: e.tensor_tensor(out=uu.t[:, fc, CW - 1:CW - 1 + N], in0=pa.t[:, 0:N], in1=sgb.t[:, fc, :], op=ALU.mult), reads=pa.all + [sgb.b[fc]], writes=[uu.b[fc]])

                if not sample:
                    S.op("dve", lambda e: e.tensor_copy(out=uu.t[:, :, 0:CW - 1], in_=hu.t[:, l, :, :]), reads=[hu.b[l]], writes=uu.all)

                    def fill():
                        yield from proj_gen(w_in[l], 3 * D, KC, hT, N, ev_gb)
                        yield from proj_gen(w_in[l], 2 * D, KC, hT, N, ev_ga)

                    def fill2():
                        yield from proj_gen(w_in[l], 1 * D, KC, hT, N, ev_zm, ring=psA6)
                        yield from proj_gen(w_in[l], 5 * D, KC, hT, N, ev_gm, ring=psA6)
                    filler = fill()
                    filler2 = fill2()
                    mlstm_prompt(l, ti, gi, lf, filler, filler2)
                    for _ in filler:
                        pass
                else:
                    proj(w_in[l], 1 * D, KC, hT, N, ev_zm)
                    proj(w_in[l], 5 * D, KC, hT, N, ev_gm)
                    mlstm_sample(l, gi, lf)

                if sample:
                    proj(w_in[l], 3 * D, KC, hT, N, ev_gb)
                    proj(w_in[l], 2 * D, KC, hT, N, ev_ga)
                def src_c(k, j):
                    if not sample:
                        return uu.t[:, k, j:j + N], [uu.b[k]]
                    if j < CW - 1:
                        return huS.t[:, k, j, :], [huS.b[k]]
                    return uu.t[:, k, CW - 1:CW - 1 + N], [uu.b[k]]

                def ev_cc(k, pa):
                    S.op("act", lambda e: e.activation(out=uc32_v[:, k, :], in_=pa.t[:, 0:N], func=AF.Identity, bias=vc(l, k, V_CCB), scale=1.0), reads=pa.all + vcol.all, writes=[uc32.b[k]])
                dwconv(l, CW, V_CCW, dgc, src_c, ev_cc, pre_c)

                for fc in range(KC):
                    for blk in range(NB):
                        S.op("pe", lambda e, fc=fc, blk=blk: e.transpose(out=psT.t[:, blk * LC:blk * LC + RB], in_=hftok.t[0:RB, blk, fc * P:(fc + 1) * P], identity=identb[0:RB, 0:RB]),
                             reads=[hftok.b[blk]] + CST, writes=psT.all, sig=(blk == NB - 1))
                    tf = tmpf.next()
                    tb_ = tmpb.next()
                    S.op("dve", lambda e, fc=fc, tf=tf: e.scalar_tensor_tensor(out=tf.t[:], in0=psT.t[:, 0:N], scalar=vc(l, fc, V_GN), in1=szm.t[:, fc, :], op0=ALU.mult, op1=ALU.mult),
                         reads=psT.all + [szm.b[fc]] + vcol.all, writes=tf.all)
                    S.op("dve", lambda e, fc=fc, tb_=tb_: e.scalar_tensor_tensor(out=tb_.t[:], in0=xc.t[:, fc, :], scalar=vc(l, fc, V_SKIP), in1=szm.t[:, fc, :], op0=ALU.mult, op1=ALU.mult),
                         reads=[xc.b[fc], szm.b[fc]] + vcol.all, writes=tb_.all)
                    S.op("dve", lambda e, fc=fc, tf=tf, tb_=tb_: e.tensor_tensor(out=hmg.t[:, fc, :], in0=tf.t[:], in1=tb_.t[:], op=ALU.add), reads=tf.all + tb_.all, writes=[hmg.b[fc]])

                def ev_ym(fc, pa):
                    S.op("dve", lambda e: e.tensor_tensor(out=ymg.t[:, fc, :], in0=pa.t[:, 0:N], in1=sgm.t[:, fc, :], op=ALU.mult), reads=pa.all + [sgm.b[fc]], writes=[ymg.b[fc]])
                proj(w_br_m[l], 0, KC, hmg, N, ev_ym)


                def ev_gc(fc, pa):
                    S.op("act", lambda e: e.activation(out=sgc.t[:, fc, :], in_=pa.t[:, 0:N], func=AF.Sigmoid), reads=pa.all, writes=[sgc.b[fc]])


                def ev_zc(fc, pa):
                    S.op("act", lambda e: e.activation(out=szc.t[:, fc, :], in_=pa.t[:, 0:N], func=AF.Silu), reads=pa.all, writes=[szc.b[fc]])

                if not sample:
                    S.op("dve", lambda e: e.tensor_copy(out=hu.t[:, l, :, :], in_=uu.t[:, :, N:N + CW - 1]), reads=uu.all, writes=[hu.b[l]])
                    if last_tile:
                        store_rows(lambda k: hu.t[:, l, k, :], [hu.b[l]], CW - 1, ocvp[l, :, :], True)
                else:
                    store_rows(lambda k: uu.t[:, k, CW - 1:CW - 1 + N], uu.all, NS, ocvs[l].rearrange("(b j) d -> b j d", j=CW - 1)[:, CW - 2, :], True)

                pm = psA.next()
                pq = psA.next()
                for k in range(KC):
                    S.op("pe", lambda e, k=k: e.matmul(out=pm.t[:, 0:N], lhsT=ones, rhs=uc32_v[:, k, :], start=(k == 0), stop=(k == KC - 1)), reads=[uc32.b[k]] + CST, writes=pm.all, sig=(k == KC - 1))
                for k in range(KC):
                    s_ = tmpb.next()
                    S.op("act", lambda e, s_=s_, k=k: e.activation(out=s_.t[:], in_=uc32_v[:, k, :], func=AF.Square), reads=[uc32.b[k]], writes=s_.all)
                    S.op("pe", lambda e, s_=s_, k=k: e.matmul(out=pq.t[:, 0:N], lhsT=onesb, rhs=s_.t[:], start=(k == 0), stop=(k == KC - 1)), reads=s_.all + CST, writes=pq.all, sig=True)
                mean = meanT
                var = rstd
                S.op("dve", lambda e: e.tensor_scalar(out=mean.t[:], in0=pm.t[:, 0:N], scalar1=1.0 / D, scalar2=None, op0=ALU.mult), reads=pm.all, writes=mean.all)
                S.op("dve", lambda e: e.tensor_tensor(out=var.t[:], in0=mean.t[:], in1=mean.t[:], op=ALU.mult), reads=mean.all, writes=var.all)
                S.op("dve", lambda e: e.scalar_tensor_tensor(out=var.t[:], in0=pq.t[:, 0:N], scalar=1.0 / D, in1=var.t[:], op0=ALU.mult, op1=ALU.subtract), reads=pq.all + var.all, writes=var.all)
                S.op("act", lambda e: e.activation(out=var.t[:], in_=var.t[:], func=AF.Sqrt, bias=epsc.t[:, 0:1], scale=1.0), reads=var.all + epsc.all, writes=var.all)
                S.op("dve", lambda e: e.reciprocal(out=var.t[:], in_=var.t[:]), reads=var.all, writes=var.all)
                for k in range(KC):
                    tf = tmpf.next()
                    S.op("dve", lambda e, k=k, tf=tf: e.tensor_tensor(out=tf.t[:], in0=uc32_v[:, k, :], in1=mean.t[:], op=ALU.subtract), reads=[uc32.b[k]] + mean.all, writes=tf.all)
                    S.op("dve", lambda e, k=k, tf=tf: e.tensor_tensor(out=tf.t[:], in0=tf.t[:], in1=var.t[:], op=ALU.mult), reads=tf.all + var.all, writes=tf.all)
                    S.op("act", lambda e, k=k, tf=tf: e.activation(out=ucg.t[:, k, :], in_=tf.t[:], func=AF.Silu, bias=vc(l, k, V_CLNB), scale=vc(l, k, V_CLNW)), reads=tf.all + vcol.all, writes=[ucg.b[k]])
                proj(w_in[l], 4 * D, KC, hT, N, ev_zc)
                for k in range(KC):
                    S.op("dve", lambda e, k=k: e.tensor_tensor(out=ucg.t[:, k, :], in0=ucg.t[:, k, :], in1=szc.t[:, k, :], op=ALU.mult), reads=[ucg.b[k], szc.b[k]], writes=[ucg.b[k]])

                proj(w_in[l], 6 * D, KC, hT, N, ev_gc)

                def ev_yc(fc, pa):
                    tf = tmpf.next()
                    S.op("dve", lambda e: e.tensor_tensor(out=tf.t[:], in0=pa.t[:, 0:N], in1=sgc.t[:, fc, :], op=ALU.mult), reads=pa.all + [sgc.b[fc]], writes=tf.all)
                    S.op("dve", lambda e: e.tensor_tensor(out=mrg.t[:, fc, :], in0=tf.t[:], in1=ymg.t[:, fc, :], op=ALU.add), reads=tf.all + [ymg.b[fc]], writes=[mrg.b[fc]])
                proj(w_br_c[l], 0, KC, ucg, N, ev_yc)

                def ev_out(fc, pa):
                    if not sample:
                        S.op("dve", lambda e: e.scalar_tensor_tensor(out=xT.t[:, fc, :], in0=pa.t[:, 0:N], scalar=modv(l, 2, fc), in1=xT.t[:, fc, :], op0=ALU.mult, op1=ALU.add), reads=pa.all + [xT.b[fc]] + MOD, writes=[xT.b[fc]])
                    else:
                        tf = tmpf.next()
                        S.op("dve", lambda e: e.tensor_tensor(out=tf.t[:], in0=pa.t[:, 0:N], in1=modv(l, 2, fc), op=ALU.mult), reads=pa.all + MOD, writes=tf.all)
                        S.op("dve", lambda e: e.tensor_tensor(out=xT.t[:, fc, :], in0=tf.t[:], in1=xT.t[:, fc, :], op=ALU.add), reads=tf.all + [xT.b[fc]], writes=[xT.b[fc]])
                    rms_chunk(fc, defer=True)
                proj(w_out[l], 0, KC, mrg, N, ev_out)
                rms_flush()

            def mlstm_prompt(l, ti, gi, lf, filler=None, filler2=None):
                aT, FT, cT, tT = rowA, rowF, rowA, rowF
                if ti == 0:
                    S.op("dve", lambda e: e.memset(C32.t[:], 0.0), writes=C32.all)
                else:
                    S.dma("sp", lambda e: e.dma_start(out=C32.t[:], in_=oCp[l].rearrange("h (k p) e -> p h k e", p=P)), reads=[dC[l]], writes=C32.all)
                Amax, FL, mall, mprv, Mx, alp = sm8
                def fill(n):
                    if filler is not None:
                        for _ in range(n):
                            next(filler, None)

                def fill_g(n):
                    if filler2 is not None:
                        for _ in range(n):
                            next(filler2, None)
                fill_g(4)
                pa = psA.next()
                pf = psA.next()
                for blk in range(NB):
                    S.op("pe", lambda e, blk=blk: e.matmul(out=pa.t[0:H, blk * LC:(blk + 1) * LC], lhsT=gi.t[:, blk, :], rhs=ident[0:LC, 0:LC], start=True, stop=False), reads=gi.all + CST, writes=pa.all, sig=False)
                    S.op("pe", lambda e, blk=blk: e.matmul(out=pa.t[0:H, blk * LC:(blk + 1) * LC], lhsT=lf.t[:, blk, :], rhs=ntri[0:LC, 0:LC], start=False, stop=True), reads=lf.all + CST, writes=pa.all, sig=False)
                    S.op("pe", lambda e, blk=blk: e.matmul(out=pf.t[0:H, blk * LC:(blk + 1) * LC], lhsT=lf.t[:, blk, :], rhs=triu[0:LC, 0:LC], start=True, stop=True), reads=lf.all + CST, writes=pf.all, sig=True)
                S.op("act", lambda e: e.activation(out=aT.t[:], in_=pa.t[0:H, 0:N], func=AF.Copy), reads=pa.all, writes=aT.all)
                S.op("act", lambda e: e.activation(out=FT.t[:], in_=pf.t[0:H, 0:N], func=AF.Copy), reads=pf.all, writes=FT.all)
                fill_g(4)
                S.op("dve", lambda e: e.tensor_reduce(out=Amax.t[:], in_=aT.t[:].rearrange("p (c s) -> p c s", s=LC), axis=AX.X, op=ALU.max), reads=aT.all, writes=Amax.all)
                S.op("dve", lambda e: e.tensor_copy(out=FL.t[:], in_=FT.t[:].rearrange("p (c s) -> p c s", s=LC)[:, :, LC - 1]), reads=FT.all, writes=FL.all)
                S.op("dve", lambda e: e.tensor_tensor_scan(out=mall.t[:], data0=Amax.t[:], data1=FL.t[:], initial=mst.t[:, l:l + 1], op0=ALU.max, op1=ALU.add), reads=Amax.all + FL.all + [mst.b[l]], writes=mall.all)
                S.op("dve", lambda e: e.tensor_copy(out=mprv.t[:, 0:1], in_=mst.t[:, l:l + 1]), reads=[mst.b[l]], writes=mprv.all)
                S.op("dve", lambda e: e.tensor_copy(out=mprv.t[:, 1:NB], in_=mall.t[:, 0:NB - 1]), reads=mall.all + mprv.all, writes=mprv.all)
                S.op("dve", lambda e: e.tensor_copy(out=mst.t[:, l:l + 1], in_=mall.t[:, NB - 1:NB]), reads=mall.all + mprv.all, writes=[mst.b[l]])
                S.op("dve", lambda e: e.tensor_tensor(out=Mx.t[:], in0=mprv.t[:], in1=Amax.t[:], op=ALU.max), reads=mprv.all + Amax.all, writes=Mx.all)
                S.op("dve", lambda e: e.tensor_sub(out=alp.t[:], in0=mprv.t[:], in1=Mx.t[:]), reads=mprv.all + Mx.all, writes=alp.all)
                S.op("act", lambda e: e.activation(out=alp.t[:], in_=alp.t[:], func=AF.Exp), reads=alp.all, writes=alp.all)
                Mb = bcast(Mx.t[:].unsqueeze(2), [H, NB, LC])
                S.op("dve", lambda e: e.tensor_tensor(out=cT.t[:].rearrange("p (c s) -> p c s", s=LC), in0=aT.t[:].rearrange("p (c s) -> p c s", s=LC), in1=Mb, op=ALU.subtract), reads=aT.all + Mx.all, writes=cT.all)
                S.op("act", lambda e: e.activation(out=cT.t[:], in_=cT.t[:], func=AF.Exp), reads=cT.all, writes=cT.all)
                S.op("dve", lambda e: e.tensor_tensor(out=tT.t[:].rearrange("p (c s) -> p c s", s=LC), in0=FT.t[:].rearrange("p (c s) -> p c s", s=LC), in1=Mb, op=ALU.add), reads=FT.all + Mx.all, writes=tT.all)
                S.op("act", lambda e: e.activation(out=tT.t[:], in_=tT.t[:], func=AF.Exp, scale=-1.0), reads=tT.all, writes=tT.all)
                fill_g(4)
                for blk in range(NB):
                    S.op("pe", lambda e, blk=blk: e.matmul(out=psS.t[0:LC, blk * 8:blk * 8 + 4], lhsT=cT.t[:, blk * LC:(blk + 1) * LC], rhs=ident[0:H, 0:H], start=True, stop=True), reads=cT.all + CST, writes=psS.all, sig=False)
                    S.op("pe", lambda e, blk=blk: e.matmul(out=psS.t[0:LC, blk * 8 + 4:blk * 8 + 8], lhsT=tT.t[:, blk * LC:(blk + 1) * LC], rhs=ident[0:H, 0:H], start=True, stop=True), reads=tT.all + CST, writes=psS.all, sig=(blk == NB - 1))
                S.op("dve", lambda e: e.tensor_copy(out=ctk.t[:], in_=psS.t[0:LC, 0:NB * 8].rearrange("p (b g) -> p b g", g=8)), reads=psS.all, writes=ctk.all)
                S.op("dve", lambda e: e.tensor_copy(out=cbk.t[:], in_=ctk.t[:, :, 0:4]), reads=ctk.all, writes=cbk.all)
                fill_g(4)
                S.op("dve", lambda e: e.tensor_tensor(out=adg.t[:], in0=bcast(alp.t[:].unsqueeze(2), [H, NB, H]), in1=bcast(ident[0:H, 0:H].unsqueeze(1), [H, NB, H]), op=ALU.mult), reads=alp.all + CST, writes=adg.all)
                S.op("pe", lambda e: e.matmul(out=psS.t[:, 0:NB * H], lhsT=ones[0:H, :], rhs=adg.t[:].rearrange("p c h -> p (c h)"), start=True, stop=True), reads=adg.all + CST, writes=psS.all, sig=True)
                S.op("dve", lambda e: e.tensor_copy(out=abc.t[:], in_=psS.t[:, 0:NB * H].rearrange("p (c h) -> p c h", h=H)), reads=psS.all, writes=abc.all)
                for blk in range(NB):
                    S.op("dve", lambda e, blk=blk: e.tensor_tensor(out=vtok_v[:, blk, :].rearrange("p (h e) -> p h e", h=H), in0=vtok_v[:, blk, :].rearrange("p (h e) -> p h e", h=H), in1=bcast(ctk.t[:, blk, 0:4].unsqueeze(2), [LC, H, DH]), op=ALU.mult),
                         reads=[vtok.b[blk]] + ctk.all, writes=[vtok.b[blk]])
                for h in range(H):
                    S.op("act", lambda e, h=h: e.activation(out=Cb.t[:, h].rearrange("p k e -> p (k e)"), in_=C32.t[:, h].rearrange("p k e -> p (k e)"), func=AF.Copy, scale=abc.t[:, 0, h:h + 1]), reads=[C32.b[h]] + abc.all, writes=[Cb.b[h]])
                S.op("dve", lambda e: e.tensor_tensor(out=nb_.t[:], in0=n32.t[:, l], in1=bcast(abc.t[:, 0, :].unsqueeze(2), [P, H, 2]), op=ALU.mult), reads=[n32.b[l]] + abc.all, writes=nb_.all)

                fill_g(16)
                sms = []
                pend = []
                for j in range(NB):
                    t0 = j * LC
                    pa = psA.next()
                    for h in range(H):
                        for k in range(2):
                            S.op("pe", lambda e, h=h, k=k, pa=pa: e.matmul(out=pa.t[0:LC, h * LC:(h + 1) * LC], lhsT=kT.t[:, h * 2 + k, t0:t0 + LC], rhs=qT.t[:, h * 2 + k, t0:t0 + LC], start=(k == 0), stop=(k == 1)),
                                 reads=[kT.b[h * 2 + k], qT.b[h * 2 + k]], writes=pa.all, sig=(h == H - 1 and k == 1))
                    sm_ = Sm.next()
                    S.op("dve", lambda e, pa=pa, sm_=sm_: e.tensor_tensor(out=sm_.t[:], in0=pa.t[0:LC, 0:H * LC].rearrange("p (h l) -> p h l", h=H), in1=bcast(maskb[0:LC, 0:LC].unsqueeze(1), [LC, H, LC]), op=ALU.mult), reads=pa.all + CST, writes=sm_.all)
                    sms.append(sm_)
                def evac_chunk(j, po):
                    dab, dmx, d2, rs, nbias = smls[j % 2]
                    S.op("dve", lambda e: e.tensor_tensor(out=dmx.t[:], in0=dab.t[:], in1=ctk.t[:, j, 4:8], op=ALU.max), reads=dab.all + ctk.all, writes=dmx.all)
                    S.op("dve", lambda e: e.tensor_tensor(out=d2.t[:], in0=dmx.t[:], in1=dmx.t[:], op=ALU.mult), reads=dmx.all, writes=d2.all)
                    for h in range(H):
                        S.op("dve", lambda e, h=h: e.bn_stats(out=st6.t[:, h, :], in_=po.t[0:LC, h * DH:(h + 1) * DH]), reads=po.all, writes=st6.all)
                    for h in range(H):
                        S.op("dve", lambda e, h=h: e.bn_aggr(out=mv.t[:, h, :], in_=st6.t[:, h, :]), reads=st6.all, writes=mv.all)
                    S.op("dve", lambda e: e.scalar_tensor_tensor(out=rs.t[:], in0=d2.t[:], scalar=EPS, in1=mv.t[:, :, 1], op0=ALU.mult, op1=ALU.add), reads=d2.all + mv.all, writes=rs.all)
                    S.op("act", lambda e: e.activation(out=rs.t[:], in_=rs.t[:], func=AF.Sqrt), reads=rs.all, writes=rs.all)
                    S.op("dve", lambda e: e.reciprocal(out=rs.t[:], in_=rs.t[:]), reads=rs.all, writes=rs.all)
                    S.op("dve", lambda e: e.scalar_tensor_tensor(out=nbias.t[:], in0=mv.t[:, :, 0], scalar=-1.0, in1=rs.t[:], op0=ALU.mult, op1=ALU.mult), reads=mv.all + rs.all, writes=nbias.all)
                    for h in range(H):
                        S.op("act", lambda e, h=h: e.activation(out=hftok.t[:, j, h * DH:(h + 1) * DH], in_=po.t[0:LC, h * DH:(h + 1) * DH], func=AF.Identity, bias=nbias.t[:, h:h + 1], scale=rs.t[:, h:h + 1]),
                             reads=po.all + rs.all + nbias.all, writes=[hftok.b[j]])
                for j in range(NB):
                    t0 = j * LC
                    po = psOr.next()
                    sm_ = sms[j]
                    for h in range(H):
                        pc = psA.next()
                        for k in range(2):
                            S.op("pe", lambda e, h=h, k=k, pc=pc: e.matmul(out=pc.t[:, k * DH:(k + 1) * DH], lhsT=ktok.t[:, j, h * DH + k * P:h * DH + (k + 1) * P], rhs=vtok_v[:, j, h * DH:(h + 1) * DH], start=True, stop=True), reads=[ktok.b[j], vtok.b[j]], writes=pc.all, sig=(k == 1))
                        S.op("dve", lambda e, h=h, pc=pc: e.scalar_tensor_tensor(out=C32.t[:, h].rearrange("p k e -> p (k e)"), in0=C32.t[:, h].rearrange("p k e -> p (k e)"), scalar=abc.t[:, j, h:h + 1], in1=pc.t[:, 0:2 * DH], op0=ALU.mult, op1=ALU.add),
                             reads=[C32.b[h]] + pc.all + abc.all, writes=[C32.b[h]])
                    pend_now = list(pend)
                    del pend[:]
                    for h in range(H):
                        S.op("pe", lambda e, h=h, sm_=sm_: e.matmul(out=po.t[0:LC, h * DH:(h + 1) * DH], lhsT=sm_.t[:, h, :], rhs=vtok_v[:, j, h * DH:(h + 1) * DH], start=True, stop=False), reads=sm_.all + [vtok.b[j]], writes=po.all, sig=False)
                        for k in range(2):
                            S.op("pe", lambda e, h=h, k=k: e.matmul(out=po.t[0:LC, h * DH:(h + 1) * DH], lhsT=qT.t[:, h * 2 + k, t0:t0 + LC], rhs=Cb.t[:, h, k, :], start=False, stop=(k == 1)), reads=[qT.b[h * 2 + k], Cb.b[h]], writes=po.all, sig=False)
                    for h in range(H):
                        S.op("pe", lambda e, h=h, sm_=sm_: e.matmul(out=psS.t[0:LC, h:h + 1], lhsT=sm_.t[:, h, :], rhs=cbk.t[:, j, h:h + 1], start=True, stop=False), reads=sm_.all + cbk.all, writes=psS.all, sig=False)
                        for k in range(2):
                            S.op("pe", lambda e, h=h, k=k: e.matmul(out=psS.t[0:LC, h:h + 1], lhsT=qT.t[:, h * 2 + k, t0:t0 + LC], rhs=nb_.t[:, h, k:k + 1], start=False, stop=(k == 1)), reads=[qT.b[h * 2 + k]] + nb_.all, writes=psS.all, sig=(h == H - 1 and k == 1))
                    dab, dmx, d2, rs, nbias = smls[j % 2]
                    S.op("act", lambda e: e.activation(out=dab.t[:], in_=psS.t[0:LC, 0:H], func=AF.Abs), reads=psS.all, writes=dab.all)
                    for pj, ppo in pend_now:
                        evac_chunk(pj, ppo)
                    if j + 1 < NB:
                        for h in range(H):
                            S.op("act", lambda e, h=h: e.activation(out=Cb.t[:, h].rearrange("p k e -> p (k e)"), in_=C32.t[:, h].rearrange("p k e -> p (k e)"), func=AF.Copy, scale=abc.t[:, j + 1, h:h + 1]), reads=[C32.b[h]] + abc.all, writes=[Cb.b[h]])
                    for h in range(H):
                        for k in range(2):
                            S.op("pe", lambda e, h=h, k=k: e.matmul(out=psS.t[:, 8 + h * 2 + k:9 + h * 2 + k], lhsT=ktok.t[:, j, h * DH + k * P:h * DH + (k + 1) * P], rhs=cbk.t[:, j, h:h + 1], start=True, stop=True), reads=[ktok.b[j]] + cbk.all, writes=psS.all, sig=(h == H - 1 and k == 1))
                    S.op("dve", lambda e: e.tensor_tensor(out=n32.t[:, l], in0=n32.t[:, l], in1=bcast(abc.t[:, j, :].unsqueeze(2), [P, H, 2]), op=ALU.mult), reads=[n32.b[l]] + abc.all, writes=[n32.b[l]])
                    S.op("dve", lambda e: e.tensor_tensor(out=n32.t[:, l], in0=n32.t[:, l], in1=psS.t[:, 8:16].rearrange("p (h k) -> p h k", k=2), op=ALU.add), reads=[n32.b[l]] + psS.all, writes=[n32.b[l]])
                    if j + 1 < NB:
                        S.op("dve", lambda e: e.tensor_tensor(out=nb_.t[:], in0=n32.t[:, l], in1=bcast(abc.t[:, j + 1, :].unsqueeze(2), [P, H, 2]), op=ALU.mult), reads=[n32.b[l]] + abc.all, writes=nb_.all)
                    pend.append((j, po))
                    fill(2)

                while pend:
                    evac_chunk(*pend.pop(0))
                S.dma("sp", lambda e: e.dma_start(out=oCp[l].rearrange("h (k p) e -> p h k e", p=P), in_=C32.t[:]), reads=C32.all, writes=[dC[l]])
                if ti == NT - 1:
                    S.dma("sp", lambda e: e.dma_start(out=onp[l].rearrange("(h k p) -> p h k", h=H, p=P), in_=n32.t[:, l], allow_slow_non_contiguous=True), reads=[n32.b[l]])
                    S.dma("sp", lambda e: e.dma_start(out=omp[l].rearrange("(h o) -> h o", o=1), in_=mst.t[:, l:l + 1], allow_slow_non_contiguous=True), reads=[mst.b[l]])

            def mlstm_sample(l, gi, lf):
                R = NS
                a_, mt, dec, wsc, thr, qk, qn, den, dmx, tmp = s16
                S.dma("sp", lambda e: e.dma_start(out=mprev.t[:], in_=sm[l, :, :]), writes=mprev.all)
                S.dma("sp", lambda e: e.dma_start(out=nst.t[:].rearrange("p h e -> p (h e)"), in_=sn[l, :, :]), writes=nst.all)
                g_i = gi.t[0:R, 0, :]
                g_f = lf.t[0:R, 0, :]
                S.op("dve", lambda e: e.tensor_add(out=a_.t[:], in0=g_f, in1=mprev.t[:]), reads=lf.all + mprev.all, writes=a_.all)
                S.op("dve", lambda e: e.tensor_max(out=mt.t[:], in0=a_.t[:], in1=g_i), reads=a_.all + gi.all, writes=mt.all)
                S.op("dve", lambda e: e.tensor_sub(out=dec.t[:], in0=a_.t[:], in1=mt.t[:]), reads=a_.all + mt.all, writes=dec.all)
                S.op("act", lambda e: e.activation(out=dec.t[:], in_=dec.t[:], func=AF.Exp), reads=dec.all, writes=dec.all)
                S.op("dve", lambda e: e.tensor_sub(out=wsc.t[:], in0=g_i, in1=mt.t[:]), reads=gi.all + mt.all, writes=wsc.all)
                S.op("act", lambda e: e.activation(out=wsc.t[:], in_=wsc.t[:], func=AF.Exp), reads=wsc.all, writes=wsc.all)
                S.op("act", lambda e: e.activation(out=thr.t[:], in_=mt.t[:], func=AF.Exp, scale=-1.0), reads=mt.all, writes=thr.all)
                S.dma("sp", lambda e: e.dma_start(out=oms[l, :, :], in_=mt.t[:]), reads=mt.all)
                ktf = tmpS_k
                S.op("act", lambda e: e.activation(out=ktf.t[:], in_=ktok.t[0:R, 0, :], func=AF.Copy), reads=ktok.all, writes=ktf.all)
                S.op("dve", lambda e: e.tensor_tensor(out=prod.t[:], in0=qtok.t[:], in1=ktf.t[:], op=ALU.mult), reads=qtok.all + ktf.all, writes=prod.all)
                S.op("dve", lambda e: e.tensor_reduce(out=qk.t[:], in_=prod.t[:].rearrange("p (h e) -> p h e", h=H), axis=AX.X, op=ALU.add), reads=prod.all, writes=qk.all)
                S.op("dve", lambda e: e.tensor_tensor(out=prod.t[:], in0=qtok.t[:], in1=nst.t[:].rearrange("p h e -> p (h e)"), op=ALU.mult), reads=qtok.all + nst.all, writes=prod.all)
                S.op("dve", lambda e: e.tensor_reduce(out=qn.t[:], in_=prod.t[:].rearrange("p (h e) -> p h e", h=H), axis=AX.X, op=ALU.add), reads=prod.all, writes=qn.all)
                S.op("dve", lambda e: e.tensor_mul(out=qk.t[:], in0=qk.t[:], in1=wsc.t[:]), reads=qk.all + wsc.all, writes=qk.all)
                S.op("dve", lambda e: e.tensor_mul(out=den.t[:], in0=dec.t[:], in1=qn.t[:]), reads=dec.all + qn.all, writes=den.all)
                S.op("dve", lambda e: e.tensor_add(out=den.t[:], in0=den.t[:], in1=qk.t[:]), reads=den.all + qk.all, writes=den.all)
                S.op("act", lambda e: e.activation(out=den.t[:], in_=den.t[:], func=AF.Abs), reads=den.all, writes=den.all)
                S.op("dve", lambda e: e.tensor_max(out=dmx.t[:], in0=den.t[:], in1=thr.t[:]), reads=den.all + thr.all, writes=dmx.all)
                S.op("dve", lambda e: e.tensor_tensor(out=nnew.t[:], in0=nst.t[:], in1=bcast(dec.t[:].unsqueeze(2), [R, H, DH]), op=ALU.mult), reads=nst.all + dec.all, writes=nnew.all)
                S.op("dve", lambda e: e.tensor_tensor(out=ktf.t[:].rearrange("p (h e) -> p h e", h=H), in0=ktf.t[:].rearrange("p (h e) -> p h e", h=H), in1=bcast(wsc.t[:].unsqueeze(2), [R, H, DH]), op=ALU.mult), reads=ktf.all + wsc.all, writes=ktf.all)
                S.op("dve", lambda e: e.tensor_add(out=nnew.t[:].rearrange("p h e -> p (h e)"), in0=nnew.t[:].rearrange("p h e -> p (h e)"), in1=ktf.t[:]), reads=nnew.all + ktf.all, writes=nnew.all)
                S.dma("sp", lambda e: e.dma_start(out=ons[l, :, :], in_=nnew.t[:].rearrange("p h e -> p (h e)")), reads=nnew.all)
                for half in range(2):
                    pa = psA.next()
                    for kk in range(4):
                        k = half * 4 + kk
                        S.op("pe", lambda e, pa=pa, k=k, kk=kk: e.transpose(out=pa.t[:, kk * R:(kk + 1) * R], in_=qtok.t[:, k * P:(k + 1) * P], identity=ident[0:R, 0:R]), reads=qtok.all + CST, writes=pa.all, sig=(kk == 3))
                    S.op("dve", lambda e, pa=pa, half=half: e.tensor_copy(out=qTf.t[:, half * 4:(half + 1) * 4, :], in_=pa.t[:, 0:4 * R].rearrange("p (k b) -> p k b", b=R)), reads=pa.all, writes=qTf.b[half * 4:(half + 1) * 4])
                for k in range(KC):
                    S.op("dve", lambda e, k=k: e.tensor_tensor(out=qm.t[:, k], in0=bcast(qTf.t[:, k, :].unsqueeze(1), [P, R, R]), in1=cst.t[:, 4 * P:4 * P + R * R].rearrange("p (a b) -> p a b", b=R), op=ALU.mult), reads=[qTf.b[k]] + CST, writes=[qm.b[k]])
                S.op("dve", lambda e: e.tensor_tensor(out=dexp.t[:], in0=bcast(dec.t[:].unsqueeze(1), [R, R, H]), in1=bcast(ident[0:R, 0:R].unsqueeze(2), [R, R, H]), op=ALU.mult), reads=dec.all + CST, writes=dexp.all)
                S.op("pe", lambda e: e.matmul(out=psS.t[:, 0:R * H], lhsT=ones[0:R, :], rhs=dexp.t[:].rearrange("p b h -> p (b h)"), start=True, stop=True), reads=dexp.all + CST, writes=psS.all, sig=True)
                S.op("dve", lambda e: e.tensor_copy(out=dbc.t[:], in_=psS.t[:, 0:R * H].rearrange("p (b h) -> p b h", h=H)), reads=psS.all, writes=dbc.all)
                cins = []
                psO = psOr.next()
                cis = {}

                def issue_in(bb):
                    ci_ = Cin.next()
                    S.dma("sp", lambda e: e.dma_start(out=ci_.t[:], in_=sC[l, bb].rearrange("h (k p) e -> p h k e", p=P)), writes=ci_.all)
                    cis[bb] = ci_
                issue_in(0)
                issue_in(1)
                for b in range(R):
                    if b + 2 < R:
                        issue_in(b + 2)
                    ci = cis[b]
                    if b % 4 == 0:
                        S.op("dve", lambda e, b=b: e.tensor_tensor(out=kexp.t[:], in0=bcast(ktf.t[:].unsqueeze(1), [R, 4, D]), in1=bcast(ident[0:R, b:b + 4].unsqueeze(2), [R, 4, D]), op=ALU.mult), reads=ktf.all + CST, writes=kexp.all)
                    cb16 = Cbf.next()
                    S.op("act", lambda e, ci=ci, cb16=cb16: e.activation(out=cb16.t[:].rearrange("p h k e -> p (h k e)"), in_=ci.t[:].rearrange("p h k e -> p (h k e)"), func=AF.Copy), reads=ci.all, writes=cb16.all)
                    for h in range(H):
                        for k in range(2):
                            S.op("pe", lambda e, ci=ci, b=b, h=h, k=k: e.matmul(out=psO.t[0:R, h * DH:(h + 1) * DH], lhsT=qm.t[:, h * 2 + k, b, :], rhs=cb16.t[:, h, k, :], start=(b == 0 and k == 0 and h % 2 == 0), stop=(b == R - 1 and k == 1), skip_group_check=True),
                                 reads=[qm.b[h * 2 + k]] + cb16.all, writes=psO.all, sig=(h == H - 1 and k == 1))
                    co = Cout.next()
                    for h in range(H):
                        pc = psA.next()
                        for k in range(2):
                            S.op("pe", lambda e, pc=pc, b=b, h=h, k=k: e.matmul(out=pc.t[:, k * DH:(k + 1) * DH], lhsT=kexp.t[:, b % 4, h * DH + k * P:h * DH + (k + 1) * P], rhs=vtok_v[0:R, 0, h * DH:(h + 1) * DH], start=True, stop=True),
                                 reads=kexp.all + vtok.all, writes=pc.all, sig=(k == 1))
                        S.op("dve", lambda e, pc=pc, ci=ci, co=co, b=b, h=h: e.scalar_tensor_tensor(out=co.t[:, h].rearrange("p k e -> p (k e)"), in0=ci.t[:, h].rearrange("p k e -> p (k e)"), scalar=dbc.t[:, b, h:h + 1], in1=pc.t[:, 0:2 * DH], op0=ALU.mult, op1=ALU.add),
                             reads=ci.all + pc.all + dbc.all, writes=co.all)
                    S.dma("sp", lambda e, co=co, b=b: e.dma_start(out=oCs[l, b].rearrange("h (k p) e -> p h k e", p=P), in_=co.t[:]), reads=co.all)
                S.op("dve", lambda e: e.tensor_tensor(out=numt.t[:].rearrange("p (h e) -> p h e", h=H), in0=psO.t[0:R, 0:D].rearrange("p (h e) -> p h e", h=H), in1=bcast(dec.t[:].unsqueeze(2), [R, H, DH]), op=ALU.mult), reads=psO.all + dec.all, writes=numt.all)
                S.op("dve", lambda e: e.tensor_tensor(out=prod.t[:].rearrange("p (h e) -> p h e", h=H), in0=vtok_v[0:R, 0, :].rearrange("p (h e) -> p h e", h=H), in1=bcast(qk.t[:].unsqueeze(2), [R, H, DH]), op=ALU.mult), reads=vtok.all + qk.all, writes=prod.all)
                S.op("dve", lambda e: e.tensor_add(out=numt.t[:], in0=numt.t[:], in1=prod.t[:]), reads=numt.all + prod.all, writes=numt.all)
                for h in range(H):
                    S.op("dve", lambda e, h=h: e.bn_stats(out=st6.t[:, h, :], in_=numt.t[:, h * DH:(h + 1) * DH]), reads=numt.all, writes=st6.all)
                for h in range(H):
                    S.op("dve", lambda e, h=h: e.bn_aggr(out=mv.t[:, h, :], in_=st6.t[:, h, :]), reads=st6.all, writes=mv.all)
                S.op("dve", lambda e: e.tensor_mul(out=tmp.t[:], in0=dmx.t[:], in1=dmx.t[:]), reads=dmx.all, writes=tmp.all)
                S.op("dve", lambda e: e.scalar_tensor_tensor(out=tmp.t[:], in0=tmp.t[:], scalar=EPS, in1=mv.t[:, :, 1], op0=ALU.mult, op1=ALU.add), reads=tmp.all + mv.all, writes=tmp.all)
                S.op("act", lambda e: e.activation(out=tmp.t[:], in_=tmp.t[:], func=AF.Sqrt), reads=tmp.all, writes=tmp.all)
                S.op("dve", lambda e: e.reciprocal(out=tmp.t[:], in_=tmp.t[:]), reads=tmp.all, writes=tmp.all)
                S.op("dve", lambda e: e.tensor_tensor(out=numt.t[:].rearrange("p (h e) -> p h e", h=H), in0=numt.t[:].rearrange("p (h e) -> p h e", h=H), in1=bcast(mv.t[:, :, 0:1], [R, H, DH]), op=ALU.subtract), reads=numt.all + mv.all, writes=numt.all)
                S.op("dve", lambda e: e.tensor_tensor(out=hftok.t[0:R, 0, :].rearrange("p (h e) -> p h e", h=H), in0=numt.t[:].rearrange("p (h e) -> p h e", h=H), in1=bcast(tmp.t[:].unsqueeze(2), [R, H, DH]), op=ALU.mult), reads=numt.all + tmp.all, writes=hftok.all)

            sb_wg = A([P, 3 * KC, 8], BF16)
            if sample:
                tmpS_k = A([NS, D], F32)

            for ti in tiles:
                if not sample:
                    for blk in range(N // P):
                        def wr_x(k, n, pa, blk=blk):
                            S.op("act", lambda e: e.activation(out=xT.t[:, k:k + n, blk * P:(blk + 1) * P], in_=pa.t[:, 0:n * P].rearrange("p (k t) -> p k t", t=P), func=AF.Copy), reads=pa.all, writes=xT.b[k:k + n])
                        load_rows(xp[ti * T + blk * P:ti * T + (blk + 1) * P, :], P, wr_x)
                else:
                    def wr_xs(k, n, pa):
                        S.op("act", lambda e: e.activation(out=xT.t[:, k:k + n, :], in_=pa.t[:, 0:n * NS].rearrange("p (k t) -> p k t", t=NS), func=AF.Copy), reads=pa.all, writes=xT.b[k:k + n])
                    load_rows(xs[:, :], NS, wr_xs)
                for k in range(KC):
                    rms_chunk(k)
                for l in range(DEPTH):
                    layer(l, ti)
                rms_finish()
                for k in range(KC):
                    S.op("dve", lambda e, k=k: e.scalar_tensor_tensor(out=uc32_v[:, k, :], in0=xT.t[:, k, :], scalar=vcol.t[:, 0, k, V_FIN:V_FIN + 1], in1=rstd.t[:], op0=ALU.mult, op1=ALU.mult), reads=[xT.b[k]] + rstd.all + vcol.all, writes=[uc32.b[k]])
                if not sample:
                    for blk in range(N // P):
                        store_rows(lambda k, blk=blk: uc32_v[:, k, blk * P:(blk + 1) * P], uc32.all, P, yp[ti * T + blk * P:ti * T + (blk + 1) * P, :], False)
                else:
                    store_rows(lambda k: uc32_v[:, k, :], uc32.all, NS, ys[:, :], False)

        if NS > 0:
            with contextlib.ExitStack() as st1:
                run_group(NS, True, [0], st1)
                barrier()
        with contextlib.ExitStack() as st2:
            run_group(T, False, list(range(NT)), st2)
            S.finish()
            S.emit()
    return nc


_CACHE = {}
DBG = {}


def make_consts():
    c = np.zeros((P, 4 * P + 256), np.float32)
    c[:, 4 * P:] = np.eye(16, dtype=np.float32).reshape(1, 256)
    c[:, 0:P] = np.eye(P, dtype=np.float32)
    tri = np.triu(np.ones((P, P), np.float32))
    c[:, P:2 * P] = tri
    c[:, 2 * P:3 * P] = 1.0
    c[:, 3 * P:4 * P] = -tri
    return c


def run(inputs, SEQ, DEPTH, NS, ncores=8):
    key = (SEQ, DEPTH, NS)
    if key not in _CACHE:
        _CACHE[key] = build(SEQ, DEPTH, NS)
    nc = _CACHE[key]
    f = lambda a: np.ascontiguousarray(np.asarray(a, dtype=np.float32))
    g = {k: f(v) for k, v in inputs.items()}
    vecs = np.zeros((DEPTH, NV, D), np.float32)
    vecs[:, V_NORM] = g["norm_w"]
    vecs[:, V_MCB] = g["mconv_b"]
    vecs[:, V_GN] = g["gn_w"]
    vecs[:, V_SKIP] = g["skip"]
    vecs[:, V_CCB] = g["cconv_b"]
    vecs[:, V_CLNW] = g["cln_w"]
    vecs[:, V_CLNB] = g["cln_b"]
    vecs[:, V_BADA:V_BADA + 3] = g["b_ada"].reshape(DEPTH, 3, D)
    vecs[:, V_MCW:V_MCW + MW] = g["mconv_w"]
    vecs[:, V_CCW:V_CCW + CW] = g["cconv_w"]
    vecs[:, V_FIN] = g["final_norm_w"][None, :]
    consts = make_consts()
    shared = {k: g[k] for k in ("w_ada", "w_in", "w_q", "w_k", "w_v", "w_gate", "b_gate", "w_br_m", "w_br_c", "w_out")}
    shared["vecs"] = vecs
    shared["consts"] = consts
    in_maps = []
    for i in range(ncores):
        sl = slice(i * NS, (i + 1) * NS)
        m = dict(shared)
        m["xp"] = g["x_prompt"][i]
        m["xs"] = g["x_sample"][sl, 0, :]
        m["call"] = np.concatenate([g["c_prompt"][i:i + 1], g["c_sample"][sl]], axis=0)
        m["sC"] = g["state_mlstm_C"][:, sl]
        m["sn"] = g["state_mlstm_n"][:, sl].reshape(DEPTH, NS, H * DH)
        m["sm"] = g["state_mlstm_m"][:, sl]
        m["smc"] = g["state_mlstm_conv"][:, sl].reshape(DEPTH, NS * (MW - 1), D)
        m["scv"] = g["state_conv"][:, sl].reshape(DEPTH, NS * (CW - 1), D)
        in_maps.append({k: np.ascontiguousarray(v) for k, v in m.items()})
    res = run_bass_kernel_spmd(nc, in_maps, core_ids=list(range(ncores)))
    R = res.results
    cat = lambda name, ax: np.concatenate([np.asarray(r[name]) for r in R], axis=ax)
    stk = lambda name: np.stack([np.asarray(r[name]) for r in R], axis=1)
    y_prompt = np.stack([np.asarray(r["yp"]) for r in R], axis=0)
    y_sample = cat("ys", 0).reshape(ncores * NS, 1, D)
    Cp = stk("oCp")
    np_ = stk("onp").reshape(DEPTH, ncores, H, DH)
    mp = stk("omp")
    mcp = stk("omcp")
    cvp = stk("ocvp")
    Cs = cat("oCs", 1)
    ns = cat("ons", 1).reshape(DEPTH, ncores * NS, H, DH)
    ms = cat("oms", 1)
    mcs = cat("omcs", 1).reshape(DEPTH, ncores * NS, MW - 1, D)
    cvs = cat("ocvs", 1).reshape(DEPTH, ncores * NS, CW - 1, D)
    outs = (y_prompt, y_sample, Cp, np_, mp, mcp, cvp, Cs, ns, ms, mcs, cvs)
    return tuple(np.ascontiguousarray(o, dtype=np.float32) for o in outs)


def kernel(**inputs):
    return run(inputs, SEQ=2048, DEPTH=4, NS=16)
```

```python
import contextlib
import numpy as np
import concourse.bass as bass
import concourse.mybir as mybir
from concourse.bass_utils import run_bass_kernel_spmd

F32 = mybir.dt.float32
BF16 = mybir.dt.bfloat16
AF = mybir.ActivationFunctionType
ALU = mybir.AluOpType
AX = mybir.AxisListType

P = 128
D = 1024
KC = 8
H = 4
DH = 256
T = 512
LC = 128
MW = 4
CW = 31
EPS = 1e-6
NV = 46
V_NORM, V_MCB, V_GN, V_SKIP, V_CCB, V_CLNW, V_CLNB, V_BADA, V_MCW, V_CCW, V_FIN = 0, 1, 2, 3, 4, 5, 6, 7, 10, 14, 45


class Buf:
    __slots__ = ("w", "r")

    def __init__(self):
        self.w = None
        self.r = {}


class TB:
    def __init__(self, t, nb=1):
        self.t = t
        self.b = [Buf() for _ in range(nb)]

    @property
    def all(self):
        return self.b


class Rec:
    def __getattr__(self, name):
        def f(*a, **kw):
            self.call = (name, a, kw)
            return self
        return f


def _bind(fn):
    r = Rec()
    fn(r)
    name, a, kw = r.call
    return lambda eng: getattr(eng, name)(*a, **kw)


class Sched:
    def __init__(self, nc, stack):
        self.nc = nc
        self.engs = ["pe", "act", "dve", "pool", "sp"]
        self.prog = {e: [] for e in self.engs}
        self.semh = {}
        self.cnt = {}
        for e in ["pe", "act", "dve", "pool"]:
            self.semh[e] = stack.enter_context(nc.semaphore("sem_" + e))
            self.cnt[e] = 0
        self.KD = 8
        self.dq = {}
        for q in ["sp", "pool", "act"]:
            lst = []
            for i in range(self.KD):
                key = f"d_{q}{i}"
                self.semh[key] = stack.enter_context(nc.semaphore(key))
                self.cnt[key] = 0
                lst.append(key)
            self.dq[q] = [lst, 0]
        self.known = {e: {} for e in self.engs}
        self.pe_pending = False

    def _wait(self, e, key, val):
        if self.known[e].get(key, 0) >= val:
            return
        self.known[e][key] = val
        h = self.semh[key]
        self.prog[e].append(lambda eng, h=h, val=val: eng.wait_ge(h, val))

    def _deps(self, e, reads, writes):
        deps = {}
        for b in reads:
            if b.w:
                deps[b.w[0]] = max(deps.get(b.w[0], 0), b.w[1])
        for b in writes:
            if b.w:
                deps[b.w[0]] = max(deps.get(b.w[0], 0), b.w[1])
            for k, v in b.r.items():
                deps[k] = max(deps.get(k, 0), v)
        for k, v in deps.items():
            if k == "pe" and e == "pe":
                continue
            self._wait(e, k, v)

    def _mark(self, tok, reads, writes):
        for b in writes:
            b.w = tok
            b.r = {}
        for b in reads:
            b.r[tok[0]] = max(b.r.get(tok[0], 0), tok[1])

    def op(self, e, fn, reads=(), writes=(), sig=True):
        fn = _bind(fn)
        self._deps(e, reads, writes)
        if e == "pe" and not sig:
            tick = self.cnt[e] + 1
            self.pe_pending = True
            self.prog[e].append(lambda eng, fn=fn: fn(eng))
        else:
            self.cnt[e] += 1
            tick = self.cnt[e]
            h = self.semh[e]
            if e == "pe":
                self.pe_pending = False
            self.prog[e].append(lambda eng, fn=fn, h=h: fn(eng).then_inc(h, 1))
        self._mark((e, tick), reads, writes)

    def dma(self, q, fn, reads=(), writes=()):
        fn = _bind(fn)
        lst, i = self.dq[q]
        key = lst[i % self.KD]
        self.dq[q][1] = i + 1
        if self.cnt[key] > 0:
            self._wait(q, key, self.cnt[key])
        self._deps(q, reads, writes)
        self.cnt[key] += 16
        val = self.cnt[key]
        h = self.semh[key]
        self.prog[q].append(lambda eng, fn=fn, h=h: fn(eng).then_inc(h, 16))
        self._mark((key, val), reads, writes)

    def finish(self):
        assert not self.pe_pending
        for key, c in self.cnt.items():
            if c > 0:
                self._wait("sp", key, c)

    def emit(self):
        nc = self.nc
        prog = self.prog
        with nc.Block() as block:

            @block.tensor
            def _(e):
                for f in prog["pe"]:
                    f(e)

            @block.scalar
            def _(e):
                for f in prog["act"]:
                    f(e)

            @block.vector
            def _(e):
                for f in prog["dve"]:
                    f(e)

            @block.gpsimd
            def _(e):
                for f in prog["pool"]:
                    f(e)

            @block.sync
            def _(e):
                for f in prog["sp"]:
                    f(e)


class Ring:
    def __init__(self, items):
        self.items = items
        self.i = 0

    def next(self):
        it = self.items[self.i % len(self.items)]
        self.i += 1
        return it


def bcast(ap, shape):
    return ap.broadcast_to(list(shape))


def build(SEQ, DEPTH, NS):
    nc = bass.Bass("TRN2", target_bir_lowering=False)
    NT = SEQ // T
    NTOK = 1 + NS

    def din(name, shape):
        return nc.dram_tensor(name, list(shape), F32, kind="ExternalInput").ap()

    def dout(name, shape):
        return nc.dram_tensor(name, list(shape), F32, kind="ExternalOutput").ap()

    xp = din("xp", [SEQ, D])
    xs = din("xs", [NS, D])
    call = din("call", [NTOK, D])
    sC = din("sC", [DEPTH, NS, H, DH, DH])
    sn = din("sn", [DEPTH, NS, H * DH])
    sm = din("sm", [DEPTH, NS, H])
    smc = din("smc", [DEPTH, NS * (MW - 1), D])
    scv = din("scv", [DEPTH, NS * (CW - 1), D])
    w_ada = din("w_ada", [DEPTH, D, 3 * D])
    w_in = din("w_in", [DEPTH, D, 7 * D])
    w_q = din("w_q", [DEPTH, H, DH, DH])
    w_k = din("w_k", [DEPTH, H, DH, DH])
    w_v = din("w_v", [DEPTH, H, DH, DH])
    w_gate = din("w_gate", [DEPTH, 3 * D, 8])
    b_gate = din("b_gate", [DEPTH, 8])
    w_br_m = din("w_br_m", [DEPTH, D, D])
    w_br_c = din("w_br_c", [DEPTH, D, D])
    w_out = din("w_out", [DEPTH, D, D])
    vecs = din("vecs", [DEPTH, NV, D])
    consts = din("consts", [P, 4 * P + 256])

    yp = dout("yp", [SEQ, D])
    ys = dout("ys", [NS, D])
    oCp = dout("oCp", [DEPTH, H, DH, DH])
    onp = dout("onp", [DEPTH, H * DH])
    omp = dout("omp", [DEPTH, H])
    omcp = dout("omcp", [DEPTH, MW - 1, D])
    ocvp = dout("ocvp", [DEPTH, CW - 1, D])
    oCs = dout("oCs", [DEPTH, NS, H, DH, DH])
    ons = dout("ons", [DEPTH, NS, H * DH])
    oms = dout("oms", [DEPTH, NS, H])
    omcs = dout("omcs", [DEPTH, NS * (MW - 1), D])
    ocvs = dout("ocvs", [DEPTH, NS * (CW - 1), D])

    dgscr = nc.dram_tensor("dgscr", [DEPTH, KC, P, CW * P], BF16).ap()
    stack = contextlib.ExitStack()
    with stack:
        S = Sched(nc, stack)
        uid = [0]

        def barrier():
            snap = dict(S.cnt)
            for e in S.engs:
                for key, c in snap.items():
                    if c > 0 and key != e:
                        S._wait(e, key, c)


        pst = contextlib.ExitStack()

        def sb(shape, dt, nb=1, st=stack):
            uid[0] += 1
            t = st.enter_context(nc.sbuf_tensor(f"t{uid[0]}", list(shape), dt))
            return TB(t, nb)

        def ps(shape, dt):
            uid[0] += 1
            t = stack.enter_context(nc.psum_tensor(f"p{uid[0]}", list(shape), dt))
            return TB(t, 1)

        psA = Ring([ps([P, 512], F32) for _ in range(2)])
        _po = [ps([P, 1024], F32) for _ in range(2)]
        for _t in _po:
            _t.b = [Buf(), Buf()]
        psOr = Ring(_po)
        _halves = []
        for _t in _po:
            for _i in range(2):
                _h = TB(_t.t[:, _i * 512:(_i + 1) * 512], 1)
                _h.b = [_t.b[_i]]
                _halves.append(_h)
        psA6 = Ring(list(psA.items) + _halves)
        psT = ps([P, 1024], BF16)
        psS = ps([P, 512], F32)

        cst = sb([P, 4 * P + 256], F32)
        cbf = sb([P, 3 * P], BF16)
        epsc = sb([P, 1], F32)
        vcol = sb([P, DEPTH, KC, NV], F32)
        NWB = 4
        wring = Ring([sb([P, KC, 256], BF16) for _ in range(NWB)])
        modall = sb([P, DEPTH, 24, NTOK], F32)
        Acol = sb([P, DEPTH, KC, NTOK], F32)
        n32 = sb([P, DEPTH, H, 2], F32, nb=DEPTH)
        mst = sb([H, DEPTH], F32, nb=DEPTH)
        hxm = sb([P, DEPTH, KC, MW - 1], BF16, nb=DEPTH)
        hu = sb([P, DEPTH, KC, CW - 1], BF16, nb=DEPTH)
        vstg = sb([NV, D], F32, 1, pst)
        cstg = sb([NTOK, D], F32, 1, pst)
        csl = sb([NTOK, D], F32, 1, pst)
        scT = sb([P, KC, NTOK], BF16, KC, pst)
        S.dma("sp", lambda e: e.dma_start(out=cst.t[:], in_=consts[:, :]), writes=cst.all)
        ident = cst.t[:, 0:P]
        triu = cst.t[:, P:2 * P]
        ones = cst.t[:, 2 * P:3 * P]
        ntri = cst.t[:, 3 * P:4 * P]
        identb = cbf.t[:, 0:P]
        maskb = cbf.t[:, P:2 * P]
        S.op("dve", lambda e: e.tensor_copy(out=cbf.t[:], in_=cst.t[:, 0:3 * P]), reads=cst.all, writes=cbf.all)
        onesb = cbf.t[:, 2 * P:3 * P]
        CST = cst.all + cbf.all
        S.op("dve", lambda e: e.memset(epsc.t[:], EPS), writes=epsc.all)

        for l in range(DEPTH):
            S.dma("sp", lambda e, l=l: e.dma_start(out=vstg.t[:], in_=vecs[l, :, :]), writes=vstg.all)
            for half in range(2):
                pa = psA.next()
                for kk in range(4):
                    k = half * 4 + kk
                    S.op("pe", lambda e, pa=pa, k=k, kk=kk: e.transpose(out=pa.t[:, kk * NV:(kk + 1) * NV], in_=vstg.t[:, k * P:(k + 1) * P], identity=ident[0:NV, 0:NV]),
                         reads=vstg.all + CST, writes=pa.all, sig=(kk == 3))
                S.op("dve", lambda e, pa=pa, l=l, half=half: e.tensor_copy(out=vcol.t[:, l, half * 4:(half + 1) * 4, :], in_=pa.t[:, 0:4 * NV].rearrange("p (k v) -> p k v", v=NV)),
                     reads=pa.all, writes=vcol.all)

        def vc(l, k, v):
            return vcol.t[:, l, k, v:v + 1]


        def wload(src3, kk, cols):
            w = wring.next()
            S.dma("pool", lambda e, w=w: e.dma_start(out=w.t[:, 0:kk, 0:cols], in_=src3), writes=w.all)
            return w

        def wmat(wd, f0, cols):
            return wload(wd.rearrange("(k p) f -> p k f", p=P)[:, :, f0:f0 + cols], KC, cols)

        def proj(wd, f0, nfc, src, N, evac, src_cols=None):
            for _ in proj_gen(wd, f0, nfc, src, N, evac, src_cols, ring=psA6):
                pass

        def proj_gen(wd, f0, nfc, src, N, evac, src_cols=None, ring=None):
            ring = ring or psA
            done = 0
            while done < nfc:
                n = min(2, nfc - done)
                w = wmat(wd, f0 + done * P, n * P)
                for c in range(n):
                    pa = ring.next()
                    for k in range(KC):
                        rhs = src.t[:, k, 0:N] if src_cols is None else src.t[:, k, src_cols[0]:src_cols[1]]
                        S.op("pe", lambda e, pa=pa, w=w, c=c, k=k, rhs=rhs: e.matmul(out=pa.t[:, 0:N], lhsT=w.t[:, k, c * P:(c + 1) * P], rhs=rhs, start=(k == 0), stop=(k == KC - 1)),
                             reads=w.all + [src.b[k]], writes=pa.all, sig=(k == KC - 1))
                    evac(done + c, pa)
                done += n
                yield

        S.dma("sp", lambda e: e.dma_start(out=cstg.t[:], in_=call[:, :]), writes=cstg.all)
        S.op("act", lambda e: e.activation(out=csl.t[:], in_=cstg.t[:], func=AF.Silu), reads=cstg.all, writes=csl.all)
        for half in range(2):
            pa = psA.next()
            for kk in range(4):
                k = half * 4 + kk
                S.op("pe", lambda e, pa=pa, k=k, kk=kk: e.transpose(out=pa.t[:, kk * NTOK:(kk + 1) * NTOK], in_=csl.t[:, k * P:(k + 1) * P], identity=ident[0:NTOK, 0:NTOK]),
                     reads=csl.all + CST, writes=pa.all, sig=(kk == 3))
            S.op("dve", lambda e, pa=pa, half=half: e.tensor_copy(out=scT.t[:, half * 4:(half + 1) * 4, :], in_=pa.t[:, 0:4 * NTOK].rearrange("p (k v) -> p k v", v=NTOK)),
                 reads=pa.all, writes=scT.b[half * 4:(half + 1) * 4])
        for l in range(DEPTH):
            def ev(fc, pa, l=l):
                S.op("act", lambda e: e.activation(out=modall.t[:, l, fc, :], in_=pa.t[:, 0:NTOK], func=AF.Identity, bias=vc(l, fc % KC, V_BADA + fc // KC), scale=1.0),
                     reads=pa.all + vcol.all, writes=modall.all)
            proj(w_ada[l], 0, 24, scT, NTOK, ev)
            S.op("dve", lambda e, l=l: e.scalar_tensor_tensor(out=Acol.t[:, l, :, :], in0=modall.t[:, l, 8:16, :], scalar=1.0, in1=bcast(vcol.t[:, l, :, V_NORM:V_NORM + 1], [P, KC, NTOK]), op0=ALU.add, op1=ALU.mult),
                 reads=modall.all + vcol.all, writes=Acol.all)
        MOD = modall.all + Acol.all + vcol.all
        barrier()
        pst.close()
        dC = [Buf() for _ in range(DEPTH)]
        dgB = [[Buf() for _ in range(KC)] for _ in range(DEPTH)]
        use_scr = NS > 0

        for tb in (n32, mst, hxm, hu):
            S.op("dve", lambda e, tb=tb: e.memset(tb.t[:], 0.0), writes=tb.all)

        def run_group(N, sample, tiles, st):
            def A(shape, dt, nb=1):
                return sb(shape, dt, nb, st)

            xT = A([P, KC, N], F32, KC)
            hT = A([P, KC, N], BF16, KC)
            XU = A([P, KC, CW - 1 + N], BF16, KC)
            xm = TB(XU.t, 1); xm.b = XU.b
            uu = XU
            XO = CW - MW
            szm = A([P, KC, N], BF16, KC)
            sgm = A([P, KC, N], BF16, KC)
            xc = A([P, KC, N], BF16, KC)
            qT = A([P, KC, N], BF16, KC)
            kT = A([P, KC, N], BF16, KC)
            vT = A([P, 2, N], BF16, 1)
            NB = (N + LC - 1) // LC
            RB = LC if not sample else NS
            ktok = A([LC, NB, D], BF16, NB)
            hftok = ktok
            szc = szm
            sgc = sgm
            ymg = xc
            if not sample:
                big = A([P, KC * N * 2], BF16, KC)
                vtok = TB(None, 1); vtok.b = big.b[0:NB]
                vtok_v = big.t[0:LC, 0:NB * D].rearrange("p (b d) -> p b d", d=D)
                uc32 = TB(None, 1); uc32.b = big.b
                uc32_v = big.t[:].bitcast(F32).rearrange("p (k n) -> p k n", n=N)
            else:
                vtok = A([LC, NB, D], BF16, NB)
                vtok_v = vtok.t[:]
                uc32 = A([P, KC, N], F32, KC)
                uc32_v = uc32.t[:]
            hmg = qT
            sgb = qT if sample else A([P, KC, N], BF16, KC)
            ucg = kT
            mrg = hT
            tmpf = Ring([A([P, N], F32) for _ in range(4)])
            sq = tmpf
            tmpb = Ring([A([P, N], BF16) for _ in range(2)])
            sqb = Ring([A([P, N], BF16) for _ in range(2)])
            rstd = A([P, N], F32)
            meanT = A([P, N], F32)
            dgm = Ring([A([P, MW, P], BF16) for _ in range(2)])
            dgc = Ring([A([P, CW, P], BF16) for _ in range(2)])
            stg = Ring([A([P, D], F32) for _ in range(1 if not sample else 2)])
            gsb = A([LC, NB, 8], F32)
            bgt = A([LC, 8], F32)
            gt = [A([LC, NB, 4], F32) for _ in range(4)]
            if not sample:
                C32 = A([P, H, 2, DH], F32, H)
                Cb = A([P, H, 2, DH], BF16, H)
                nb_ = A([P, H, 2], BF16)
                rowA = A([H, N], F32)
                rowF = A([H, N], F32)
                sm8 = [A([H, NB], F32) for _ in range(6)]
                adg = A([H, NB, H], F32)
                abc = A([P, NB, H], F32)
                ctk = A([LC, NB, 8], F32)
                cbk = A([LC, NB, H], BF16)
                Sm = Ring([A([LC, H, LC], BF16) for _ in range(NB)])
                st6 = A([LC, H, 6], F32)
                mv = A([LC, H, 2], F32)
                smls = [[A([LC, H], F32) for _ in range(5)] for _ in range(2)]
                st6s = [A([LC, H, 6], F32) for _ in range(2)]
                mvs = [A([LC, H, 2], F32) for _ in range(2)]
            else:
                hxmS = A([P, KC, MW - 1, NS], BF16, KC)
                huS = A([P, KC, CW - 1, NS], BF16, KC)
                qtok = A([NS, D], F32)
                qTf = A([P, KC, NS], F32, KC)
                qm = A([P, KC, NS, NS], BF16, KC)
                kexp = A([NS, 4, D], BF16)
                Cin = Ring([A([P, H, 2, DH], F32) for _ in range(3)])
                Cbf = Ring([A([P, H, 2, DH], BF16) for _ in range(2)])
                Cout = Ring([A([P, H, 2, DH], F32) for _ in range(2)])
                nst = A([NS, H, DH], F32)
                nnew = A([NS, H, DH], F32)
                mprev = A([NS, H], F32)
                s16 = [A([NS, H], F32) for _ in range(10)]
                dexp = A([NS, NS, H], F32)
                dbc = A([P, NS, H], F32)
                prod = A([NS, D], F32)
                numt = A([NS, D], F32)
                st6 = A([NS, H, 6], F32)
                mv = A([NS, H, 2], F32)

            for _nm, _val in list(locals().items()):
                if isinstance(_val, TB) and _val.t is not None:
                    DBG[(sample, _nm)] = _val.t.name

            def modv(l, which, k):
                if not sample:
                    return modall.t[:, l, which * 8 + k, 0:1]
                return modall.t[:, l, which * 8 + k, 1:NTOK]

            def Av(l, k):
                if not sample:
                    return Acol.t[:, l, k, 0:1]
                return Acol.t[:, l, k, 1:NTOK]

            rms_pending = []

            def rms_flush():
                while rms_pending:
                    k, s_ = rms_pending.pop(0)
                    S.op("pe", lambda e: e.matmul(out=psS.t[:, 0:N], lhsT=onesb, rhs=s_.t[:], start=(k == 0), stop=(k == KC - 1)),
                         reads=s_.all + CST, writes=psS.all, sig=True)

            def rms_chunk(k, defer=False):
                rms_flush()
                s_ = sqb.next()
                S.op("act", lambda e: e.activation(out=s_.t[:], in_=xT.t[:, k, :], func=AF.Square), reads=[xT.b[k]], writes=s_.all)
                rms_pending.append((k, s_))
                if not defer:
                    rms_flush()

            def rms_finish():
                S.op("act", lambda e: e.activation(out=rstd.t[:], in_=psS.t[:, 0:N], func=AF.Sqrt, bias=epsc.t[:, 0:1], scale=1.0 / D), reads=psS.all + epsc.all, writes=rstd.all)
                S.op("dve", lambda e: e.reciprocal(out=rstd.t[:], in_=rstd.t[:]), reads=rstd.all, writes=rstd.all)

            def build_diag(l, W, vbase, ring, k):
                dg = ring.next()
                if W == CW and use_scr and not sample:
                    S.dma("sp", lambda e: e.dma_start(out=dg.t[:].rearrange("p j m -> p (j m)"), in_=dgscr[l, k]), reads=[dgB[l][k]], writes=dg.all)
                    return dg
                S.op("dve", lambda e: e.tensor_tensor(out=dg.t[:], in0=bcast(identb.unsqueeze(1), [P, W, P]), in1=bcast(vcol.t[:, l, k, vbase:vbase + W].unsqueeze(2), [P, W, P]), op=ALU.mult),
                     reads=CST + vcol.all, writes=dg.all)
                if W == CW and use_scr and sample:
                    S.dma("sp", lambda e: e.dma_start(out=dgscr[l, k], in_=dg.t[:].rearrange("p j m -> p (j m)")), reads=dg.all, writes=[dgB[l][k]])
                return dg

            def dwconv(l, W, vbase, ring, src_of, evac, pre=()):
                for k in range(KC):
                    dg = pre[k] if k < len(pre) else build_diag(l, W, vbase, ring, k)
                    pa = psA6.next()
                    for j in range(W):
                        rhs, rb = src_of(k, j)
                        S.op("pe", lambda e, dg=dg, j=j, pa=pa, rhs=rhs: e.matmul(out=pa.t[:, 0:N], lhsT=dg.t[:, j, :], rhs=rhs, start=(j == 0), stop=(j == W - 1)),
                             reads=dg.all + rb, writes=pa.all, sig=(j == W - 1))
                    evac(k, pa)

            def store_rows(src_ap_of_k, src_bufs, R, dram_rows, bf):
                s_ = stg.next()
                for half in range(2):
                    if bf:
                        po = psT
                    else:
                        po = psA.next()
                    for kk in range(4):
                        k = half * 4 + kk
                        S.op("pe", lambda e, po=po, k=k, kk=kk: e.transpose(out=po.t[0:R, kk * P:(kk + 1) * P], in_=src_ap_of_k(k), identity=(identb if bf else ident)),
                             reads=src_bufs + CST, writes=po.all, sig=(kk == 3))
                    S.op("dve", lambda e, po=po, half=half, s_=s_: e.tensor_copy(out=s_.t[0:R, half * 512:(half + 1) * 512], in_=po.t[0:R, 0:512]), reads=po.all, writes=s_.all)
                S.dma("sp", lambda e, s_=s_: e.dma_start(out=dram_rows, in_=s_.t[0:R, :]), reads=s_.all)

            def load_rows(dram_rows, R, writer):
                s_ = stg.next()
                S.dma("sp", lambda e, s_=s_: e.dma_start(out=s_.t[0:R, :], in_=dram_rows), writes=s_.all)
                nper = max(1, min(4, 512 // R))
                k = 0
                while k < KC:
                    n = min(nper, KC - k)
                    pa = psA.next()
                    for kk in range(n):
                        S.op("pe", lambda e, pa=pa, k=k, kk=kk, s_=s_: e.transpose(out=pa.t[:, kk * R:(kk + 1) * R], in_=s_.t[0:R, (k + kk) * P:(k + kk + 1) * P], identity=ident[0:R, 0:R]),
                             reads=s_.all + CST, writes=pa.all, sig=(kk == n - 1))
                    writer(k, n, pa)
                    k += n

            def layer(l, ti):
                last_tile = (ti == NT - 1)
                rms_finish()
                for k in range(KC):
                    tf = tmpf.next()
                    if not sample:
                        S.op("dve", lambda e, tf=tf, k=k: e.scalar_tensor_tensor(out=tf.t[:], in0=xT.t[:, k, :], scalar=Av(l, k), in1=rstd.t[:], op0=ALU.mult, op1=ALU.mult),
                             reads=[xT.b[k]] + rstd.all + MOD, writes=tf.all)
                        S.op("act", lambda e, tf=tf, k=k: e.activation(out=hT.t[:, k, :], in_=tf.t[:], func=AF.Identity, bias=modv(l, 0, k), scale=1.0),
                             reads=tf.all + MOD, writes=[hT.b[k]])
                    else:
                        S.op("dve", lambda e, tf=tf, k=k: e.tensor_tensor(out=tf.t[:], in0=xT.t[:, k, :], in1=rstd.t[:], op=ALU.mult), reads=[xT.b[k]] + rstd.all, writes=tf.all)
                        S.op("dve", lambda e, tf=tf, k=k: e.tensor_tensor(out=tf.t[:], in0=tf.t[:], in1=Av(l, k), op=ALU.mult), reads=tf.all + MOD, writes=tf.all)
                        S.op("dve", lambda e, tf=tf, k=k: e.tensor_tensor(out=hT.t[:, k, :], in0=tf.t[:], in1=modv(l, 0, k), op=ALU.add), reads=tf.all + MOD, writes=[hT.b[k]])

                if not sample:
                    S.op("dve", lambda e: e.tensor_copy(out=XU.t[:, :, XO:XO + MW - 1], in_=hxm.t[:, l, :, :]), reads=[hxm.b[l]], writes=xm.all)
                else:
                    def wr_m(k, n, pa):
                        S.op("dve", lambda e: e.tensor_copy(out=hxmS.t[:, k:k + n, :, :].rearrange("p k j b -> p k b j"), in_=pa.t[:, 0:n * NS * (MW - 1)].rearrange("p (k b j) -> p k b j", k=n, b=NS)),
                             reads=pa.all, writes=hxmS.b[k:k + n])
                    load_rows(smc[l, :, :], NS * (MW - 1), wr_m)
                    SB4 = 4
                    for b0 in range(0, NS, SB4):
                        def wr_c(k, n, pa, b0=b0):
                            S.op("dve", lambda e: e.tensor_copy(out=huS.t[:, k:k + n, :, b0:b0 + SB4].rearrange("p k j b -> p k b j"), in_=pa.t[:, 0:n * SB4 * (CW - 1)].rearrange("p (k b j) -> p k b j", k=n, b=SB4)),
                                 reads=pa.all, writes=huS.b[k:k + n])
                        load_rows(scv[l, b0 * (CW - 1):(b0 + SB4) * (CW - 1), :], SB4 * (CW - 1), wr_c)
                    S.dma("sp", lambda e: e.dma_start(out=omcs[l].rearrange("(b j) d -> b j d", j=MW - 1)[:, 0:MW - 2, :], in_=smc[l].rearrange("(b j) d -> b j d", j=MW - 1)[:, 1:MW - 1, :]))
                    S.dma("sp", lambda e: e.dma_start(out=ocvs[l].rearrange("(b j) d -> b j d", j=CW - 1)[:, 0:CW - 2, :], in_=scv[l].rearrange("(b j) d -> b j d", j=CW - 1)[:, 1:CW - 1, :]))

                def ev_xm(fc, pa):
                    S.op("dve", lambda e: e.tensor_copy(out=XU.t[:, fc, CW - 1:CW - 1 + N], in_=pa.t[:, 0:N]), reads=pa.all, writes=[xm.b[fc]])
                proj(w_in[l], 0 * D, KC, hT, N, ev_xm)
                pre_c = [build_diag(l, CW, V_CCW, dgc, 0), build_diag(l, CW, V_CCW, dgc, 1)]

                def ev_zm(fc, pa):
                    S.op("act", lambda e: e.activation(out=szm.t[:, fc, :], in_=pa.t[:, 0:N], func=AF.Silu), reads=pa.all, writes=[szm.b[fc]])

                def src_m(k, j):
                    if not sample:
                        return XU.t[:, k, XO + j:XO + j + N], [xm.b[k]]
                    if j < MW - 1:
                        return hxmS.t[:, k, j, :], [hxmS.b[k]]
                    return XU.t[:, k, CW - 1:CW - 1 + N], [xm.b[k]]

                def ev_mc(k, pa):
                    S.op("act", lambda e: e.activation(out=xc.t[:, k, :], in_=pa.t[:, 0:N], func=AF.Silu, bias=vc(l, k, V_MCB), scale=1.0), reads=pa.all + vcol.all, writes=[xc.b[k]])
                dwconv(l, MW, V_MCW, dgm, src_m, ev_mc)

                def ev_gm(fc, pa):
                    S.op("act", lambda e: e.activation(out=sgm.t[:, fc, :], in_=pa.t[:, 0:N], func=AF.Sigmoid), reads=pa.all, writes=[sgm.b[fc]])

                if not sample:
                    S.op("dve", lambda e: e.tensor_copy(out=hxm.t[:, l, :, :], in_=XU.t[:, :, XO + N:XO + N + MW - 1]), reads=xm.all, writes=[hxm.b[l]])
                    if last_tile:
                        store_rows(lambda k: hxm.t[:, l, k, :], [hxm.b[l]], MW - 1, omcp[l, :, :], True)
                else:
                    store_rows(lambda k: XU.t[:, k, CW - 1:CW - 1 + N], xm.all, NS, omcs[l].rearrange("(b j) d -> b j d", j=MW - 1)[:, MW - 2, :], True)

                wq = wload(w_q[l].rearrange("h (k p) e -> p (h k) e", p=P), KC, DH)
                wk = wload(w_k[l].rearrange("h (k p) e -> p (h k) e", p=P), KC, DH)
                wv = wload(w_v[l].rearrange("h (k p) e -> p (h k) e", p=P), KC, DH)
                wg = sb_wg
                S.dma("pool", lambda e: e.dma_start(out=wg.t[:], in_=w_gate[l].rearrange("(c p) g -> p c g", p=P)), writes=wg.all)
                S.dma("sp", lambda e: e.dma_start(out=bgt.t[:], in_=b_gate[l:l + 1, :].partition_broadcast(LC)), writes=bgt.all)

                def fm_proj(w, src, src_off, dst, scale, h):
                    for ec in range(2):
                        pa = psA6.next()
                        for k in range(2):
                            S.op("pe", lambda e, pa=pa, k=k, ec=ec: e.matmul(out=pa.t[:, 0:N], lhsT=w.t[:, h * 2 + k, ec * P:(ec + 1) * P], rhs=src.t[:, h * 2 + k, src_off:src_off + N], start=(k == 0), stop=(k == 1)),
                                 reads=w.all + [src.b[h * 2 + k]], writes=pa.all, sig=(k == 1))
                        if dst is vT:
                            S.op("act", lambda e, pa=pa, ec=ec: e.activation(out=vT.t[:, ec, :], in_=pa.t[:, 0:N], func=AF.Copy, scale=scale), reads=pa.all, writes=vT.all)
                        elif dst is kT:
                            S.op("dve", lambda e, pa=pa, ec=ec: e.tensor_scalar(out=dst.t[:, h * 2 + ec, :], in0=pa.t[:, 0:N], scalar1=scale, scalar2=None, op0=ALU.mult), reads=pa.all, writes=[dst.b[h * 2 + ec]])
                        else:
                            S.op("act", lambda e, pa=pa, ec=ec: e.activation(out=dst.t[:, h * 2 + ec, :], in_=pa.t[:, 0:N], func=AF.Copy, scale=scale), reads=pa.all, writes=[dst.b[h * 2 + ec]])

                def gate_mm(src_ap, srcb, cidx, first, last):
                    for blk in range(NB):
                        S.op("pe", lambda e, blk=blk: e.matmul(out=psS.t[0:RB, blk * 8:(blk + 1) * 8], lhsT=src_ap(blk), rhs=wg.t[:, cidx, :], start=(first and blk == 0), stop=last, skip_group_check=True),
                             reads=srcb + wg.all, writes=psS.all, sig=(blk == NB - 1))

                for h in range(H):
                    fm_proj(wq, xc, 0, qT, 1.0, h)
                    fm_proj(wk, xc, 0, kT, DH ** -0.5, h)
                    fm_proj(wv, xm, CW - 1, vT, 1.0, h)
                    for blk in range(NB):
                        for (w, src, off, dst, scale, eng) in ((wk, xc, 0, ktok, DH ** -0.5, "act"), (wv, xm, CW - 1, vtok, 1.0, "dve")) + (((wq, xc, 0, None, 1.0, "dve"),) if sample else ()):
                            pa = psA6.next()
                            for k in range(2):
                                S.op("pe", lambda e, pa=pa, k=k, w=w, src=src, off=off, blk=blk: e.matmul(out=pa.t[0:RB, 0:DH], lhsT=src.t[:, h * 2 + k, off + blk * LC:off + blk * LC + RB], rhs=w.t[:, h * 2 + k, 0:DH], start=(k == 0), stop=(k == 1)),
                                     reads=w.all + [src.b[h * 2 + k]], writes=pa.all, sig=(k == 1))
                            if dst is None:
                                S.op("dve", lambda e, pa=pa: e.tensor_copy(out=qtok.t[:, h * DH:(h + 1) * DH], in_=pa.t[0:RB, 0:DH]), reads=pa.all, writes=qtok.all)
                            elif eng == "act":
                                S.op("act", lambda e, pa=pa, dst=dst, blk=blk, scale=scale: e.activation(out=dst.t[0:RB, blk, h * DH:(h + 1) * DH], in_=pa.t[0:RB, 0:DH], func=AF.Copy, scale=scale), reads=pa.all, writes=[dst.b[blk]])
                            else:
                                S.op("dve", lambda e, pa=pa, dst=dst, blk=blk: e.tensor_copy(out=vtok_v[0:RB, blk, h * DH:(h + 1) * DH], in_=pa.t[0:RB, 0:DH]), reads=pa.all, writes=[dst.b[blk]])
                    for ec in range(2):
                        gate_mm(lambda blk, ec=ec: qT.t[:, h * 2 + ec, blk * LC:blk * LC + RB], [qT.b[h * 2 + ec]], h * 2 + ec, (h == 0 and ec == 0), False)
                        gate_mm(lambda blk, ec=ec: kT.t[:, h * 2 + ec, blk * LC:blk * LC + RB], [kT.b[h * 2 + ec]], KC + h * 2 + ec, False, False)
                    for ec in range(2):
                        gate_mm(lambda blk, ec=ec: vT.t[:, ec, blk * LC:blk * LC + RB], vT.all, 2 * KC + h * 2 + ec, False, (h == H - 1 and ec == 1))

                gi, lf, t1, t2 = gt
                S.op("dve", lambda e: e.tensor_tensor(out=gsb.t[0:RB], in0=psS.t[0:RB, 0:NB * 8].rearrange("p (b g) -> p b g", g=8), in1=bcast(bgt.t[0:RB].unsqueeze(1), [RB, NB, 8]), op=ALU.add),
                     reads=psS.all + bgt.all, writes=gsb.all)
                S.op("act", lambda e: e.activation(out=t1.t[0:RB], in_=gsb.t[0:RB, :, 4:8], func=AF.Abs), reads=gsb.all, writes=t1.all)
                S.op("act", lambda e: e.activation(out=t1.t[0:RB], in_=t1.t[0:RB], func=AF.Exp, scale=-1.0), reads=t1.all, writes=t1.all)
                S.op("act", lambda e: e.activation(out=t1.t[0:RB], in_=t1.t[0:RB], func=AF.Ln, bias=1.0, scale=1.0), reads=t1.all, writes=t1.all)
                S.op("dve", lambda e: e.tensor_scalar_min(out=t2.t[0:RB], in0=gsb.t[0:RB, :, 4:8], scalar1=0.0), reads=gsb.all, writes=t2.all)
                S.op("dve", lambda e: e.tensor_sub(out=lf.t[0:RB], in0=t2.t[0:RB], in1=t1.t[0:RB]), reads=t1.all + t2.all, writes=lf.all)
                S.op("dve", lambda e: e.tensor_copy(out=gi.t[0:RB], in_=gsb.t[0:RB, :, 0:4]), reads=gsb.all, writes=gi.all)

                def ev_gb(fc, pa):
                    S.op("act", lambda e: e.activation(out=sgb.t[:, fc, :], in_=pa.t[:, 0:N], func=AF.Sigmoid), reads=pa.all, writes=[sgb.b[fc]])

                def ev_ga(fc, pa):
                    S.op("dve", lambda e: e.tensor_tensor(out=uu.t[:, fc, CW - 1:CW - 1 + N], in0=pa.t[:, 0:N], in1=sgb.t[:, fc, :], op=ALU.mult), reads=pa.all + [sgb.b[fc]], writes=[uu.b[fc]])

                if not sample:
                    S.op("dve", lambda e: e.tensor_copy(out=uu.t[:, :, 0:CW - 1], in_=hu.t[:, l, :, :]), reads=[hu.b[l]], writes=uu.all)

                    def fill():
                        yield from proj_gen(w_in[l], 3 * D, KC, hT, N, ev_gb)
                        yield from proj_gen(w_in[l], 2 * D, KC, hT, N, ev_ga)

                    def fill2():
                        yield from proj_gen(w_in[l], 1 * D, KC, hT, N, ev_zm, ring=psA6)
                        yield from proj_gen(w_in[l], 5 * D, KC, hT, N, ev_gm, ring=psA6)
                    filler = fill()
                    filler2 = fill2()
                    mlstm_prompt(l, ti, gi, lf, filler, filler2)
                    for _ in filler:
                        pass
                else:
                    proj(w_in[l], 1 * D, KC, hT, N, ev_zm)
                    proj(w_in[l], 5 * D, KC, hT, N, ev_gm)
                    mlstm_sample(l, gi, lf)

                if sample:
                    proj(w_in[l], 3 * D, KC, hT, N, ev_gb)
                    proj(w_in[l], 2 * D, KC, hT, N, ev_ga)
                def src_c(k, j):
                    if not sample:
                        return uu.t[:, k, j:j + N], [uu.b[k]]
                    if j < CW - 1:
                        return huS.t[:, k, j, :], [huS.b[k]]
                    return uu.t[:, k, CW - 1:CW - 1 + N], [uu.b[k]]

                def ev_cc(k, pa):
                    S.op("act", lambda e: e.activation(out=uc32_v[:, k, :], in_=pa.t[:, 0:N], func=AF.Identity, bias=vc(l, k, V_CCB), scale=1.0), reads=pa.all + vcol.all, writes=[uc32.b[k]])
                dwconv(l, CW, V_CCW, dgc, src_c, ev_cc, pre_c)

                for fc in range(KC):
                    for blk in range(NB):
                        S.op("pe", lambda e, fc=fc, blk=blk: e.transpose(out=psT.t[:, blk * LC:blk * LC + RB], in_=hftok.t[0:RB, blk, fc * P:(fc + 1) * P], identity=identb[0:RB, 0:RB]),
                             reads=[hftok.b[blk]] + CST, writes=psT.all, sig=(blk == NB - 1))
                    tf = sqb.next()
                    tb_ = tmpb.next()
                    S.op("dve", lambda e, fc=fc, tf=tf: e.scalar_tensor_tensor(out=tf.t[:], in0=psT.t[:, 0:N], scalar=vc(l, fc, V_GN), in1=szm.t[:, fc, :], op0=ALU.mult, op1=ALU.mult),
                         reads=psT.all + [szm.b[fc]] + vcol.all, writes=tf.all)
                    S.op("dve", lambda e, fc=fc, tb_=tb_: e.scalar_tensor_tensor(out=tb_.t[:], in0=xc.t[:, fc, :], scalar=vc(l, fc, V_SKIP), in1=szm.t[:, fc, :], op0=ALU.mult, op1=ALU.mult),
                         reads=[xc.b[fc], szm.b[fc]] + vcol.all, writes=tb_.all)
                    S.op("dve", lambda e, fc=fc, tf=tf, tb_=tb_: e.tensor_tensor(out=hmg.t[:, fc, :], in0=tf.t[:], in1=tb_.t[:], op=ALU.add), reads=tf.all + tb_.all, writes=[hmg.b[fc]])

                def ev_ym(fc, pa):
                    S.op("dve", lambda e: e.tensor_tensor(out=ymg.t[:, fc, :], in0=pa.t[:, 0:N], in1=sgm.t[:, fc, :], op=ALU.mult), reads=pa.all + [sgm.b[fc]], writes=[ymg.b[fc]])
                proj(w_br_m[l], 0, KC, hmg, N, ev_ym)


                def ev_gc(fc, pa):
                    S.op("act", lambda e: e.activation(out=sgc.t[:, fc, :], in_=pa.t[:, 0:N], func=AF.Sigmoid), reads=pa.all, writes=[sgc.b[fc]])


                def ev_zc(fc, pa):
                    S.op("act", lambda e: e.activation(out=szc.t[:, fc, :], in_=pa.t[:, 0:N], func=AF.Silu), reads=pa.all, writes=[szc.b[fc]])

                if not sample:
                    S.op("dve", lambda e: e.tensor_copy(out=hu.t[:, l, :, :], in_=uu.t[:, :, N:N + CW - 1]), reads=uu.all, writes=[hu.b[l]])
                    if last_tile:
                        store_rows(lambda k: hu.t[:, l, k, :], [hu.b[l]], CW - 1, ocvp[l, :, :], True)
                else:
                    store_rows(lambda k: uu.t[:, k, CW - 1:CW - 1 + N], uu.all, NS, ocvs[l].rearrange("(b j) d -> b j d", j=CW - 1)[:, CW - 2, :], True)

                pm = psA.next()
                pq = psA.next()
                for k in range(KC):
                    S.op("pe", lambda e, k=k: e.matmul(out=pm.t[:, 0:N], lhsT=ones, rhs=uc32_v[:, k, :], start=(k == 0), stop=(k == KC - 1)), reads=[uc32.b[k]] + CST, writes=pm.all, sig=(k == KC - 1))
                for k in range(KC):
                    s_ = tmpb.next()
                    S.op("act", lambda e, s_=s_, k=k: e.activation(out=s_.t[:], in_=uc32_v[:, k, :], func=AF.Square), reads=[uc32.b[k]], writes=s_.all)
                    S.op("pe", lambda e, s_=s_, k=k: e.matmul(out=pq.t[:, 0:N], lhsT=onesb, rhs=s_.t[:], start=(k == 0), stop=(k == KC - 1)), reads=s_.all + CST, writes=pq.all, sig=True)
                mean = meanT
                var = rstd
                S.op("dve", lambda e: e.tensor_scalar(out=mean.t[:], in0=pm.t[:, 0:N], scalar1=1.0 / D, scalar2=None, op0=ALU.mult), reads=pm.all, writes=mean.all)
                S.op("dve", lambda e: e.tensor_tensor(out=var.t[:], in0=mean.t[:], in1=mean.t[:], op=ALU.mult), reads=mean.all, writes=var.all)
                S.op("dve", lambda e: e.scalar_tensor_tensor(out=var.t[:], in0=pq.t[:, 0:N], scalar=1.0 / D, in1=var.t[:], op0=ALU.mult, op1=ALU.subtract), reads=pq.all + var.all, writes=var.all)
                S.op("act", lambda e: e.activation(out=var.t[:], in_=var.t[:], func=AF.Sqrt, bias=epsc.t[:, 0:1], scale=1.0), reads=var.all + epsc.all, writes=var.all)
                S.op("dve", lambda e: e.reciprocal(out=var.t[:], in_=var.t[:]), reads=var.all, writes=var.all)
                for k in range(KC):
                    tf = tmpf.next()
                    S.op("dve", lambda e, k=k, tf=tf: e.tensor_tensor(out=tf.t[:], in0=uc32_v[:, k, :], in1=mean.t[:], op=ALU.subtract), reads=[uc32.b[k]] + mean.all, writes=tf.all)
                    S.op("dve", lambda e, k=k, tf=tf: e.tensor_tensor(out=tf.t[:], in0=tf.t[:], in1=var.t[:], op=ALU.mult), reads=tf.all + var.all, writes=tf.all)
                    S.op("act", lambda e, k=k, tf=tf: e.activation(out=ucg.t[:, k, :], in_=tf.t[:], func=AF.Silu, bias=vc(l, k, V_CLNB), scale=vc(l, k, V_CLNW)), reads=tf.all + vcol.all, writes=[ucg.b[k]])
                proj(w_in[l], 4 * D, KC, hT, N, ev_zc)
                for k in range(KC):
                    S.op("dve", lambda e, k=k: e.tensor_tensor(out=ucg.t[:, k, :], in0=ucg.t[:, k, :], in1=szc.t[:, k, :], op=ALU.mult), reads=[ucg.b[k], szc.b[k]], writes=[ucg.b[k]])

                proj(w_in[l], 6 * D, KC, hT, N, ev_gc)

                def ev_yc(fc, pa):
                    tf = tmpb.next()
                    S.op("dve", lambda e: e.tensor_tensor(out=tf.t[:], in0=pa.t[:, 0:N], in1=sgc.t[:, fc, :], op=ALU.mult), reads=pa.all + [sgc.b[fc]], writes=tf.all)
                    S.op("dve", lambda e: e.tensor_tensor(out=mrg.t[:, fc, :], in0=tf.t[:], in1=ymg.t[:, fc, :], op=ALU.add), reads=tf.all + [ymg.b[fc]], writes=[mrg.b[fc]])
                proj(w_br_c[l], 0, KC, ucg, N, ev_yc)

                def ev_out(fc, pa):
                    if not sample:
                        S.op("dve", lambda e: e.scalar_tensor_tensor(out=xT.t[:, fc, :], in0=pa.t[:, 0:N], scalar=modv(l, 2, fc), in1=xT.t[:, fc, :], op0=ALU.mult, op1=ALU.add), reads=pa.all + [xT.b[fc]] + MOD, writes=[xT.b[fc]])
                    else:
                        tf = tmpf.next()
                        S.op("dve", lambda e: e.tensor_tensor(out=tf.t[:], in0=pa.t[:, 0:N], in1=modv(l, 2, fc), op=ALU.mult), reads=pa.all + MOD, writes=tf.all)
                        S.op("dve", lambda e: e.tensor_tensor(out=xT.t[:, fc, :], in0=tf.t[:], in1=xT.t[:, fc, :], op=ALU.add), reads=tf.all + [xT.b[fc]], writes=[xT.b[fc]])
                    rms_chunk(fc, defer=True)
                proj(w_out[l], 0, KC, mrg, N, ev_out)
                rms_flush()

            def mlstm_prompt(l, ti, gi, lf, filler=None, filler2=None):
                aT, FT, cT, tT = rowA, rowF, rowA, rowF
                if ti == 0:
                    S.op("dve", lambda e: e.memset(C32.t[:], 0.0), writes=C32.all)
                else:
                    S.dma("sp", lambda e: e.dma_start(out=C32.t[:], in_=oCp[l].rearrange("h (k p) e -> p h k e", p=P)), reads=[dC[l]], writes=C32.all)
                Amax, FL, mall, mprv, Mx, alp = sm8
                def fill(n):
                    if filler is not None:
                        for _ in range(n):
                            next(filler, None)

                def fill_g(n):
                    if filler2 is not None:
                        for _ in range(n):
                            next(filler2, None)
                fill_g(4)
                pa = psA.next()
                pf = psA.next()
                for blk in range(NB):
                    S.op("pe", lambda e, blk=blk: e.matmul(out=pa.t[0:H, blk * LC:(blk + 1) * LC], lhsT=gi.t[:, blk, :], rhs=ident[0:LC, 0:LC], start=True, stop=False), reads=gi.all + CST, writes=pa.all, sig=False)
                    S.op("pe", lambda e, blk=blk: e.matmul(out=pa.t[0:H, blk * LC:(blk + 1) * LC], lhsT=lf.t[:, blk, :], rhs=ntri[0:LC, 0:LC], start=False, stop=True), reads=lf.all + CST, writes=pa.all, sig=False)
                    S.op("pe", lambda e, blk=blk: e.matmul(out=pf.t[0:H, blk * LC:(blk + 1) * LC], lhsT=lf.t[:, blk, :], rhs=triu[0:LC, 0:LC], start=True, stop=True), reads=lf.all + CST, writes=pf.all, sig=True)
                S.op("act", lambda e: e.activation(out=aT.t[:], in_=pa.t[0:H, 0:N], func=AF.Copy), reads=pa.all, writes=aT.all)
                S.op("act", lambda e: e.activation(out=FT.t[:], in_=pf.t[0:H, 0:N], func=AF.Copy), reads=pf.all, writes=FT.all)
                fill_g(4)
                S.op("dve", lambda e: e.tensor_reduce(out=Amax.t[:], in_=aT.t[:].rearrange("p (c s) -> p c s", s=LC), axis=AX.X, op=ALU.max), reads=aT.all, writes=Amax.all)
                S.op("dve", lambda e: e.tensor_copy(out=FL.t[:], in_=FT.t[:].rearrange("p (c s) -> p c s", s=LC)[:, :, LC - 1]), reads=FT.all, writes=FL.all)
                S.op("dve", lambda e: e.tensor_tensor_scan(out=mall.t[:], data0=Amax.t[:], data1=FL.t[:], initial=mst.t[:, l:l + 1], op0=ALU.max, op1=ALU.add), reads=Amax.all + FL.all + [mst.b[l]], writes=mall.all)
                S.op("dve", lambda e: e.tensor_copy(out=mprv.t[:, 0:1], in_=mst.t[:, l:l + 1]), reads=[mst.b[l]], writes=mprv.all)
                S.op("dve", lambda e: e.tensor_copy(out=mprv.t[:, 1:NB], in_=mall.t[:, 0:NB - 1]), reads=mall.all + mprv.all, writes=mprv.all)
                S.op("dve", lambda e: e.tensor_copy(out=mst.t[:, l:l + 1], in_=mall.t[:, NB - 1:NB]), reads=mall.all + mprv.all, writes=[mst.b[l]])
                S.op("dve", lambda e: e.tensor_tensor(out=Mx.t[:], in0=mprv.t[:], in1=Amax.t[:], op=ALU.max), reads=mprv.all + Amax.all, writes=Mx.all)
                S.op("dve", lambda e: e.tensor_sub(out=alp.t[:], in0=mprv.t[:], in1=Mx.t[:]), reads=mprv.all + Mx.all, writes=alp.all)
                S.op("act", lambda e: e.activation(out=alp.t[:], in_=alp.t[:], func=AF.Exp), reads=alp.all, writes=alp.all)
                Mb = bcast(Mx.t[:].unsqueeze(2), [H, NB, LC])
                S.op("dve", lambda e: e.tensor_tensor(out=cT.t[:].rearrange("p (c s) -> p c s", s=LC), in0=aT.t[:].rearrange("p (c s) -> p c s", s=LC), in1=Mb, op=ALU.subtract), reads=aT.all + Mx.all, writes=cT.all)
                S.op("act", lambda e: e.activation(out=cT.t[:], in_=cT.t[:], func=AF.Exp), reads=cT.all, writes=cT.all)
                S.op("dve", lambda e: e.tensor_tensor(out=tT.t[:].rearrange("p (c s) -> p c s", s=LC), in0=FT.t[:].rearrange("p (c s) -> p c s", s=LC), in1=Mb, op=ALU.add), reads=FT.all + Mx.all, writes=tT.all)
                S.op("act", lambda e: e.activation(out=tT.t[:], in_=tT.t[:], func=AF.Exp, scale=-1.0), reads=tT.all, writes=tT.all)
                fill_g(4)
                for blk in range(NB):
                    S.op("pe", lambda e, blk=blk: e.matmul(out=psS.t[0:LC, blk * 8:blk * 8 + 4], lhsT=cT.t[:, blk * LC:(blk + 1) * LC], rhs=ident[0:H, 0:H], start=True, stop=True), reads=cT.all + CST, writes=psS.all, sig=False)
                    S.op("pe", lambda e, blk=blk: e.matmul(out=psS.t[0:LC, blk * 8 + 4:blk * 8 + 8], lhsT=tT.t[:, blk * LC:(blk + 1) * LC], rhs=ident[0:H, 0:H], start=True, stop=True), reads=tT.all + CST, writes=psS.all, sig=(blk == NB - 1))
                S.op("dve", lambda e: e.tensor_copy(out=ctk.t[:], in_=psS.t[0:LC, 0:NB * 8].rearrange("p (b g) -> p b g", g=8)), reads=psS.all, writes=ctk.all)
                S.op("dve", lambda e: e.tensor_copy(out=cbk.t[:], in_=ctk.t[:, :, 0:4]), reads=ctk.all, writes=cbk.all)
                fill_g(4)
                S.op("dve", lambda e: e.tensor_tensor(out=adg.t[:], in0=bcast(alp.t[:].unsqueeze(2), [H, NB, H]), in1=bcast(ident[0:H, 0:H].unsqueeze(1), [H, NB, H]), op=ALU.mult), reads=alp.all + CST, writes=adg.all)
                S.op("pe", lambda e: e.matmul(out=psS.t[:, 0:NB * H], lhsT=ones[0:H, :], rhs=adg.t[:].rearrange("p c h -> p (c h)"), start=True, stop=True), reads=adg.all + CST, writes=psS.all, sig=True)
                S.op("dve", lambda e: e.tensor_copy(out=abc.t[:], in_=psS.t[:, 0:NB * H].rearrange("p (c h) -> p c h", h=H)), reads=psS.all, writes=abc.all)
                for blk in range(NB):
                    S.op("dve", lambda e, blk=blk: e.tensor_tensor(out=vtok_v[:, blk, :].rearrange("p (h e) -> p h e", h=H), in0=vtok_v[:, blk, :].rearrange("p (h e) -> p h e", h=H), in1=bcast(ctk.t[:, blk, 0:4].unsqueeze(2), [LC, H, DH]), op=ALU.mult),
                         reads=[vtok.b[blk]] + ctk.all, writes=[vtok.b[blk]])
                for h in range(H):
                    S.op("act", lambda e, h=h: e.activation(out=Cb.t[:, h].rearrange("p k e -> p (k e)"), in_=C32.t[:, h].rearrange("p k e -> p (k e)"), func=AF.Copy, scale=abc.t[:, 0, h:h + 1]), reads=[C32.b[h]] + abc.all, writes=[Cb.b[h]])
                S.op("dve", lambda e: e.tensor_tensor(out=nb_.t[:], in0=n32.t[:, l], in1=bcast(abc.t[:, 0, :].unsqueeze(2), [P, H, 2]), op=ALU.mult), reads=[n32.b[l]] + abc.all, writes=nb_.all)

                fill_g(16)
                sms = []
                pend = []
                for j in range(NB):
                    t0 = j * LC
                    pa = psA.next()
                    for h in range(H):
                        for k in range(2):
                            S.op("pe", lambda e, h=h, k=k, pa=pa: e.matmul(out=pa.t[0:LC, h * LC:(h + 1) * LC], lhsT=kT.t[:, h * 2 + k, t0:t0 + LC], rhs=qT.t[:, h * 2 + k, t0:t0 + LC], start=(k == 0), stop=(k == 1)),
                                 reads=[kT.b[h * 2 + k], qT.b[h * 2 + k]], writes=pa.all, sig=(h == H - 1 and k == 1))
                    sm_ = Sm.next()
                    S.op("dve", lambda e, pa=pa, sm_=sm_: e.tensor_tensor(out=sm_.t[:], in0=pa.t[0:LC, 0:H * LC].rearrange("p (h l) -> p h l", h=H), in1=bcast(maskb[0:LC, 0:LC].unsqueeze(1), [LC, H, LC]), op=ALU.mult), reads=pa.all + CST, writes=sm_.all)
                    sms.append(sm_)
                def evac_chunk(j, po):
                    dab, dmx, d2, rs, nbias = smls[j % 2]
                    S.op("dve", lambda e: e.tensor_tensor(out=dmx.t[:], in0=dab.t[:], in1=ctk.t[:, j, 4:8], op=ALU.max), reads=dab.all + ctk.all, writes=dmx.all)
                    S.op("dve", lambda e: e.tensor_tensor(out=d2.t[:], in0=dmx.t[:], in1=dmx.t[:], op=ALU.mult), reads=dmx.all, writes=d2.all)
                    for h in range(H):
                        S.op("dve", lambda e, h=h: e.bn_stats(out=st6.t[:, h, :], in_=po.t[0:LC, h * DH:(h + 1) * DH]), reads=po.all, writes=st6.all)
                    for h in range(H):
                        S.op("dve", lambda e, h=h: e.bn_aggr(out=mv.t[:, h, :], in_=st6.t[:, h, :]), reads=st6.all, writes=mv.all)
                    S.op("dve", lambda e: e.scalar_tensor_tensor(out=rs.t[:], in0=d2.t[:], scalar=EPS, in1=mv.t[:, :, 1], op0=ALU.mult, op1=ALU.add), reads=d2.all + mv.all, writes=rs.all)
                    S.op("act", lambda e: e.activation(out=rs.t[:], in_=rs.t[:], func=AF.Sqrt), reads=rs.all, writes=rs.all)
                    S.op("dve", lambda e: e.reciprocal(out=rs.t[:], in_=rs.t[:]), reads=rs.all, writes=rs.all)
                    S.op("dve", lambda e: e.scalar_tensor_tensor(out=nbias.t[:], in0=mv.t[:, :, 0], scalar=-1.0, in1=rs.t[:], op0=ALU.mult, op1=ALU.mult), reads=mv.all + rs.all, writes=nbias.all)
                    for h in range(H):
                        S.op("act", lambda e, h=h: e.activation(out=hftok.t[:, j, h * DH:(h + 1) * DH], in_=po.t[0:LC, h * DH:(h + 1) * DH], func=AF.Identity, bias=nbias.t[:, h:h + 1], scale=rs.t[:, h:h + 1]),
                             reads=po.all + rs.all + nbias.all, writes=[hftok.b[j]])
                for j in range(NB):
                    t0 = j * LC
                    po = psOr.next()
                    sm_ = sms[j]
                    for h in range(H):
                        pc = psA.next()
                        for k in range(2):
                            S.op("pe", lambda e, h=h, k=k, pc=pc: e.matmul(out=pc.t[:, k * DH:(k + 1) * DH], lhsT=ktok.t[:, j, h * DH + k * P:h * DH + (k + 1) * P], rhs=vtok_v[:, j, h * DH:(h + 1) * DH], start=True, stop=True), reads=[ktok.b[j], vtok.b[j]], writes=pc.all, sig=(k == 1))
                        S.op("dve", lambda e, h=h, pc=pc: e.scalar_tensor_tensor(out=C32.t[:, h].rearrange("p k e -> p (k e)"), in0=C32.t[:, h].rearrange("p k e -> p (k e)"), scalar=abc.t[:, j, h:h + 1], in1=pc.t[:, 0:2 * DH], op0=ALU.mult, op1=ALU.add),
                             reads=[C32.b[h]] + pc.all + abc.all, writes=[C32.b[h]])
                    pend_now = list(pend)
                    del pend[:]
                    for h in range(H):
                        S.op("pe", lambda e, h=h, sm_=sm_: e.matmul(out=po.t[0:LC, h * DH:(h + 1) * DH], lhsT=sm_.t[:, h, :], rhs=vtok_v[:, j, h * DH:(h + 1) * DH], start=True, stop=False), reads=sm_.all + [vtok.b[j]], writes=po.all, sig=False)
                        for k in range(2):
                            S.op("pe", lambda e, h=h, k=k: e.matmul(out=po.t[0:LC, h * DH:(h + 1) * DH], lhsT=qT.t[:, h * 2 + k, t0:t0 + LC], rhs=Cb.t[:, h, k, :], start=False, stop=(k == 1)), reads=[qT.b[h * 2 + k], Cb.b[h]], writes=po.all, sig=False)
                    for h in range(H):
                        S.op("pe", lambda e, h=h, sm_=sm_: e.matmul(out=psS.t[0:LC, h:h + 1], lhsT=sm_.t[:, h, :], rhs=cbk.t[:, j, h:h + 1], start=True, stop=False), reads=sm_.all + cbk.all, writes=psS.all, sig=False)
                        for k in range(2):
                            S.op("pe", lambda e, h=h, k=k: e.matmul(out=psS.t[0:LC, h:h + 1], lhsT=qT.t[:, h * 2 + k, t0:t0 + LC], rhs=nb_.t[:, h, k:k + 1], start=False, stop=(k == 1)), reads=[qT.b[h * 2 + k]] + nb_.all, writes=psS.all, sig=(h == H - 1 and k == 1))
                    dab, dmx, d2, rs, nbias = smls[j % 2]
                    S.op("act", lambda e: e.activation(out=dab.t[:], in_=psS.t[0:LC, 0:H], func=AF.Abs), reads=psS.all, writes=dab.all)
                    for pj, ppo in pend_now:
                        evac_chunk(pj, ppo)
                    if j + 1 < NB:
                        for h in range(H):
                            S.op("act", lambda e, h=h: e.activation(out=Cb.t[:, h].rearrange("p k e -> p (k e)"), in_=C32.t[:, h].rearrange("p k e -> p (k e)"), func=AF.Copy, scale=abc.t[:, j + 1, h:h + 1]), reads=[C32.b[h]] + abc.all, writes=[Cb.b[h]])
                    for h in range(H):
                        for k in range(2):
                            S.op("pe", lambda e, h=h, k=k: e.matmul(out=psS.t[:, 8 + h * 2 + k:9 + h * 2 + k], lhsT=ktok.t[:, j, h * DH + k * P:h * DH + (k + 1) * P], rhs=cbk.t[:, j, h:h + 1], start=True, stop=True), reads=[ktok.b[j]] + cbk.all, writes=psS.all, sig=(h == H - 1 and k == 1))
                    S.op("dve", lambda e: e.tensor_tensor(out=n32.t[:, l], in0=n32.t[:, l], in1=bcast(abc.t[:, j, :].unsqueeze(2), [P, H, 2]), op=ALU.mult), reads=[n32.b[l]] + abc.all, writes=[n32.b[l]])
                    S.op("dve", lambda e: e.tensor_tensor(out=n32.t[:, l], in0=n32.t[:, l], in1=psS.t[:, 8:16].rearrange("p (h k) -> p h k", k=2), op=ALU.add), reads=[n32.b[l]] + psS.all, writes=[n32.b[l]])
                    if j + 1 < NB:
                        S.op("dve", lambda e: e.tensor_tensor(out=nb_.t[:], in0=n32.t[:, l], in1=bcast(abc.t[:, j + 1, :].unsqueeze(2), [P, H, 2]), op=ALU.mult), reads=[n32.b[l]] + abc.all, writes=nb_.all)
                    pend.append((j, po))
                    fill(2)

                while pend:
                    evac_chunk(*pend.pop(0))
                S.dma("sp", lambda e: e.dma_start(out=oCp[l].rearrange("h (k p) e -> p h k e", p=P), in_=C32.t[:]), reads=C32.all, writes=[dC[l]])
                if ti == NT - 1:
                    S.dma("sp", lambda e: e.dma_start(out=onp[l].rearrange("(h k p) -> p h k", h=H, p=P), in_=n32.t[:, l], allow_slow_non_contiguous=True), reads=[n32.b[l]])
                    S.dma("sp", lambda e: e.dma_start(out=omp[l].rearrange("(h o) -> h o", o=1), in_=mst.t[:, l:l + 1], allow_slow_non_contiguous=True), reads=[mst.b[l]])

            def mlstm_sample(l, gi, lf):
                R = NS
                a_, mt, dec, wsc, thr, qk, qn, den, dmx, tmp = s16
                S.dma("sp", lambda e: e.dma_start(out=mprev.t[:], in_=sm[l, :, :]), writes=mprev.all)
                S.dma("sp", lambda e: e.dma_start(out=nst.t[:].rearrange("p h e -> p (h e)"), in_=sn[l, :, :]), writes=nst.all)
                g_i = gi.t[0:R, 0, :]
                g_f = lf.t[0:R, 0, :]
                S.op("dve", lambda e: e.tensor_add(out=a_.t[:], in0=g_f, in1=mprev.t[:]), reads=lf.all + mprev.all, writes=a_.all)
                S.op("dve", lambda e: e.tensor_max(out=mt.t[:], in0=a_.t[:], in1=g_i), reads=a_.all + gi.all, writes=mt.all)
                S.op("dve", lambda e: e.tensor_sub(out=dec.t[:], in0=a_.t[:], in1=mt.t[:]), reads=a_.all + mt.all, writes=dec.all)
                S.op("act", lambda e: e.activation(out=dec.t[:], in_=dec.t[:], func=AF.Exp), reads=dec.all, writes=dec.all)
                S.op("dve", lambda e: e.tensor_sub(out=wsc.t[:], in0=g_i, in1=mt.t[:]), reads=gi.all + mt.all, writes=wsc.all)
                S.op("act", lambda e: e.activation(out=wsc.t[:], in_=wsc.t[:], func=AF.Exp), reads=wsc.all, writes=wsc.all)
                S.op("act", lambda e: e.activation(out=thr.t[:], in_=mt.t[:], func=AF.Exp, scale=-1.0), reads=mt.all, writes=thr.all)
                S.dma("sp", lambda e: e.dma_start(out=oms[l, :, :], in_=mt.t[:]), reads=mt.all)
                ktf = tmpS_k
                S.op("act", lambda e: e.activation(out=ktf.t[:], in_=ktok.t[0:R, 0, :], func=AF.Copy), reads=ktok.all, writes=ktf.all)
                S.op("dve", lambda e: e.tensor_tensor(out=prod.t[:], in0=qtok.t[:], in1=ktf.t[:], op=ALU.mult), reads=qtok.all + ktf.all, writes=prod.all)
                S.op("dve", lambda e: e.tensor_reduce(out=qk.t[:], in_=prod.t[:].rearrange("p (h e) -> p h e", h=H), axis=AX.X, op=ALU.add), reads=prod.all, writes=qk.all)
                S.op("dve", lambda e: e.tensor_tensor(out=prod.t[:], in0=qtok.t[:], in1=nst.t[:].rearrange("p h e -> p (h e)"), op=ALU.mult), reads=qtok.all + nst.all, writes=prod.all)
                S.op("dve", lambda e: e.tensor_reduce(out=qn.t[:], in_=prod.t[:].rearrange("p (h e) -> p h e", h=H), axis=AX.X, op=ALU.add), reads=prod.all, writes=qn.all)
                S.op("dve", lambda e: e.tensor_mul(out=qk.t[:], in0=qk.t[:], in1=wsc.t[:]), reads=qk.all + wsc.all, writes=qk.all)
                S.op("dve", lambda e: e.tensor_mul(out=den.t[:], in0=dec.t[:], in1=qn.t[:]), reads=dec.all + qn.all, writes=den.all)
                S.op("dve", lambda e: e.tensor_add(out=den.t[:], in0=den.t[:], in1=qk.t[:]), reads=den.all + qk.all, writes=den.all)
                S.op("act", lambda e: e.activation(out=den.t[:], in_=den.t[:], func=AF.Abs), reads=den.all, writes=den.all)
                S.op("dve", lambda e: e.tensor_max(out=dmx.t[:], in0=den.t[:], in1=thr.t[:]), reads=den.all + thr.all, writes=dmx.all)
                S.op("dve", lambda e: e.tensor_tensor(out=nnew.t[:], in0=nst.t[:], in1=bcast(dec.t[:].unsqueeze(2), [R, H, DH]), op=ALU.mult), reads=nst.all + dec.all, writes=nnew.all)
                S.op("dve", lambda e: e.tensor_tensor(out=ktf.t[:].rearrange("p (h e) -> p h e", h=H), in0=ktf.t[:].rearrange("p (h e) -> p h e", h=H), in1=bcast(wsc.t[:].unsqueeze(2), [R, H, DH]), op=ALU.mult), reads=ktf.all + wsc.all, writes=ktf.all)
                S.op("dve", lambda e: e.tensor_add(out=nnew.t[:].rearrange("p h e -> p (h e)"), in0=nnew.t[:].rearrange("p h e -> p (h e)"), in1=ktf.t[:]), reads=nnew.all + ktf.all, writes=nnew.all)
                S.dma("sp", lambda e: e.dma_start(out=ons[l, :, :], in_=nnew.t[:].rearrange("p h e -> p (h e)")), reads=nnew.all)
                for half in range(2):
                    pa = psA.next()
                    for kk in range(4):
                        k = half * 4 + kk
                        S.op("pe", lambda e, pa=pa, k=k, kk=kk: e.transpose(out=pa.t[:, kk * R:(kk + 1) * R], in_=qtok.t[:, k * P:(k + 1) * P], identity=ident[0:R, 0:R]), reads=qtok.all + CST, writes=pa.all, sig=(kk == 3))
                    S.op("dve", lambda e, pa=pa, half=half: e.tensor_copy(out=qTf.t[:, half * 4:(half + 1) * 4, :], in_=pa.t[:, 0:4 * R].rearrange("p (k b) -> p k b", b=R)), reads=pa.all, writes=qTf.b[half * 4:(half + 1) * 4])
                for k in range(KC):
                    S.op("dve", lambda e, k=k: e.tensor_tensor(out=qm.t[:, k], in0=bcast(qTf.t[:, k, :].unsqueeze(1), [P, R, R]), in1=cst.t[:, 4 * P:4 * P + R * R].rearrange("p (a b) -> p a b", b=R), op=ALU.mult), reads=[qTf.b[k]] + CST, writes=[qm.b[k]])
                S.op("dve", lambda e: e.tensor_tensor(out=dexp.t[:], in0=bcast(dec.t[:].unsqueeze(1), [R, R, H]), in1=bcast(ident[0:R, 0:R].unsqueeze(2), [R, R, H]), op=ALU.mult), reads=dec.all + CST, writes=dexp.all)
                S.op("pe", lambda e: e.matmul(out=psS.t[:, 0:R * H], lhsT=ones[0:R, :], rhs=dexp.t[:].rearrange("p b h -> p (b h)"), start=True, stop=True), reads=dexp.all + CST, writes=psS.all, sig=True)
                S.op("dve", lambda e: e.tensor_copy(out=dbc.t[:], in_=psS.t[:, 0:R * H].rearrange("p (b h) -> p b h", h=H)), reads=psS.all, writes=dbc.all)
                cins = []
                psO = psOr.next()
                cis = {}

                def issue_in(bb):
                    ci_ = Cin.next()
                    S.dma("sp", lambda e: e.dma_start(out=ci_.t[:], in_=sC[l, bb].rearrange("h (k p) e -> p h k e", p=P)), writes=ci_.all)
                    cis[bb] = ci_
                issue_in(0)
                issue_in(1)
                for b in range(R):
                    if b + 2 < R:
                        issue_in(b + 2)
                    ci = cis[b]
                    if b % 4 == 0:
                        S.op("dve", lambda e, b=b: e.tensor_tensor(out=kexp.t[:], in0=bcast(ktf.t[:].unsqueeze(1), [R, 4, D]), in1=bcast(ident[0:R, b:b + 4].unsqueeze(2), [R, 4, D]), op=ALU.mult), reads=ktf.all + CST, writes=kexp.all)
                    cb16 = Cbf.next()
                    S.op("act", lambda e, ci=ci, cb16=cb16: e.activation(out=cb16.t[:].rearrange("p h k e -> p (h k e)"), in_=ci.t[:].rearrange("p h k e -> p (h k e)"), func=AF.Copy), reads=ci.all, writes=cb16.all)
                    for h in range(H):
                        for k in range(2):
                            S.op("pe", lambda e, ci=ci, b=b, h=h, k=k: e.matmul(out=psO.t[0:R, h * DH:(h + 1) * DH], lhsT=qm.t[:, h * 2 + k, b, :], rhs=cb16.t[:, h, k, :], start=(b == 0 and k == 0 and h % 2 == 0), stop=(b == R - 1 and k == 1), skip_group_check=True),
                                 reads=[qm.b[h * 2 + k]] + cb16.all, writes=psO.all, sig=(h == H - 1 and k == 1))
                    co = Cout.next()
                    for h in range(H):
                        pc = psA.next()
                        for k in range(2):
                            S.op("pe", lambda e, pc=pc, b=b, h=h, k=k: e.matmul(out=pc.t[:, k * DH:(k + 1) * DH], lhsT=kexp.t[:, b % 4, h * DH + k * P:h * DH + (k + 1) * P], rhs=vtok_v[0:R, 0, h * DH:(h + 1) * DH], start=True, stop=True),
                                 reads=kexp.all + vtok.all, writes=pc.all, sig=(k == 1))
                        S.op("dve", lambda e, pc=pc, ci=ci, co=co, b=b, h=h: e.scalar_tensor_tensor(out=co.t[:, h].rearrange("p k e -> p (k e)"), in0=ci.t[:, h].rearrange("p k e -> p (k e)"), scalar=dbc.t[:, b, h:h + 1], in1=pc.t[:, 0:2 * DH], op0=ALU.mult, op1=ALU.add),
                             reads=ci.all + pc.all + dbc.all, writes=co.all)
                    S.dma("sp", lambda e, co=co, b=b: e.dma_start(out=oCs[l, b].rearrange("h (k p) e -> p h k e", p=P), in_=co.t[:]), reads=co.all)
                S.op("dve", lambda e: e.tensor_tensor(out=numt.t[:].rearrange("p (h e) -> p h e", h=H), in0=psO.t[0:R, 0:D].rearrange("p (h e) -> p h e", h=H), in1=bcast(dec.t[:].unsqueeze(2), [R, H, DH]), op=ALU.mult), reads=psO.all + dec.all, writes=numt.all)
                S.op("dve", lambda e: e.tensor_tensor(out=prod.t[:].rearrange("p (h e) -> p h e", h=H), in0=vtok_v[0:R, 0, :].rearrange("p (h e) -> p h e", h=H), in1=bcast(qk.t[:].unsqueeze(2), [R, H, DH]), op=ALU.mult), reads=vtok.all + qk.all, writes=prod.all)
                S.op("dve", lambda e: e.tensor_add(out=numt.t[:], in0=numt.t[:], in1=prod.t[:]), reads=numt.all + prod.all, writes=numt.all)
                for h in range(H):
                    S.op("dve", lambda e, h=h: e.bn_stats(out=st6.t[:, h, :], in_=numt.t[:, h * DH:(h + 1) * DH]), reads=numt.all, writes=st6.all)
                for h in range(H):
                    S.op("dve", lambda e, h=h: e.bn_aggr(out=mv.t[:, h, :], in_=st6.t[:, h, :]), reads=st6.all, writes=mv.all)
                S.op("dve", lambda e: e.tensor_mul(out=tmp.t[:], in0=dmx.t[:], in1=dmx.t[:]), reads=dmx.all, writes=tmp.all)
                S.op("dve", lambda e: e.scalar_tensor_tensor(out=tmp.t[:], in0=tmp.t[:], scalar=EPS, in1=mv.t[:, :, 1], op0=ALU.mult, op1=ALU.add), reads=tmp.all + mv.all, writes=tmp.all)
                S.op("act", lambda e: e.activation(out=tmp.t[:], in_=tmp.t[:], func=AF.Sqrt), reads=tmp.all, writes=tmp.all)
                S.op("dve", lambda e: e.reciprocal(out=tmp.t[:], in_=tmp.t[:]), reads=tmp.all, writes=tmp.all)
                S.op("dve", lambda e: e.tensor_tensor(out=numt.t[:].rearrange("p (h e) -> p h e", h=H), in0=numt.t[:].rearrange("p (h e) -> p h e", h=H), in1=bcast(mv.t[:, :, 0:1], [R, H, DH]), op=ALU.subtract), reads=numt.all + mv.all, writes=numt.all)
                S.op("dve", lambda e: e.tensor_tensor(out=hftok.t[0:R, 0, :].rearrange("p (h e) -> p h e", h=H), in0=numt.t[:].rearrange("p (h e) -> p h e", h=H), in1=bcast(tmp.t[:].unsqueeze(2), [R, H, DH]), op=ALU.mult), reads=numt.all + tmp.all, writes=hftok.all)

            sb_wg = A([P, 3 * KC, 8], BF16)
            if sample:
                tmpS_k = A([NS, D], F32)

            for ti in tiles:
                if not sample:
                    for blk in range(N // P):
                        def wr_x(k, n, pa, blk=blk):
                            S.op("act", lambda e: e.activation(out=xT.t[:, k:k + n, blk * P:(blk + 1) * P], in_=pa.t[:, 0:n * P].rearrange("p (k t) -> p k t", t=P), func=AF.Copy), reads=pa.all, writes=xT.b[k:k + n])
                        load_rows(xp[ti * T + blk * P:ti * T + (blk + 1) * P, :], P, wr_x)
                else:
                    def wr_xs(k, n, pa):
                        S.op("act", lambda e: e.activation(out=xT.t[:, k:k + n, :], in_=pa.t[:, 0:n * NS].rearrange("p (k t) -> p k t", t=NS), func=AF.Copy), reads=pa.all, writes=xT.b[k:k + n])
                    load_rows(xs[:, :], NS, wr_xs)
                for k in range(KC):
                    rms_chunk(k)
                for l in range(DEPTH):
                    layer(l, ti)
                rms_finish()
                for k in range(KC):
                    S.op("dve", lambda e, k=k: e.scalar_tensor_tensor(out=uc32_v[:, k, :], in0=xT.t[:, k, :], scalar=vcol.t[:, 0, k, V_FIN:V_FIN + 1], in1=rstd.t[:], op0=ALU.mult, op1=ALU.mult), reads=[xT.b[k]] + rstd.all + vcol.all, writes=[uc32.b[k]])
                if not sample:
                    for blk in range(N // P):
                        store_rows(lambda k, blk=blk: uc32_v[:, k, blk * P:(blk + 1) * P], uc32.all, P, yp[ti * T + blk * P:ti * T + (blk + 1) * P, :], False)
                else:
                    store_rows(lambda k: uc32_v[:, k, :], uc32.all, NS, ys[:, :], False)

        if NS > 0:
            with contextlib.ExitStack() as st1:
                run_group(NS, True, [0], st1)
                barrier()
        with contextlib.ExitStack() as st2:
            run_group(T, False, list(range(NT)), st2)
            S.finish()
            S.emit()
    return nc


_CACHE = {}
DBG = {}


def make_consts():
    c = np.zeros((P, 4 * P + 256), np.float32)
    c[:, 4 * P:] = np.eye(16, dtype=np.float32).reshape(1, 256)
    c[:, 0:P] = np.eye(P, dtype=np.float32)
    tri = np.triu(np.ones((P, P), np.float32))
    c[:, P:2 * P] = tri
    c[:, 2 * P:3 * P] = 1.0
    c[:, 3 * P:4 * P] = -tri
    return c


def run(inputs, SEQ, DEPTH, NS, ncores=8):
    key = (SEQ, DEPTH, NS)
    if key not in _CACHE:
        _CACHE[key] = build(SEQ, DEPTH, NS)
    nc = _CACHE[key]
    f = lambda a: np.ascontiguousarray(np.asarray(a, dtype=np.float32))
    g = {k: f(v) for k, v in inputs.items()}
    vecs = np.zeros((DEPTH, NV, D), np.float32)
    vecs[:, V_NORM] = g["norm_w"]
    vecs[:, V_MCB] = g["mconv_b"]
    vecs[:, V_GN] = g["gn_w"]
    vecs[:, V_SKIP] = g["skip"]
    vecs[:, V_CCB] = g["cconv_b"]
    vecs[:, V_CLNW] = g["cln_w"]
    vecs[:, V_CLNB] = g["cln_b"]
    vecs[:, V_BADA:V_BADA + 3] = g["b_ada"].reshape(DEPTH, 3, D)
    vecs[:, V_MCW:V_MCW + MW] = g["mconv_w"]
    vecs[:, V_CCW:V_CCW + CW] = g["cconv_w"]
    vecs[:, V_FIN] = g["final_norm_w"][None, :]
    consts = make_consts()
    shared = {k: g[k] for k in ("w_ada", "w_in", "w_q", "w_k", "w_v", "w_gate", "b_gate", "w_br_m", "w_br_c", "w_out")}
    shared["vecs"] = vecs
    shared["consts"] = consts
    in_maps = []
    for i in range(ncores):
        sl = slice(i * NS, (i + 1) * NS)
        m = dict(shared)
        m["xp"] = g["x_prompt"][i]
        m["xs"] = g["x_sample"][sl, 0, :]
        m["call"] = np.concatenate([g["c_prompt"][i:i + 1], g["c_sample"][sl]], axis=0)
        m["sC"] = g["state_mlstm_C"][:, sl]
        m["sn"] = g["state_mlstm_n"][:, sl].reshape(DEPTH, NS, H * DH)
        m["sm"] = g["state_mlstm_m"][:, sl]
        m["smc"] = g["state_mlstm_conv"][:, sl].reshape(DEPTH, NS * (MW - 1), D)
        m["scv"] = g["state_conv"][:, sl].reshape(DEPTH, NS * (CW - 1), D)
        in_maps.append({k: np.ascontiguousarray(v) for k, v in m.items()})
    res = run_bass_kernel_spmd(nc, in_maps, core_ids=list(range(ncores)))
    R = res.results
    cat = lambda name, ax: np.concatenate([np.asarray(r[name]) for r in R], axis=ax)
    stk = lambda name: np.stack([np.asarray(r[name]) for r in R], axis=1)
    y_prompt = np.stack([np.asarray(r["yp"]) for r in R], axis=0)
    y_sample = cat("ys", 0).reshape(ncores * NS, 1, D)
    Cp = stk("oCp")
    np_ = stk("onp").reshape(DEPTH, ncores, H, DH)
    mp = stk("omp")
    mcp = stk("omcp")
    cvp = stk("ocvp")
    Cs = cat("oCs", 1)
    ns = cat("ons", 1).reshape(DEPTH, ncores * NS, H, DH)
    ms = cat("oms", 1)
    mcs = cat("omcs", 1).reshape(DEPTH, ncores * NS, MW - 1, D)
    cvs = cat("ocvs", 1).reshape(DEPTH, ncores * NS, CW - 1, D)
    outs = (y_prompt, y_sample, Cp, np_, mp, mcp, cvp, Cs, ns, ms, mcs, cvs)
    return tuple(np.ascontiguousarray(o, dtype=np.float32) for o in outs)


def kernel(**inputs):
    return run(inputs, SEQ=2048, DEPTH=4, NS=16)
```

```python
import contextlib
import numpy as np
import concourse.bass as bass
import concourse.mybir as mybir
from concourse.bass_utils import run_bass_kernel_spmd

F32 = mybir.dt.float32
BF16 = mybir.dt.bfloat16
AF = mybir.ActivationFunctionType
ALU = mybir.AluOpType
AX = mybir.AxisListType

P = 128
D = 1024
KC = 8
H = 4
DH = 256
T = 512
LC = 128
MW = 4
CW = 31
EPS = 1e-6
NV = 46
V_NORM, V_MCB, V_GN, V_SKIP, V_CCB, V_CLNW, V_CLNB, V_BADA, V_MCW, V_CCW, V_FIN = 0, 1, 2, 3, 4, 5, 6, 7, 10, 14, 45


class Buf:
    __slots__ = ("w", "r")

    def __init__(self):
        self.w = None
        self.r = {}


class TB:
    def __init__(self, t, nb=1):
        self.t = t
        self.b = [Buf() for _ in range(nb)]

    @property
    def all(self):
        return self.b


class Rec:
    def __getattr__(self, name):
        def f(*a, **kw):
            self.call = (name, a, kw)
            return self
        return f


def _bind(fn):
    r = Rec()
    fn(r)
    name, a, kw = r.call
    return lambda eng: getattr(eng, name)(*a, **kw)


class Sched:
    def __init__(self, nc, stack):
        self.nc = nc
        self.engs = ["pe", "act", "dve", "pool", "sp"]
        self.prog = {e: [] for e in self.engs}
        self.semh = {}
        self.cnt = {}
        for e in ["pe", "act", "dve", "pool"]:
            self.semh[e] = stack.enter_context(nc.semaphore("sem_" + e))
            self.cnt[e] = 0
        self.KD = 8
        self.dq = {}
        for q in ["sp", "pool", "act"]:
            lst = []
            for i in range(self.KD):
                key = f"d_{q}{i}"
                self.semh[key] = stack.enter_context(nc.semaphore(key))
                self.cnt[key] = 0
                lst.append(key)
            self.dq[q] = [lst, 0]
        self.known = {e: {} for e in self.engs}
        self.pe_pending = False

    def _wait(self, e, key, val):
        if self.known[e].get(key, 0) >= val:
            return
        self.known[e][key] = val
        h = self.semh[key]
        self.prog[e].append(lambda eng, h=h, val=val: eng.wait_ge(h, val))

    def _deps(self, e, reads, writes):
        deps = {}
        for b in reads:
            if b.w:
                deps[b.w[0]] = max(deps.get(b.w[0], 0), b.w[1])
        for b in writes:
            if b.w:
                deps[b.w[0]] = max(deps.get(b.w[0], 0), b.w[1])
            for k, v in b.r.items():
                deps[k] = max(deps.get(k, 0), v)
        for k, v in deps.items():
            if k == "pe" and e == "pe":
                continue
            self._wait(e, k, v)

    def _mark(self, tok, reads, writes):
        for b in writes:
            b.w = tok
            b.r = {}
        for b in reads:
            b.r[tok[0]] = max(b.r.get(tok[0], 0), tok[1])

    def op(self, e, fn, reads=(), writes=(), sig=True):
        fn = _bind(fn)
        self._deps(e, reads, writes)
        if e == "pe" and not sig:
            tick = self.cnt[e] + 1
            self.pe_pending = True
            self.prog[e].append(lambda eng, fn=fn: fn(eng))
        else:
            self.cnt[e] += 1
            tick = self.cnt[e]
            h = self.semh[e]
            if e == "pe":
                self.pe_pending = False
            self.prog[e].append(lambda eng, fn=fn, h=h: fn(eng).then_inc(h, 1))
        self._mark((e, tick), reads, writes)

    def dma(self, q, fn, reads=(), writes=()):
        fn = _bind(fn)
        lst, i = self.dq[q]
        key = lst[i % self.KD]
        self.dq[q][1] = i + 1
        if self.cnt[key] > 0:
            self._wait(q, key, self.cnt[key])
        self._deps(q, reads, writes)
        self.cnt[key] += 16
        val = self.cnt[key]
        h = self.semh[key]
        self.prog[q].append(lambda eng, fn=fn, h=h: fn(eng).then_inc(h, 16))
        self._mark((key, val), reads, writes)

    def finish(self):
        assert not self.pe_pending
        for key, c in self.cnt.items():
            if c > 0:
                self._wait("sp", key, c)

    def emit(self):
        nc = self.nc
        prog = self.prog
        with nc.Block() as block:

            @block.tensor
            def _(e):
                for f in prog["pe"]:
                    f(e)

            @block.scalar
            def _(e):
                for f in prog["act"]:
                    f(e)

            @block.vector
            def _(e):
                for f in prog["dve"]:
                    f(e)

            @block.gpsimd
            def _(e):
                for f in prog["pool"]:
                    f(e)

            @block.sync
            def _(e):
                for f in prog["sp"]:
                    f(e)


class Ring:
    def __init__(self, items):
        self.items = items
        self.i = 0

    def next(self):
        it = self.items[self.i % len(self.items)]
        self.i += 1
        return it


def bcast(ap, shape):
    return ap.broadcast_to(list(shape))


def build(SEQ, DEPTH, NS):
    nc = bass.Bass("TRN2", target_bir_lowering=False)
    NT = SEQ // T
    NTOK = 1 + NS

    def din(name, shape):
        return nc.dram_tensor(name, list(shape), F32, kind="ExternalInput").ap()

    def dout(name, shape):
        return nc.dram_tensor(name, list(shape), F32, kind="ExternalOutput").ap()

    xp = din("xp", [SEQ, D])
    xs = din("xs", [NS, D])
    call = din("call", [NTOK, D])
    sC = din("sC", [DEPTH, NS, H, DH, DH])
    sn = din("sn", [DEPTH, NS, H * DH])
    sm = din("sm", [DEPTH, NS, H])
    smc = din("smc", [DEPTH, NS * (MW - 1), D])
    scv = din("scv", [DEPTH, NS * (CW - 1), D])
    w_ada = din("w_ada", [DEPTH, D, 3 * D])
    w_in = din("w_in", [DEPTH, D, 7 * D])
    w_q = din("w_q", [DEPTH, H, DH, DH])
    w_k = din("w_k", [DEPTH, H, DH, DH])
    w_v = din("w_v", [DEPTH, H, DH, DH])
    w_gate = din("w_gate", [DEPTH, 3 * D, 8])
    b_gate = din("b_gate", [DEPTH, 8])
    w_br_m = din("w_br_m", [DEPTH, D, D])
    w_br_c = din("w_br_c", [DEPTH, D, D])
    w_out = din("w_out", [DEPTH, D, D])
    vecs = din("vecs", [DEPTH, NV, D])
    consts = din("consts", [P, 4 * P + 256])

    yp = dout("yp", [SEQ, D])
    ys = dout("ys", [NS, D])
    oCp = dout("oCp", [DEPTH, H, DH, DH])
    onp = dout("onp", [DEPTH, H * DH])
    omp = dout("omp", [DEPTH, H])
    omcp = dout("omcp", [DEPTH, MW - 1, D])
    ocvp = dout("ocvp", [DEPTH, CW - 1, D])
    oCs = dout("oCs", [DEPTH, NS, H, DH, DH])
    ons = dout("ons", [DEPTH, NS, H * DH])
    oms = dout("oms", [DEPTH, NS, H])
    omcs = dout("omcs", [DEPTH, NS * (MW - 1), D])
    ocvs = dout("ocvs", [DEPTH, NS * (CW - 1), D])

    dgscr = nc.dram_tensor("dgscr", [DEPTH, KC, P, CW * P], BF16).ap()
    stack = contextlib.ExitStack()
    with stack:
        S = Sched(nc, stack)
        uid = [0]

        def barrier():
            snap = dict(S.cnt)
            for e in S.engs:
                for key, c in snap.items():
                    if c > 0 and key != e:
                        S._wait(e, key, c)


        pst = contextlib.ExitStack()

        def sb(shape, dt, nb=1, st=stack):
            uid[0] += 1
            t = st.enter_context(nc.sbuf_tensor(f"t{uid[0]}", list(shape), dt))
            return TB(t, nb)

        def ps(shape, dt):
            uid[0] += 1
            t = stack.enter_context(nc.psum_tensor(f"p{uid[0]}", list(shape), dt))
            return TB(t, 1)

        psA = Ring([ps([P, 512], F32) for _ in range(2)])
        _po = [ps([P, 1024], F32) for _ in range(2)]
        for _t in _po:
            _t.b = [Buf(), Buf()]
        psOr = Ring(_po)
        _halves = []
        for _t in _po:
            for _i in range(2):
                _h = TB(_t.t[:, _i * 512:(_i + 1) * 512], 1)
                _h.b = [_t.b[_i]]
                _halves.append(_h)
        psA6 = Ring(list(psA.items) + _halves)
        psT = ps([P, 1024], BF16)
        psS = ps([P, 512], F32)

        cst = sb([P, 4 * P + 256], F32)
        cbf = sb([P, 3 * P], BF16)
        epsc = sb([P, 1], F32)
        vcol = sb([P, DEPTH, KC, NV], F32)
        NWB = 4
        wring = Ring([sb([P, KC, 256], BF16) for _ in range(NWB)])
        modall = sb([P, DEPTH, 24, NTOK], F32)
        Acol = sb([P, DEPTH, KC, NTOK], F32)
        n32 = sb([P, DEPTH, H, 2], F32, nb=DEPTH)
        mst = sb([H, DEPTH], F32, nb=DEPTH)
        hxm = sb([P, DEPTH, KC, MW - 1], BF16, nb=DEPTH)
        hu = sb([P, DEPTH, KC, CW - 1], BF16, nb=DEPTH)
        vstg = sb([NV, D], F32, 1, pst)
        cstg = sb([NTOK, D], F32, 1, pst)
        csl = sb([NTOK, D], F32, 1, pst)
        scT = sb([P, KC, NTOK], BF16, KC, pst)
        S.dma("sp", lambda e: e.dma_start(out=cst.t[:], in_=consts[:, :]), writes=cst.all)
        ident = cst.t[:, 0:P]
        triu = cst.t[:, P:2 * P]
        ones = cst.t[:, 2 * P:3 * P]
        ntri = cst.t[:, 3 * P:4 * P]
        identb = cbf.t[:, 0:P]
        maskb = cbf.t[:, P:2 * P]
        S.op("dve", lambda e: e.tensor_copy(out=cbf.t[:], in_=cst.t[:, 0:3 * P]), reads=cst.all, writes=cbf.all)
        onesb = cbf.t[:, 2 * P:3 * P]
        CST = cst.all + cbf.all
        S.op("dve", lambda e: e.memset(epsc.t[:], EPS), writes=epsc.all)

        for l in range(DEPTH):
            S.dma("sp", lambda e, l=l: e.dma_start(out=vstg.t[:], in_=vecs[l, :, :]), writes=vstg.all)
            for half in range(2):
                pa = psA.next()
                for kk in range(4):
                    k = half * 4 + kk
                    S.op("pe", lambda e, pa=pa, k=k, kk=kk: e.transpose(out=pa.t[:, kk * NV:(kk + 1) * NV], in_=vstg.t[:, k * P:(k + 1) * P], identity=ident[0:NV, 0:NV]),
                         reads=vstg.all + CST, writes=pa.all, sig=(kk == 3))
                S.op("dve", lambda e, pa=pa, l=l, half=half: e.tensor_copy(out=vcol.t[:, l, half * 4:(half + 1) * 4, :], in_=pa.t[:, 0:4 * NV].rearrange("p (k v) -> p k v", v=NV)),
                     reads=pa.all, writes=vcol.all)

        def vc(l, k, v):
            return vcol.t[:, l, k, v:v + 1]


        def wload(src3, kk, cols):
            w = wring.next()
            S.dma("pool", lambda e, w=w: e.dma_start(out=w.t[:, 0:kk, 0:cols], in_=src3), writes=w.all)
            return w

        def wmat(wd, f0, cols):
            return wload(wd.rearrange("(k p) f -> p k f", p=P)[:, :, f0:f0 + cols], KC, cols)

        def proj(wd, f0, nfc, src, N, evac, src_cols=None):
            for _ in proj_gen(wd, f0, nfc, src, N, evac, src_cols, ring=psA6):
                pass

        def proj_gen(wd, f0, nfc, src, N, evac, src_cols=None, ring=None):
            ring = ring or psA
            done = 0
            while done < nfc:
                n = min(2, nfc - done)
                w = wmat(wd, f0 + done * P, n * P)
                for c in range(n):
                    pa = ring.next()
                    for k in range(KC):
                        rhs = src.t[:, k, 0:N] if src_cols is None else src.t[:, k, src_cols[0]:src_cols[1]]
                        S.op("pe", lambda e, pa=pa, w=w, c=c, k=k, rhs=rhs: e.matmul(out=pa.t[:, 0:N], lhsT=w.t[:, k, c * P:(c + 1) * P], rhs=rhs, start=(k == 0), stop=(k == KC - 1)),
                             reads=w.all + [src.b[k]], writes=pa.all, sig=(k == KC - 1))
                    evac(done + c, pa)
                done += n
                yield

        S.dma("sp", lambda e: e.dma_start(out=cstg.t[:], in_=call[:, :]), writes=cstg.all)
        S.op("act", lambda e: e.activation(out=csl.t[:], in_=cstg.t[:], func=AF.Silu), reads=cstg.all, writes=csl.all)
        for half in range(2):
            pa = psA.next()
            for kk in range(4):
                k = half * 4 + kk
                S.op("pe", lambda e, pa=pa, k=k, kk=kk: e.transpose(out=pa.t[:, kk * NTOK:(kk + 1) * NTOK], in_=csl.t[:, k * P:(k + 1) * P], identity=ident[0:NTOK, 0:NTOK]),
                     reads=csl.all + CST, writes=pa.all, sig=(kk == 3))
            S.op("dve", lambda e, pa=pa, half=half: e.tensor_copy(out=scT.t[:, half * 4:(half + 1) * 4, :], in_=pa.t[:, 0:4 * NTOK].rearrange("p (k v) -> p k v", v=NTOK)),
                 reads=pa.all, writes=scT.b[half * 4:(half + 1) * 4])
        for l in range(DEPTH):
            def ev(fc, pa, l=l):
                S.op("act", lambda e: e.activation(out=modall.t[:, l, fc, :], in_=pa.t[:, 0:NTOK], func=AF.Identity, bias=vc(l, fc % KC, V_BADA + fc // KC), scale=1.0),
                     reads=pa.all + vcol.all, writes=modall.all)
            proj(w_ada[l], 0, 24, scT, NTOK, ev)
            S.op("dve", lambda e, l=l: e.scalar_tensor_tensor(out=Acol.t[:, l, :, :], in0=modall.t[:, l, 8:16, :], scalar=1.0, in1=bcast(vcol.t[:, l, :, V_NORM:V_NORM + 1], [P, KC, NTOK]), op0=ALU.add, op1=ALU.mult),
                 reads=modall.all + vcol.all, writes=Acol.all)
        MOD = modall.all + Acol.all + vcol.all
        barrier()
        pst.close()
        dC = [Buf() for _ in range(DEPTH)]
        dgB = [[Buf() for _ in range(KC)] for _ in range(DEPTH)]
        use_scr = NS > 0

        for tb in (n32, mst, hxm, hu):
            S.op("dve", lambda e, tb=tb: e.memset(tb.t[:], 0.0), writes=tb.all)

        def run_group(N, sample, tiles, st):
            def A(shape, dt, nb=1):
                return sb(shape, dt, nb, st)

            xT = A([P, KC, N], F32, KC)
            hT = A([P, KC, N], BF16, KC)
            XU = A([P, KC, CW - 1 + N], BF16, KC)
            xm = TB(XU.t, 1); xm.b = XU.b
            uu = XU
            XO = CW - MW
            szm = A([P, KC, N], BF16, KC)
            sgm = A([P, KC, N], BF16, KC)
            xc = A([P, KC, N], BF16, KC)
            qT = A([P, KC, N], BF16, KC)
            kT = A([P, KC, N], BF16, KC)
            vT = A([P, 2, N], BF16, 1)
            NB = (N + LC - 1) // LC
            RB = LC if not sample else NS
            ktok = A([LC, NB, D], BF16, NB)
            hftok = ktok
            szc = szm
            sgc = sgm
            ymg = xc
            if not sample:
                big = A([P, KC * N * 2], BF16, KC)
                vtok = TB(None, 1); vtok.b = big.b[0:NB]
                vtok_v = big.t[0:LC, 0:NB * D].rearrange("p (b d) -> p b d", d=D)
                uc32 = TB(None, 1); uc32.b = big.b
                uc32_v = big.t[:].bitcast(F32).rearrange("p (k n) -> p k n", n=N)
            else:
                vtok = A([LC, NB, D], BF16, NB)
                vtok_v = vtok.t[:]
                uc32 = A([P, KC, N], F32, KC)
                uc32_v = uc32.t[:]
            hmg = qT
            sgb = qT if sample else A([P, KC, N], BF16, KC)
            ucg = kT
            mrg = hT
            tmpf = Ring([A([P, N], F32) for _ in range(4)])
            sq = tmpf
            tmpb = Ring([A([P, N], BF16) for _ in range(2)])
            sqb = Ring([A([P, N], BF16) for _ in range(2)])
            rstd = A([P, N], F32)
            meanT = A([P, N], F32)
            dgm = Ring([A([P, MW, P], BF16) for _ in range(2)])
            dgc = Ring([A([P, CW, P], BF16) for _ in range(2)])
            _stg = [A([P, D], F32) for _ in range(1 if not sample else 2)]
            if not sample:
                for _i in range(2):
                    _v = TB(szm.t[:].rearrange("p k n -> p (k n)").bitcast(F32)[:, _i * D:(_i + 1) * D], 1)
                    _v.b = szm.b[_i * 4:(_i + 1) * 4]
                    _stg.append(_v)
            stg = Ring(_stg[:1] if not sample else _stg)
            stg_io = Ring(_stg)
            gsb = A([LC, NB, 8], F32)
            bgt = A([LC, 8], F32)
            gt = [A([LC, NB, 4], F32) for _ in range(4)]
            if not sample:
                C32 = A([P, H, 2, DH], F32, H)
                Cb = A([P, H, 2, DH], BF16, H)
                nb_ = A([P, H, 2], BF16)
                rowA = A([H, N], F32)
                rowF = A([H, N], F32)
                sm8 = [A([H, NB], F32) for _ in range(6)]
                adg = A([H, NB, H], F32)
                abc = A([P, NB, H], F32)
                ctk = A([LC, NB, 8], F32)
                cbk = A([LC, NB, H], BF16)
                Sm = Ring([A([LC, H, LC], BF16) for _ in range(NB)])
                st6 = A([LC, H, 6], F32)
                mv = A([LC, H, 2], F32)
                smls = [[A([LC, H], F32) for _ in range(5)] for _ in range(2)]
                st6s = [A([LC, H, 6], F32) for _ in range(2)]
                mvs = [A([LC, H, 2], F32) for _ in range(2)]
            else:
                hxmS = A([P, KC, MW - 1, NS], BF16, KC)
                huS = A([P, KC, CW - 1, NS], BF16, KC)
                qtok = A([NS, D], F32)
                qTf = A([P, KC, NS], F32, KC)
                qm = A([P, KC, NS, NS], BF16, KC)
                kexp = A([NS, 4, D], BF16)
                Cin = Ring([A([P, H, 2, DH], F32) for _ in range(3)])
                Cbf = Ring([A([P, H, 2, DH], BF16) for _ in range(2)])
                Cout = Ring([A([P, H, 2, DH], F32) for _ in range(2)])
                nst = A([NS, H, DH], F32)
                nnew = A([NS, H, DH], F32)
                mprev = A([NS, H], F32)
                s16 = [A([NS, H], F32) for _ in range(10)]
                dexp = A([NS, NS, H], F32)
                dbc = A([P, NS, H], F32)
                prod = A([NS, D], F32)
                numt = A([NS, D], F32)
                st6 = A([NS, H, 6], F32)
                mv = A([NS, H, 2], F32)

            for _nm, _val in list(locals().items()):
                if isinstance(_val, TB) and _val.t is not None:
                    DBG[(sample, _nm)] = _val.t.name

            def modv(l, which, k):
                if not sample:
                    return modall.t[:, l, which * 8 + k, 0:1]
                return modall.t[:, l, which * 8 + k, 1:NTOK]

            def Av(l, k):
                if not sample:
                    return Acol.t[:, l, k, 0:1]
                return Acol.t[:, l, k, 1:NTOK]

            rms_pending = []

            def rms_flush():
                while rms_pending:
                    k, s_ = rms_pending.pop(0)
                    S.op("pe", lambda e: e.matmul(out=psS.t[:, 0:N], lhsT=onesb, rhs=s_.t[:], start=(k == 0), stop=(k == KC - 1)),
                         reads=s_.all + CST, writes=psS.all, sig=True)

            def rms_chunk(k, defer=False):
                rms_flush()
                s_ = sqb.next()
                S.op("act", lambda e: e.activation(out=s_.t[:], in_=xT.t[:, k, :], func=AF.Square), reads=[xT.b[k]], writes=s_.all)
                rms_pending.append((k, s_))
                if not defer:
                    rms_flush()

            def rms_finish():
                S.op("act", lambda e: e.activation(out=rstd.t[:], in_=psS.t[:, 0:N], func=AF.Sqrt, bias=epsc.t[:, 0:1], scale=1.0 / D), reads=psS.all + epsc.all, writes=rstd.all)
                S.op("dve", lambda e: e.reciprocal(out=rstd.t[:], in_=rstd.t[:]), reads=rstd.all, writes=rstd.all)

            def build_diag(l, W, vbase, ring, k):
                dg = ring.next()
                if W == CW and use_scr and not sample:
                    S.dma("sp", lambda e: e.dma_start(out=dg.t[:].rearrange("p j m -> p (j m)"), in_=dgscr[l, k]), reads=[dgB[l][k]], writes=dg.all)
                    return dg
                S.op("dve", lambda e: e.tensor_tensor(out=dg.t[:], in0=bcast(identb.unsqueeze(1), [P, W, P]), in1=bcast(vcol.t[:, l, k, vbase:vbase + W].unsqueeze(2), [P, W, P]), op=ALU.mult),
                     reads=CST + vcol.all, writes=dg.all)
                if W == CW and use_scr and sample:
                    S.dma("sp", lambda e: e.dma_start(out=dgscr[l, k], in_=dg.t[:].rearrange("p j m -> p (j m)")), reads=dg.all, writes=[dgB[l][k]])
                return dg

            def dwconv(l, W, vbase, ring, src_of, evac, pre=()):
                for k in range(KC):
                    dg = pre[k] if k < len(pre) else build_diag(l, W, vbase, ring, k)
                    pa = psA6.next()
                    for j in range(W):
                        rhs, rb = src_of(k, j)
                        S.op("pe", lambda e, dg=dg, j=j, pa=pa, rhs=rhs: e.matmul(out=pa.t[:, 0:N], lhsT=dg.t[:, j, :], rhs=rhs, start=(j == 0), stop=(j == W - 1)),
                             reads=dg.all + rb, writes=pa.all, sig=(j == W - 1))
                    evac(k, pa)

            def store_rows(src_ap_of_k, src_bufs, R, dram_rows, bf, ring=None):
                s_ = (ring or stg).next()
                for half in range(2):
                    if bf:
                        po = psT
                    else:
                        po = psA.next()
                    for kk in range(4):
                        k = half * 4 + kk
                        S.op("pe", lambda e, po=po, k=k, kk=kk: e.transpose(out=po.t[0:R, kk * P:(kk + 1) * P], in_=src_ap_of_k(k), identity=(identb if bf else ident)),
                             reads=src_bufs + CST, writes=po.all, sig=(kk == 3))
                    S.op("dve", lambda e, po=po, half=half, s_=s_: e.tensor_copy(out=s_.t[0:R, half * 512:(half + 1) * 512], in_=po.t[0:R, 0:512]), reads=po.all, writes=s_.all)
                S.dma("sp", lambda e, s_=s_: e.dma_start(out=dram_rows, in_=s_.t[0:R, :]), reads=s_.all)

            def load_rows(dram_rows, R, writer, ring=None):
                s_ = (ring or stg).next()
                S.dma("sp", lambda e, s_=s_: e.dma_start(out=s_.t[0:R, :], in_=dram_rows), writes=s_.all)
                nper = max(1, min(4, 512 // R))
                k = 0
                while k < KC:
                    n = min(nper, KC - k)
                    pa = psA.next()
                    for kk in range(n):
                        S.op("pe", lambda e, pa=pa, k=k, kk=kk, s_=s_: e.transpose(out=pa.t[:, kk * R:(kk + 1) * R], in_=s_.t[0:R, (k + kk) * P:(k + kk + 1) * P], identity=ident[0:R, 0:R]),
                             reads=s_.all + CST, writes=pa.all, sig=(kk == n - 1))
                    writer(k, n, pa)
                    k += n

            def layer(l, ti):
                last_tile = (ti == NT - 1)
                rms_finish()
                for k in range(KC):
                    tf = tmpf.next()
                    if not sample:
                        S.op("dve", lambda e, tf=tf, k=k: e.scalar_tensor_tensor(out=tf.t[:], in0=xT.t[:, k, :], scalar=Av(l, k), in1=rstd.t[:], op0=ALU.mult, op1=ALU.mult),
                             reads=[xT.b[k]] + rstd.all + MOD, writes=tf.all)
                        S.op("act", lambda e, tf=tf, k=k: e.activation(out=hT.t[:, k, :], in_=tf.t[:], func=AF.Identity, bias=modv(l, 0, k), scale=1.0),
                             reads=tf.all + MOD, writes=[hT.b[k]])
                    else:
                        S.op("dve", lambda e, tf=tf, k=k: e.tensor_tensor(out=tf.t[:], in0=xT.t[:, k, :], in1=rstd.t[:], op=ALU.mult), reads=[xT.b[k]] + rstd.all, writes=tf.all)
                        S.op("dve", lambda e, tf=tf, k=k: e.tensor_tensor(out=tf.t[:], in0=tf.t[:], in1=Av(l, k), op=ALU.mult), reads=tf.all + MOD, writes=tf.all)
                        S.op("dve", lambda e, tf=tf, k=k: e.tensor_tensor(out=hT.t[:, k, :], in0=tf.t[:], in1=modv(l, 0, k), op=ALU.add), reads=tf.all + MOD, writes=[hT.b[k]])

                if not sample:
                    S.op("dve", lambda e: e.tensor_copy(out=XU.t[:, :, XO:XO + MW - 1], in_=hxm.t[:, l, :, :]), reads=[hxm.b[l]], writes=xm.all)
                else:
                    def wr_m(k, n, pa):
                        S.op("dve", lambda e: e.tensor_copy(out=hxmS.t[:, k:k + n, :, :].rearrange("p k j b -> p k b j"), in_=pa.t[:, 0:n * NS * (MW - 1)].rearrange("p (k b j) -> p k b j", k=n, b=NS)),
                             reads=pa.all, writes=hxmS.b[k:k + n])
                    load_rows(smc[l, :, :], NS * (MW - 1), wr_m)
                    SB4 = 4
                    for b0 in range(0, NS, SB4):
                        def wr_c(k, n, pa, b0=b0):
                            S.op("dve", lambda e: e.tensor_copy(out=huS.t[:, k:k + n, :, b0:b0 + SB4].rearrange("p k j b -> p k b j"), in_=pa.t[:, 0:n * SB4 * (CW - 1)].rearrange("p (k b j) -> p k b j", k=n, b=SB4)),
                                 reads=pa.all, writes=huS.b[k:k + n])
                        load_rows(scv[l, b0 * (CW - 1):(b0 + SB4) * (CW - 1), :], SB4 * (CW - 1), wr_c)
                    S.dma("sp", lambda e: e.dma_start(out=omcs[l].rearrange("(b j) d -> b j d", j=MW - 1)[:, 0:MW - 2, :], in_=smc[l].rearrange("(b j) d -> b j d", j=MW - 1)[:, 1:MW - 1, :]))
                    S.dma("sp", lambda e: e.dma_start(out=ocvs[l].rearrange("(b j) d -> b j d", j=CW - 1)[:, 0:CW - 2, :], in_=scv[l].rearrange("(b j) d -> b j d", j=CW - 1)[:, 1:CW - 1, :]))

                def ev_xm(fc, pa):
                    S.op("dve", lambda e: e.tensor_copy(out=XU.t[:, fc, CW - 1:CW - 1 + N], in_=pa.t[:, 0:N]), reads=pa.all, writes=[xm.b[fc]])
                proj(w_in[l], 0 * D, KC, hT, N, ev_xm)
                pre_c = [build_diag(l, CW, V_CCW, dgc, 0), build_diag(l, CW, V_CCW, dgc, 1)]

                def ev_zm(fc, pa):
                    S.op("act", lambda e: e.activation(out=szm.t[:, fc, :], in_=pa.t[:, 0:N], func=AF.Silu), reads=pa.all, writes=[szm.b[fc]])

                def src_m(k, j):
                    if not sample:
                        return XU.t[:, k, XO + j:XO + j + N], [xm.b[k]]
                    if j < MW - 1:
                        return hxmS.t[:, k, j, :], [hxmS.b[k]]
                    return XU.t[:, k, CW - 1:CW - 1 + N], [xm.b[k]]

                def ev_mc(k, pa):
                    S.op("act", lambda e: e.activation(out=xc.t[:, k, :], in_=pa.t[:, 0:N], func=AF.Silu, bias=vc(l, k, V_MCB), scale=1.0), reads=pa.all + vcol.all, writes=[xc.b[k]])
                dwconv(l, MW, V_MCW, dgm, src_m, ev_mc)

                def ev_gm(fc, pa):
                    S.op("act", lambda e: e.activation(out=sgm.t[:, fc, :], in_=pa.t[:, 0:N], func=AF.Sigmoid), reads=pa.all, writes=[sgm.b[fc]])

                if not sample:
                    S.op("dve", lambda e: e.tensor_copy(out=hxm.t[:, l, :, :], in_=XU.t[:, :, XO + N:XO + N + MW - 1]), reads=xm.all, writes=[hxm.b[l]])
                    if last_tile:
                        store_rows(lambda k: hxm.t[:, l, k, :], [hxm.b[l]], MW - 1, omcp[l, :, :], True)
                else:
                    store_rows(lambda k: XU.t[:, k, CW - 1:CW - 1 + N], xm.all, NS, omcs[l].rearrange("(b j) d -> b j d", j=MW - 1)[:, MW - 2, :], True)

                wq = wload(w_q[l].rearrange("h (k p) e -> p (h k) e", p=P), KC, DH)
                wk = wload(w_k[l].rearrange("h (k p) e -> p (h k) e", p=P), KC, DH)
                wv = wload(w_v[l].rearrange("h (k p) e -> p (h k) e", p=P), KC, DH)
                wg = sb_wg
                S.dma("pool", lambda e: e.dma_start(out=wg.t[:], in_=w_gate[l].rearrange("(c p) g -> p c g", p=P)), writes=wg.all)
                S.dma("sp", lambda e: e.dma_start(out=bgt.t[:], in_=b_gate[l:l + 1, :].partition_broadcast(LC)), writes=bgt.all)

                def fm_proj(w, src, src_off, dst, scale, h):
                    for ec in range(2):
                        pa = psA6.next()
                        for k in range(2):
                            S.op("pe", lambda e, pa=pa, k=k, ec=ec: e.matmul(out=pa.t[:, 0:N], lhsT=w.t[:, h * 2 + k, ec * P:(ec + 1) * P], rhs=src.t[:, h * 2 + k, src_off:src_off + N], start=(k == 0), stop=(k == 1)),
                                 reads=w.all + [src.b[h * 2 + k]], writes=pa.all, sig=(k == 1))
                        if dst is vT:
                            S.op("act", lambda e, pa=pa, ec=ec: e.activation(out=vT.t[:, ec, :], in_=pa.t[:, 0:N], func=AF.Copy, scale=scale), reads=pa.all, writes=vT.all)
                        elif dst is kT:
                            S.op("dve", lambda e, pa=pa, ec=ec: e.tensor_scalar(out=dst.t[:, h * 2 + ec, :], in0=pa.t[:, 0:N], scalar1=scale, scalar2=None, op0=ALU.mult), reads=pa.all, writes=[dst.b[h * 2 + ec]])
                        else:
                            S.op("act", lambda e, pa=pa, ec=ec: e.activation(out=dst.t[:, h * 2 + ec, :], in_=pa.t[:, 0:N], func=AF.Copy, scale=scale), reads=pa.all, writes=[dst.b[h * 2 + ec]])

                def gate_mm(src_ap, srcb, cidx, first, last):
                    for blk in range(NB):
                        S.op("pe", lambda e, blk=blk: e.matmul(out=psS.t[0:RB, blk * 8:(blk + 1) * 8], lhsT=src_ap(blk), rhs=wg.t[:, cidx, :], start=(first and blk == 0), stop=last, skip_group_check=True),
                             reads=srcb + wg.all, writes=psS.all, sig=(blk == NB - 1))

                for h in range(H):
                    fm_proj(wq, xc, 0, qT, 1.0, h)
                    fm_proj(wk, xc, 0, kT, DH ** -0.5, h)
                    fm_proj(wv, xm, CW - 1, vT, 1.0, h)
                    for blk in range(NB):
                        for (w, src, off, dst, scale, eng) in ((wk, xc, 0, ktok, DH ** -0.5, "act"), (wv, xm, CW - 1, vtok, 1.0, "dve")) + (((wq, xc, 0, None, 1.0, "dve"),) if sample else ()):
                            pa = psA6.next()
                            for k in range(2):
                                S.op("pe", lambda e, pa=pa, k=k, w=w, src=src, off=off, blk=blk: e.matmul(out=pa.t[0:RB, 0:DH], lhsT=src.t[:, h * 2 + k, off + blk * LC:off + blk * LC + RB], rhs=w.t[:, h * 2 + k, 0:DH], start=(k == 0), stop=(k == 1)),
                                     reads=w.all + [src.b[h * 2 + k]], writes=pa.all, sig=(k == 1))
                            if dst is None:
                                S.op("dve", lambda e, pa=pa: e.tensor_copy(out=qtok.t[:, h * DH:(h + 1) * DH], in_=pa.t[0:RB, 0:DH]), reads=pa.all, writes=qtok.all)
                            elif eng == "act":
                                S.op("act", lambda e, pa=pa, dst=dst, blk=blk, scale=scale: e.activation(out=dst.t[0:RB, blk, h * DH:(h + 1) * DH], in_=pa.t[0:RB, 0:DH], func=AF.Copy, scale=scale), reads=pa.all, writes=[dst.b[blk]])
                            else:
                                S.op("dve", lambda e, pa=pa, dst=dst, blk=blk: e.tensor_copy(out=vtok_v[0:RB, blk, h * DH:(h + 1) * DH], in_=pa.t[0:RB, 0:DH]), reads=pa.all, writes=[dst.b[blk]])
                    for ec in range(2):
                        gate_mm(lambda blk, ec=ec: qT.t[:, h * 2 + ec, blk * LC:blk * LC + RB], [qT.b[h * 2 + ec]], h * 2 + ec, (h == 0 and ec == 0), False)
                        gate_mm(lambda blk, ec=ec: kT.t[:, h * 2 + ec, blk * LC:blk * LC + RB], [kT.b[h * 2 + ec]], KC + h * 2 + ec, False, False)
                    for ec in range(2):
                        gate_mm(lambda blk, ec=ec: vT.t[:, ec, blk * LC:blk * LC + RB], vT.all, 2 * KC + h * 2 + ec, False, (h == H - 1 and ec == 1))

                gi, lf, t1, t2 = gt
                S.op("dve", lambda e: e.tensor_tensor(out=gsb.t[0:RB], in0=psS.t[0:RB, 0:NB * 8].rearrange("p (b g) -> p b g", g=8), in1=bcast(bgt.t[0:RB].unsqueeze(1), [RB, NB, 8]), op=ALU.add),
                     reads=psS.all + bgt.all, writes=gsb.all)
                S.op("act", lambda e: e.activation(out=t1.t[0:RB], in_=gsb.t[0:RB, :, 4:8], func=AF.Abs), reads=gsb.all, writes=t1.all)
                S.op("act", lambda e: e.activation(out=t1.t[0:RB], in_=t1.t[0:RB], func=AF.Exp, scale=-1.0), reads=t1.all, writes=t1.all)
                S.op("act", lambda e: e.activation(out=t1.t[0:RB], in_=t1.t[0:RB], func=AF.Ln, bias=1.0, scale=1.0), reads=t1.all, writes=t1.all)
                S.op("dve", lambda e: e.tensor_scalar_min(out=t2.t[0:RB], in0=gsb.t[0:RB, :, 4:8], scalar1=0.0), reads=gsb.all, writes=t2.all)
                S.op("dve", lambda e: e.tensor_sub(out=lf.t[0:RB], in0=t2.t[0:RB], in1=t1.t[0:RB]), reads=t1.all + t2.all, writes=lf.all)
                S.op("dve", lambda e: e.tensor_copy(out=gi.t[0:RB], in_=gsb.t[0:RB, :, 0:4]), reads=gsb.all, writes=gi.all)

                def ev_gb(fc, pa):
                    S.op("act", lambda e: e.activation(out=sgb.t[:, fc, :], in_=pa.t[:, 0:N], func=AF.Sigmoid), reads=pa.all, writes=[sgb.b[fc]])

                def ev_ga(fc, pa):
                    S.op("dve", lambda e: e.tensor_tensor(out=uu.t[:, fc, CW - 1:CW - 1 + N], in0=pa.t[:, 0:N], in1=sgb.t[:, fc, :], op=ALU.mult), reads=pa.all + [sgb.b[fc]], writes=[uu.b[fc]])

                if not sample:
                    S.op("dve", lambda e: e.tensor_copy(out=uu.t[:, :, 0:CW - 1], in_=hu.t[:, l, :, :]), reads=[hu.b[l]], writes=uu.all)

                    def fill():
                        yield from proj_gen(w_in[l], 3 * D, KC, hT, N, ev_gb)
                        yield from proj_gen(w_in[l], 2 * D, KC, hT, N, ev_ga)

                    def fill2():
                        yield from proj_gen(w_in[l], 1 * D, KC, hT, N, ev_zm, ring=psA6)
                        yield from proj_gen(w_in[l], 5 * D, KC, hT, N, ev_gm, ring=psA6)
                    filler = fill()
                    filler2 = fill2()
                    mlstm_prompt(l, ti, gi, lf, filler, filler2)
                    for _ in filler:
                        pass
                else:
                    proj(w_in[l], 1 * D, KC, hT, N, ev_zm)
                    proj(w_in[l], 5 * D, KC, hT, N, ev_gm)
                    mlstm_sample(l, gi, lf)

                if sample:
                    proj(w_in[l], 3 * D, KC, hT, N, ev_gb)
                    proj(w_in[l], 2 * D, KC, hT, N, ev_ga)
                def src_c(k, j):
                    if not sample:
                        return uu.t[:, k, j:j + N], [uu.b[k]]
                    if j < CW - 1:
                        return huS.t[:, k, j, :], [huS.b[k]]
                    return uu.t[:, k, CW - 1:CW - 1 + N], [uu.b[k]]

                def ev_cc(k, pa):
                    S.op("act", lambda e: e.activation(out=uc32_v[:, k, :], in_=pa.t[:, 0:N], func=AF.Identity, bias=vc(l, k, V_CCB), scale=1.0), reads=pa.all + vcol.all, writes=[uc32.b[k]])
                dwconv(l, CW, V_CCW, dgc, src_c, ev_cc, pre_c)

                for fc in range(KC):
                    for blk in range(NB):
                        S.op("pe", lambda e, fc=fc, blk=blk: e.transpose(out=psT.t[:, blk * LC:blk * LC + RB], in_=hftok.t[0:RB, blk, fc * P:(fc + 1) * P], identity=identb[0:RB, 0:RB]),
                             reads=[hftok.b[blk]] + CST, writes=psT.all, sig=(blk == NB - 1))
                    tf = sqb.next()
                    tb_ = tmpb.next()
                    S.op("dve", lambda e, fc=fc, tf=tf: e.scalar_tensor_tensor(out=tf.t[:], in0=psT.t[:, 0:N], scalar=vc(l, fc, V_GN), in1=szm.t[:, fc, :], op0=ALU.mult, op1=ALU.mult),
                         reads=psT.all + [szm.b[fc]] + vcol.all, writes=tf.all)
                    S.op("dve", lambda e, fc=fc, tb_=tb_: e.scalar_tensor_tensor(out=tb_.t[:], in0=xc.t[:, fc, :], scalar=vc(l, fc, V_SKIP), in1=szm.t[:, fc, :], op0=ALU.mult, op1=ALU.mult),
                         reads=[xc.b[fc], szm.b[fc]] + vcol.all, writes=tb_.all)
                    S.op("dve", lambda e, fc=fc, tf=tf, tb_=tb_: e.tensor_tensor(out=hmg.t[:, fc, :], in0=tf.t[:], in1=tb_.t[:], op=ALU.add), reads=tf.all + tb_.all, writes=[hmg.b[fc]])

                def ev_ym(fc, pa):
                    S.op("dve", lambda e: e.tensor_tensor(out=ymg.t[:, fc, :], in0=pa.t[:, 0:N], in1=sgm.t[:, fc, :], op=ALU.mult), reads=pa.all + [sgm.b[fc]], writes=[ymg.b[fc]])
                proj(w_br_m[l], 0, KC, hmg, N, ev_ym)


                def ev_gc(fc, pa):
                    S.op("act", lambda e: e.activation(out=sgc.t[:, fc, :], in_=pa.t[:, 0:N], func=AF.Sigmoid), reads=pa.all, writes=[sgc.b[fc]])


                def ev_zc(fc, pa):
                    S.op("act", lambda e: e.activation(out=szc.t[:, fc, :], in_=pa.t[:, 0:N], func=AF.Silu), reads=pa.all, writes=[szc.b[fc]])

                if not sample:
                    S.op("dve", lambda e: e.tensor_copy(out=hu.t[:, l, :, :], in_=uu.t[:, :, N:N + CW - 1]), reads=uu.all, writes=[hu.b[l]])
                    if last_tile:
                        store_rows(lambda k: hu.t[:, l, k, :], [hu.b[l]], CW - 1, ocvp[l, :, :], True)
                else:
                    store_rows(lambda k: uu.t[:, k, CW - 1:CW - 1 + N], uu.all, NS, ocvs[l].rearrange("(b j) d -> b j d", j=CW - 1)[:, CW - 2, :], True)

                pm = psA.next()
                pq = psA.next()
                for k in range(KC):
                    S.op("pe", lambda e, k=k: e.matmul(out=pm.t[:, 0:N], lhsT=ones, rhs=uc32_v[:, k, :], start=(k == 0), stop=(k == KC - 1)), reads=[uc32.b[k]] + CST, writes=pm.all, sig=(k == KC - 1))
                for k in range(KC):
                    s_ = tmpb.next()
                    S.op("act", lambda e, s_=s_, k=k: e.activation(out=s_.t[:], in_=uc32_v[:, k, :], func=AF.Square), reads=[uc32.b[k]], writes=s_.all)
                    S.op("pe", lambda e, s_=s_, k=k: e.matmul(out=pq.t[:, 0:N], lhsT=onesb, rhs=s_.t[:], start=(k == 0), stop=(k == KC - 1)), reads=s_.all + CST, writes=pq.all, sig=True)
                mean = meanT
                var = rstd
                S.op("dve", lambda e: e.tensor_scalar(out=mean.t[:], in0=pm.t[:, 0:N], scalar1=1.0 / D, scalar2=None, op0=ALU.mult), reads=pm.all, writes=mean.all)
                S.op("dve", lambda e: e.tensor_tensor(out=var.t[:], in0=mean.t[:], in1=mean.t[:], op=ALU.mult), reads=mean.all, writes=var.all)
                S.op("dve", lambda e: e.scalar_tensor_tensor(out=var.t[:], in0=pq.t[:, 0:N], scalar=1.0 / D, in1=var.t[:], op0=ALU.mult, op1=ALU.subtract), reads=pq.all + var.all, writes=var.all)
                S.op("act", lambda e: e.activation(out=var.t[:], in_=var.t[:], func=AF.Sqrt, bias=epsc.t[:, 0:1], scale=1.0), reads=var.all + epsc.all, writes=var.all)
                S.op("dve", lambda e: e.reciprocal(out=var.t[:], in_=var.t[:]), reads=var.all, writes=var.all)
                for k in range(KC):
                    tf = tmpf.next()
                    S.op("dve", lambda e, k=k, tf=tf: e.tensor_tensor(out=tf.t[:], in0=uc32_v[:, k, :], in1=mean.t[:], op=ALU.subtract), reads=[uc32.b[k]] + mean.all, writes=tf.all)
                    S.op("dve", lambda e, k=k, tf=tf: e.tensor_tensor(out=tf.t[:], in0=tf.t[:], in1=var.t[:], op=ALU.mult), reads=tf.all + var.all, writes=tf.all)
                    S.op("act", lambda e, k=k, tf=tf: e.activation(out=ucg.t[:, k, :], in_=tf.t[:], func=AF.Silu, bias=vc(l, k, V_CLNB), scale=vc(l, k, V_CLNW)), reads=tf.all + vcol.all, writes=[ucg.b[k]])
                proj(w_in[l], 4 * D, KC, hT, N, ev_zc)
                for k in range(KC):
                    S.op("dve", lambda e, k=k: e.tensor_tensor(out=ucg.t[:, k, :], in0=ucg.t[:, k, :], in1=szc.t[:, k, :], op=ALU.mult), reads=[ucg.b[k], szc.b[k]], writes=[ucg.b[k]])

                proj(w_in[l], 6 * D, KC, hT, N, ev_gc)

                def ev_yc(fc, pa):
                    tf = tmpb.next()
                    S.op("dve", lambda e: e.tensor_tensor(out=tf.t[:], in0=pa.t[:, 0:N], in1=sgc.t[:, fc, :], op=ALU.mult), reads=pa.all + [sgc.b[fc]], writes=tf.all)
                    S.op("dve", lambda e: e.tensor_tensor(out=mrg.t[:, fc, :], in0=tf.t[:], in1=ymg.t[:, fc, :], op=ALU.add), reads=tf.all + [ymg.b[fc]], writes=[mrg.b[fc]])
                proj(w_br_c[l], 0, KC, ucg, N, ev_yc)

                def ev_out(fc, pa):
                    if not sample:
                        S.op("dve", lambda e: e.scalar_tensor_tensor(out=xT.t[:, fc, :], in0=pa.t[:, 0:N], scalar=modv(l, 2, fc), in1=xT.t[:, fc, :], op0=ALU.mult, op1=ALU.add), reads=pa.all + [xT.b[fc]] + MOD, writes=[xT.b[fc]])
                    else:
                        tf = tmpf.next()
                        S.op("dve", lambda e: e.tensor_tensor(out=tf.t[:], in0=pa.t[:, 0:N], in1=modv(l, 2, fc), op=ALU.mult), reads=pa.all + MOD, writes=tf.all)
                        S.op("dve", lambda e: e.tensor_tensor(out=xT.t[:, fc, :], in0=tf.t[:], in1=xT.t[:, fc, :], op=ALU.add), reads=tf.all + [xT.b[fc]], writes=[xT.b[fc]])
                    rms_chunk(fc, defer=True)
                proj(w_out[l], 0, KC, mrg, N, ev_out)
                rms_flush()

            def mlstm_prompt(l, ti, gi, lf, filler=None, filler2=None):
                aT, FT, cT, tT = rowA, rowF, rowA, rowF
                if ti == 0:
                    S.op("dve", lambda e: e.memset(C32.t[:], 0.0), writes=C32.all)
                else:
                    S.dma("sp", lambda e: e.dma_start(out=C32.t[:], in_=oCp[l].rearrange("h (k p) e -> p h k e", p=P)), reads=[dC[l]], writes=C32.all)
                Amax, FL, mall, mprv, Mx, alp = sm8
                def fill(n):
                    if filler is not None:
                        for _ in range(n):
                            next(filler, None)

                def fill_g(n):
                    if filler2 is not None:
                        for _ in range(n):
                            next(filler2, None)
                fill_g(4)
                pa = psA.next()
                pf = psA.next()
                for blk in range(NB):
                    S.op("pe", lambda e, blk=blk: e.matmul(out=pa.t[0:H, blk * LC:(blk + 1) * LC], lhsT=gi.t[:, blk, :], rhs=ident[0:LC, 0:LC], start=True, stop=False), reads=gi.all + CST, writes=pa.all, sig=False)
                    S.op("pe", lambda e, blk=blk: e.matmul(out=pa.t[0:H, blk * LC:(blk + 1) * LC], lhsT=lf.t[:, blk, :], rhs=ntri[0:LC, 0:LC], start=False, stop=True), reads=lf.all + CST, writes=pa.all, sig=False)
                    S.op("pe", lambda e, blk=blk: e.matmul(out=pf.t[0:H, blk * LC:(blk + 1) * LC], lhsT=lf.t[:, blk, :], rhs=triu[0:LC, 0:LC], start=True, stop=True), reads=lf.all + CST, writes=pf.all, sig=True)
                S.op("act", lambda e: e.activation(out=aT.t[:], in_=pa.t[0:H, 0:N], func=AF.Copy), reads=pa.all, writes=aT.all)
                S.op("act", lambda e: e.activation(out=FT.t[:], in_=pf.t[0:H, 0:N], func=AF.Copy), reads=pf.all, writes=FT.all)
                fill_g(4)
                S.op("dve", lambda e: e.tensor_reduce(out=Amax.t[:], in_=aT.t[:].rearrange("p (c s) -> p c s", s=LC), axis=AX.X, op=ALU.max), reads=aT.all, writes=Amax.all)
                S.op("dve", lambda e: e.tensor_copy(out=FL.t[:], in_=FT.t[:].rearrange("p (c s) -> p c s", s=LC)[:, :, LC - 1]), reads=FT.all, writes=FL.all)
                S.op("dve", lambda e: e.tensor_tensor_scan(out=mall.t[:], data0=Amax.t[:], data1=FL.t[:], initial=mst.t[:, l:l + 1], op0=ALU.max, op1=ALU.add), reads=Amax.all + FL.all + [mst.b[l]], writes=mall.all)
                S.op("dve", lambda e: e.tensor_copy(out=mprv.t[:, 0:1], in_=mst.t[:, l:l + 1]), reads=[mst.b[l]], writes=mprv.all)
                S.op("dve", lambda e: e.tensor_copy(out=mprv.t[:, 1:NB], in_=mall.t[:, 0:NB - 1]), reads=mall.all + mprv.all, writes=mprv.all)
                S.op("dve", lambda e: e.tensor_copy(out=mst.t[:, l:l + 1], in_=mall.t[:, NB - 1:NB]), reads=mall.all + mprv.all, writes=[mst.b[l]])
                S.op("dve", lambda e: e.tensor_tensor(out=Mx.t[:], in0=mprv.t[:], in1=Amax.t[:], op=ALU.max), reads=mprv.all + Amax.all, writes=Mx.all)
                S.op("dve", lambda e: e.tensor_sub(out=alp.t[:], in0=mprv.t[:], in1=Mx.t[:]), reads=mprv.all + Mx.all, writes=alp.all)
                S.op("act", lambda e: e.activation(out=alp.t[:], in_=alp.t[:], func=AF.Exp), reads=alp.all, writes=alp.all)
                Mb = bcast(Mx.t[:].unsqueeze(2), [H, NB, LC])
                S.op("dve", lambda e: e.tensor_tensor(out=cT.t[:].rearrange("p (c s) -> p c s", s=LC), in0=aT.t[:].rearrange("p (c s) -> p c s", s=LC), in1=Mb, op=ALU.subtract), reads=aT.all + Mx.all, writes=cT.all)
                S.op("act", lambda e: e.activation(out=cT.t[:], in_=cT.t[:], func=AF.Exp), reads=cT.all, writes=cT.all)
                S.op("dve", lambda e: e.tensor_tensor(out=tT.t[:].rearrange("p (c s) -> p c s", s=LC), in0=FT.t[:].rearrange("p (c s) -> p c s", s=LC), in1=Mb, op=ALU.add), reads=FT.all + Mx.all, writes=tT.all)
                S.op("act", lambda e: e.activation(out=tT.t[:], in_=tT.t[:], func=AF.Exp, scale=-1.0), reads=tT.all, writes=tT.all)
                fill_g(4)
                for blk in range(NB):
                    S.op("pe", lambda e, blk=blk: e.matmul(out=psS.t[0:LC, blk * 8:blk * 8 + 4], lhsT=cT.t[:, blk * LC:(blk + 1) * LC], rhs=ident[0:H, 0:H], start=True, stop=True), reads=cT.all + CST, writes=psS.all, sig=False)
                    S.op("pe", lambda e, blk=blk: e.matmul(out=psS.t[0:LC, blk * 8 + 4:blk * 8 + 8], lhsT=tT.t[:, blk * LC:(blk + 1) * LC], rhs=ident[0:H, 0:H], start=True, stop=True), reads=tT.all + CST, writes=psS.all, sig=(blk == NB - 1))
                S.op("dve", lambda e: e.tensor_copy(out=ctk.t[:], in_=psS.t[0:LC, 0:NB * 8].rearrange("p (b g) -> p b g", g=8)), reads=psS.all, writes=ctk.all)
                S.op("dve", lambda e: e.tensor_copy(out=cbk.t[:], in_=ctk.t[:, :, 0:4]), reads=ctk.all, writes=cbk.all)
                fill_g(4)
                S.op("dve", lambda e: e.tensor_tensor(out=adg.t[:], in0=bcast(alp.t[:].unsqueeze(2), [H, NB, H]), in1=bcast(ident[0:H, 0:H].unsqueeze(1), [H, NB, H]), op=ALU.mult), reads=alp.all + CST, writes=adg.all)
                S.op("pe", lambda e: e.matmul(out=psS.t[:, 0:NB * H], lhsT=ones[0:H, :], rhs=adg.t[:].rearrange("p c h -> p (c h)"), start=True, stop=True), reads=adg.all + CST, writes=psS.all, sig=True)
                S.op("dve", lambda e: e.tensor_copy(out=abc.t[:], in_=psS.t[:, 0:NB * H].rearrange("p (c h) -> p c h", h=H)), reads=psS.all, writes=abc.all)
                for blk in range(NB):
                    S.op("dve", lambda e, blk=blk: e.tensor_tensor(out=vtok_v[:, blk, :].rearrange("p (h e) -> p h e", h=H), in0=vtok_v[:, blk, :].rearrange("p (h e) -> p h e", h=H), in1=bcast(ctk.t[:, blk, 0:4].unsqueeze(2), [LC, H, DH]), op=ALU.mult),
                         reads=[vtok.b[blk]] + ctk.all, writes=[vtok.b[blk]])
                for h in range(H):
                    S.op("act", lambda e, h=h: e.activation(out=Cb.t[:, h].rearrange("p k e -> p (k e)"), in_=C32.t[:, h].rearrange("p k e -> p (k e)"), func=AF.Copy, scale=abc.t[:, 0, h:h + 1]), reads=[C32.b[h]] + abc.all, writes=[Cb.b[h]])
                S.op("dve", lambda e: e.tensor_tensor(out=nb_.t[:], in0=n32.t[:, l], in1=bcast(abc.t[:, 0, :].unsqueeze(2), [P, H, 2]), op=ALU.mult), reads=[n32.b[l]] + abc.all, writes=nb_.all)

                fill_g(16)
                sms = []
                pend = []
                for j in range(NB):
                    t0 = j * LC
                    pa = psA.next()
                    for h in range(H):
                        for k in range(2):
                            S.op("pe", lambda e, h=h, k=k, pa=pa: e.matmul(out=pa.t[0:LC, h * LC:(h + 1) * LC], lhsT=kT.t[:, h * 2 + k, t0:t0 + LC], rhs=qT.t[:, h * 2 + k, t0:t0 + LC], start=(k == 0), stop=(k == 1)),
                                 reads=[kT.b[h * 2 + k], qT.b[h * 2 + k]], writes=pa.all, sig=(h == H - 1 and k == 1))
                    sm_ = Sm.next()
                    S.op("dve", lambda e, pa=pa, sm_=sm_: e.tensor_tensor(out=sm_.t[:], in0=pa.t[0:LC, 0:H * LC].rearrange("p (h l) -> p h l", h=H), in1=bcast(maskb[0:LC, 0:LC].unsqueeze(1), [LC, H, LC]), op=ALU.mult), reads=pa.all + CST, writes=sm_.all)
                    sms.append(sm_)
                def evac_chunk(j, po):
                    dab, dmx, d2, rs, nbias = smls[j % 2]
                    S.op("dve", lambda e: e.tensor_tensor(out=dmx.t[:], in0=dab.t[:], in1=ctk.t[:, j, 4:8], op=ALU.max), reads=dab.all + ctk.all, writes=dmx.all)
                    S.op("dve", lambda e: e.tensor_tensor(out=d2.t[:], in0=dmx.t[:], in1=dmx.t[:], op=ALU.mult), reads=dmx.all, writes=d2.all)
                    for h in range(H):
                        S.op("dve", lambda e, h=h: e.bn_stats(out=st6.t[:, h, :], in_=po.t[0:LC, h * DH:(h + 1) * DH]), reads=po.all, writes=st6.all)
                    for h in range(H):
                        S.op("dve", lambda e, h=h: e.bn_aggr(out=mv.t[:, h, :], in_=st6.t[:, h, :]), reads=st6.all, writes=mv.all)
                    S.op("dve", lambda e: e.scalar_tensor_tensor(out=rs.t[:], in0=d2.t[:], scalar=EPS, in1=mv.t[:, :, 1], op0=ALU.mult, op1=ALU.add), reads=d2.all + mv.all, writes=rs.all)
                    S.op("act", lambda e: e.activation(out=rs.t[:], in_=rs.t[:], func=AF.Sqrt), reads=rs.all, writes=rs.all)
                    S.op("dve", lambda e: e.reciprocal(out=rs.t[:], in_=rs.t[:]), reads=rs.all, writes=rs.all)
                    S.op("dve", lambda e: e.scalar_tensor_tensor(out=nbias.t[:], in0=mv.t[:, :, 0], scalar=-1.0, in1=rs.t[:], op0=ALU.mult, op1=ALU.mult), reads=mv.all + rs.all, writes=nbias.all)
                    for h in range(H):
                        S.op("act", lambda e, h=h: e.activation(out=hftok.t[:, j, h * DH:(h + 1) * DH], in_=po.t[0:LC, h * DH:(h + 1) * DH], func=AF.Identity, bias=nbias.t[:, h:h + 1], scale=rs.t[:, h:h + 1]),
                             reads=po.all + rs.all + nbias.all, writes=[hftok.b[j]])
                for j in range(NB):
                    t0 = j * LC
                    po = psOr.next()
                    sm_ = sms[j]
                    for h in range(H):
                        pc = psA.next()
                        for k in range(2):
                            S.op("pe", lambda e, h=h, k=k, pc=pc: e.matmul(out=pc.t[:, k * DH:(k + 1) * DH], lhsT=ktok.t[:, j, h * DH + k * P:h * DH + (k + 1) * P], rhs=vtok_v[:, j, h * DH:(h + 1) * DH], start=True, stop=True), reads=[ktok.b[j], vtok.b[j]], writes=pc.all, sig=(k == 1))
                        S.op("dve", lambda e, h=h, pc=pc: e.scalar_tensor_tensor(out=C32.t[:, h].rearrange("p k e -> p (k e)"), in0=C32.t[:, h].rearrange("p k e -> p (k e)"), scalar=abc.t[:, j, h:h + 1], in1=pc.t[:, 0:2 * DH], op0=ALU.mult, op1=ALU.add),
                             reads=[C32.b[h]] + pc.all + abc.all, writes=[C32.b[h]])
                    pend_now = list(pend)
                    del pend[:]
                    for h in range(H):
                        S.op("pe", lambda e, h=h, sm_=sm_: e.matmul(out=po.t[0:LC, h * DH:(h + 1) * DH], lhsT=sm_.t[:, h, :], rhs=vtok_v[:, j, h * DH:(h + 1) * DH], start=True, stop=False), reads=sm_.all + [vtok.b[j]], writes=po.all, sig=False)
                        for k in range(2):
                            S.op("pe", lambda e, h=h, k=k: e.matmul(out=po.t[0:LC, h * DH:(h + 1) * DH], lhsT=qT.t[:, h * 2 + k, t0:t0 + LC], rhs=Cb.t[:, h, k, :], start=False, stop=(k == 1)), reads=[qT.b[h * 2 + k], Cb.b[h]], writes=po.all, sig=False)
                    for h in range(H):
                        S.op("pe", lambda e, h=h, sm_=sm_: e.matmul(out=psS.t[0:LC, h:h + 1], lhsT=sm_.t[:, h, :], rhs=cbk.t[:, j, h:h + 1], start=True, stop=False), reads=sm_.all + cbk.all, writes=psS.all, sig=False)
                        for k in range(2):
                            S.op("pe", lambda e, h=h, k=k: e.matmul(out=psS.t[0:LC, h:h + 1], lhsT=qT.t[:, h * 2 + k, t0:t0 + LC], rhs=nb_.t[:, h, k:k + 1], start=False, stop=(k == 1)), reads=[qT.b[h * 2 + k]] + nb_.all, writes=psS.all, sig=(h == H - 1 and k == 1))
                    dab, dmx, d2, rs, nbias = smls[j % 2]
                    S.op("act", lambda e: e.activation(out=dab.t[:], in_=psS.t[0:LC, 0:H], func=AF.Abs), reads=psS.all, writes=dab.all)
                    for pj, ppo in pend_now:
                        evac_chunk(pj, ppo)
                    if j + 1 < NB:
                        for h in range(H):
                            S.op("act", lambda e, h=h: e.activation(out=Cb.t[:, h].rearrange("p k e -> p (k e)"), in_=C32.t[:, h].rearrange("p k e -> p (k e)"), func=AF.Copy, scale=abc.t[:, j + 1, h:h + 1]), reads=[C32.b[h]] + abc.all, writes=[Cb.b[h]])
                    for h in range(H):
                        for k in range(2):
                            S.op("pe", lambda e, h=h, k=k: e.matmul(out=psS.t[:, 8 + h * 2 + k:9 + h * 2 + k], lhsT=ktok.t[:, j, h * DH + k * P:h * DH + (k + 1) * P], rhs=cbk.t[:, j, h:h + 1], start=True, stop=True), reads=[ktok.b[j]] + cbk.all, writes=psS.all, sig=(h == H - 1 and k == 1))
                    S.op("dve", lambda e: e.tensor_tensor(out=n32.t[:, l], in0=n32.t[:, l], in1=bcast(abc.t[:, j, :].unsqueeze(2), [P, H, 2]), op=ALU.mult), reads=[n32.b[l]] + abc.all, writes=[n32.b[l]])
                    S.op("dve", lambda e: e.tensor_tensor(out=n32.t[:, l], in0=n32.t[:, l], in1=psS.t[:, 8:16].rearrange("p (h k) -> p h k", k=2), op=ALU.add), reads=[n32.b[l]] + psS.all, writes=[n32.b[l]])
                    if j + 1 < NB:
                        S.op("dve", lambda e: e.tensor_tensor(out=nb_.t[:], in0=n32.t[:, l], in1=bcast(abc.t[:, j + 1, :].unsqueeze(2), [P, H, 2]), op=ALU.mult), reads=[n32.b[l]] + abc.all, writes=nb_.all)
                    pend.append((j, po))
                    fill(2)

                while pend:
                    evac_chunk(*pend.pop(0))
                S.dma("sp", lambda e: e.dma_start(out=oCp[l].rearrange("h (k p) e -> p h k e", p=P), in_=C32.t[:]), reads=C32.all, writes=[dC[l]])
                if ti == NT - 1:
                    S.dma("sp", lambda e: e.dma_start(out=onp[l].rearrange("(h k p) -> p h k", h=H, p=P), in_=n32.t[:, l], allow_slow_non_contiguous=True), reads=[n32.b[l]])
                    S.dma("sp", lambda e: e.dma_start(out=omp[l].rearrange("(h o) -> h o", o=1), in_=mst.t[:, l:l + 1], allow_slow_non_contiguous=True), reads=[mst.b[l]])

            def mlstm_sample(l, gi, lf):
                R = NS
                a_, mt, dec, wsc, thr, qk, qn, den, dmx, tmp = s16
                S.dma("sp", lambda e: e.dma_start(out=mprev.t[:], in_=sm[l, :, :]), writes=mprev.all)
                S.dma("sp", lambda e: e.dma_start(out=nst.t[:].rearrange("p h e -> p (h e)"), in_=sn[l, :, :]), writes=nst.all)
                g_i = gi.t[0:R, 0, :]
                g_f = lf.t[0:R, 0, :]
                S.op("dve", lambda e: e.tensor_add(out=a_.t[:], in0=g_f, in1=mprev.t[:]), reads=lf.all + mprev.all, writes=a_.all)
                S.op("dve", lambda e: e.tensor_max(out=mt.t[:], in0=a_.t[:], in1=g_i), reads=a_.all + gi.all, writes=mt.all)
                S.op("dve", lambda e: e.tensor_sub(out=dec.t[:], in0=a_.t[:], in1=mt.t[:]), reads=a_.all + mt.all, writes=dec.all)
                S.op("act", lambda e: e.activation(out=dec.t[:], in_=dec.t[:], func=AF.Exp), reads=dec.all, writes=dec.all)
                S.op("dve", lambda e: e.tensor_sub(out=wsc.t[:], in0=g_i, in1=mt.t[:]), reads=gi.all + mt.all, writes=wsc.all)
                S.op("act", lambda e: e.activation(out=wsc.t[:], in_=wsc.t[:], func=AF.Exp), reads=wsc.all, writes=wsc.all)
                S.op("act", lambda e: e.activation(out=thr.t[:], in_=mt.t[:], func=AF.Exp, scale=-1.0), reads=mt.all, writes=thr.all)
                S.dma("sp", lambda e: e.dma_start(out=oms[l, :, :], in_=mt.t[:]), reads=mt.all)
                ktf = tmpS_k
                S.op("act", lambda e: e.activation(out=ktf.t[:], in_=ktok.t[0:R, 0, :], func=AF.Copy), reads=ktok.all, writes=ktf.all)
                S.op("dve", lambda e: e.tensor_tensor(out=prod.t[:], in0=qtok.t[:], in1=ktf.t[:], op=ALU.mult), reads=qtok.all + ktf.all, writes=prod.all)
                S.op("dve", lambda e: e.tensor_reduce(out=qk.t[:], in_=prod.t[:].rearrange("p (h e) -> p h e", h=H), axis=AX.X, op=ALU.add), reads=prod.all, writes=qk.all)
                S.op("dve", lambda e: e.tensor_tensor(out=prod.t[:], in0=qtok.t[:], in1=nst.t[:].rearrange("p h e -> p (h e)"), op=ALU.mult), reads=qtok.all + nst.all, writes=prod.all)
                S.op("dve", lambda e: e.tensor_reduce(out=qn.t[:], in_=prod.t[:].rearrange("p (h e) -> p h e", h=H), axis=AX.X, op=ALU.add), reads=prod.all, writes=qn.all)
                S.op("dve", lambda e: e.tensor_mul(out=qk.t[:], in0=qk.t[:], in1=wsc.t[:]), reads=qk.all + wsc.all, writes=qk.all)
                S.op("dve", lambda e: e.tensor_mul(out=den.t[:], in0=dec.t[:], in1=qn.t[:]), reads=dec.all + qn.all, writes=den.all)
                S.op("dve", lambda e: e.tensor_add(out=den.t[:], in0=den.t[:], in1=qk.t[:]), reads=den.all + qk.all, writes=den.all)
                S.op("act", lambda e: e.activation(out=den.t[:], in_=den.t[:], func=AF.Abs), reads=den.all, writes=den.all)
                S.op("dve", lambda e: e.tensor_max(out=dmx.t[:], in0=den.t[:], in1=thr.t[:]), reads=den.all + thr.all, writes=dmx.all)
                S.op("dve", lambda e: e.tensor_tensor(out=nnew.t[:], in0=nst.t[:], in1=bcast(dec.t[:].unsqueeze(2), [R, H, DH]), op=ALU.mult), reads=nst.all + dec.all, writes=nnew.all)
                S.op("dve", lambda e: e.tensor_tensor(out=ktf.t[:].rearrange("p (h e) -> p h e", h=H), in0=ktf.t[:].rearrange("p (h e) -> p h e", h=H), in1=bcast(wsc.t[:].unsqueeze(2), [R, H, DH]), op=ALU.mult), reads=ktf.all + wsc.all, writes=ktf.all)
                S.op("dve", lambda e: e.tensor_add(out=nnew.t[:].rearrange("p h e -> p (h e)"), in0=nnew.t[:].rearrange("p h e -> p (h e)"), in1=ktf.t[:]), reads=nnew.all + ktf.all, writes=nnew.all)
                S.dma("sp", lambda e: e.dma_start(out=ons[l, :, :], in_=nnew.t[:].rearrange("p h e -> p (h e)")), reads=nnew.all)
                for half in range(2):
                    pa = psA.next()
                    for kk in range(4):
                        k = half * 4 + kk
                        S.op("pe", lambda e, pa=pa, k=k, kk=kk: e.transpose(out=pa.t[:, kk * R:(kk + 1) * R], in_=qtok.t[:, k * P:(k + 1) * P], identity=ident[0:R, 0:R]), reads=qtok.all + CST, writes=pa.all, sig=(kk == 3))
                    S.op("dve", lambda e, pa=pa, half=half: e.tensor_copy(out=qTf.t[:, half * 4:(half + 1) * 4, :], in_=pa.t[:, 0:4 * R].rearrange("p (k b) -> p k b", b=R)), reads=pa.all, writes=qTf.b[half * 4:(half + 1) * 4])
                for k in range(KC):
                    S.op("dve", lambda e, k=k: e.tensor_tensor(out=qm.t[:, k], in0=bcast(qTf.t[:, k, :].unsqueeze(1), [P, R, R]), in1=cst.t[:, 4 * P:4 * P + R * R].rearrange("p (a b) -> p a b", b=R), op=ALU.mult), reads=[qTf.b[k]] + CST, writes=[qm.b[k]])
                S.op("dve", lambda e: e.tensor_tensor(out=dexp.t[:], in0=bcast(dec.t[:].unsqueeze(1), [R, R, H]), in1=bcast(ident[0:R, 0:R].unsqueeze(2), [R, R, H]), op=ALU.mult), reads=dec.all + CST, writes=dexp.all)
                S.op("pe", lambda e: e.matmul(out=psS.t[:, 0:R * H], lhsT=ones[0:R, :], rhs=dexp.t[:].rearrange("p b h -> p (b h)"), start=True, stop=True), reads=dexp.all + CST, writes=psS.all, sig=True)
                S.op("dve", lambda e: e.tensor_copy(out=dbc.t[:], in_=psS.t[:, 0:R * H].rearrange("p (b h) -> p b h", h=H)), reads=psS.all, writes=dbc.all)
                cins = []
                psO = psOr.next()
                cis = {}

                def issue_in(bb):
                    ci_ = Cin.next()
                    S.dma("sp", lambda e: e.dma_start(out=ci_.t[:], in_=sC[l, bb].rearrange("h (k p) e -> p h k e", p=P)), writes=ci_.all)
                    cis[bb] = ci_
                issue_in(0)
                issue_in(1)
                for b in range(R):
                    if b + 2 < R:
                        issue_in(b + 2)
                    ci = cis[b]
                    if b % 4 == 0:
                        S.op("dve", lambda e, b=b: e.tensor_tensor(out=kexp.t[:], in0=bcast(ktf.t[:].unsqueeze(1), [R, 4, D]), in1=bcast(ident[0:R, b:b + 4].unsqueeze(2), [R, 4, D]), op=ALU.mult), reads=ktf.all + CST, writes=kexp.all)
                    cb16 = Cbf.next()
                    S.op("act", lambda e, ci=ci, cb16=cb16: e.activation(out=cb16.t[:].rearrange("p h k e -> p (h k e)"), in_=ci.t[:].rearrange("p h k e -> p (h k e)"), func=AF.Copy), reads=ci.all, writes=cb16.all)
                    for h in range(H):
                        for k in range(2):
                            S.op("pe", lambda e, ci=ci, b=b, h=h, k=k: e.matmul(out=psO.t[0:R, h * DH:(h + 1) * DH], lhsT=qm.t[:, h * 2 + k, b, :], rhs=cb16.t[:, h, k, :], start=(b == 0 and k == 0 and h % 2 == 0), stop=(b == R - 1 and k == 1), skip_group_check=True),
                                 reads=[qm.b[h * 2 + k]] + cb16.all, writes=psO.all, sig=(h == H - 1 and k == 1))
                    co = Cout.next()
                    for h in range(H):
                        pc = psA.next()
                        for k in range(2):
                            S.op("pe", lambda e, pc=pc, b=b, h=h, k=k: e.matmul(out=pc.t[:, k * DH:(k + 1) * DH], lhsT=kexp.t[:, b % 4, h * DH + k * P:h * DH + (k + 1) * P], rhs=vtok_v[0:R, 0, h * DH:(h + 1) * DH], start=True, stop=True),
                                 reads=kexp.all + vtok.all, writes=pc.all, sig=(k == 1))
                        S.op("dve", lambda e, pc=pc, ci=ci, co=co, b=b, h=h: e.scalar_tensor_tensor(out=co.t[:, h].rearrange("p k e -> p (k e)"), in0=ci.t[:, h].rearrange("p k e -> p (k e)"), scalar=dbc.t[:, b, h:h + 1], in1=pc.t[:, 0:2 * DH], op0=ALU.mult, op1=ALU.add),
                             reads=ci.all + pc.all + dbc.all, writes=co.all)
                    S.dma("sp", lambda e, co=co, b=b: e.dma_start(out=oCs[l, b].rearrange("h (k p) e -> p h k e", p=P), in_=co.t[:]), reads=co.all)
                S.op("dve", lambda e: e.tensor_tensor(out=numt.t[:].rearrange("p (h e) -> p h e", h=H), in0=psO.t[0:R, 0:D].rearrange("p (h e) -> p h e", h=H), in1=bcast(dec.t[:].unsqueeze(2), [R, H, DH]), op=ALU.mult), reads=psO.all + dec.all, writes=numt.all)
                S.op("dve", lambda e: e.tensor_tensor(out=prod.t[:].rearrange("p (h e) -> p h e", h=H), in0=vtok_v[0:R, 0, :].rearrange("p (h e) -> p h e", h=H), in1=bcast(qk.t[:].unsqueeze(2), [R, H, DH]), op=ALU.mult), reads=vtok.all + qk.all, writes=prod.all)
                S.op("dve", lambda e: e.tensor_add(out=numt.t[:], in0=numt.t[:], in1=prod.t[:]), reads=numt.all + prod.all, writes=numt.all)
                for h in range(H):
                    S.op("dve", lambda e, h=h: e.bn_stats(out=st6.t[:, h, :], in_=numt.t[:, h * DH:(h + 1) * DH]), reads=numt.all, writes=st6.all)
                for h in range(H):
                    S.op("dve", lambda e, h=h: e.bn_aggr(out=mv.t[:, h, :], in_=st6.t[:, h, :]), reads=st6.all, writes=mv.all)
                S.op("dve", lambda e: e.tensor_mul(out=tmp.t[:], in0=dmx.t[:], in1=dmx.t[:]), reads=dmx.all, writes=tmp.all)
                S.op("dve", lambda e: e.scalar_tensor_tensor(out=tmp.t[:], in0=tmp.t[:], scalar=EPS, in1=mv.t[:, :, 1], op0=ALU.mult, op1=ALU.add), reads=tmp.all + mv.all, writes=tmp.all)
                S.op("act", lambda e: e.activation(out=tmp.t[:], in_=tmp.t[:], func=AF.Sqrt), reads=tmp.all, writes=tmp.all)
                S.op("dve", lambda e: e.reciprocal(out=tmp.t[:], in_=tmp.t[:]), reads=tmp.all, writes=tmp.all)
                S.op("dve", lambda e: e.tensor_tensor(out=numt.t[:].rearrange("p (h e) -> p h e", h=H), in0=numt.t[:].rearrange("p (h e) -> p h e", h=H), in1=bcast(mv.t[:, :, 0:1], [R, H, DH]), op=ALU.subtract), reads=numt.all + mv.all, writes=numt.all)
                S.op("dve", lambda e: e.tensor_tensor(out=hftok.t[0:R, 0, :].rearrange("p (h e) -> p h e", h=H), in0=numt.t[:].rearrange("p (h e) -> p h e", h=H), in1=bcast(tmp.t[:].unsqueeze(2), [R, H, DH]), op=ALU.mult), reads=numt.all + tmp.all, writes=hftok.all)

            sb_wg = A([P, 3 * KC, 8], BF16)
            if sample:
                tmpS_k = A([NS, D], F32)

            for ti in tiles:
                if not sample:
                    for blk in range(N // P):
                        def wr_x(k, n, pa, blk=blk):
                            S.op("act", lambda e: e.activation(out=xT.t[:, k:k + n, blk * P:(blk + 1) * P], in_=pa.t[:, 0:n * P].rearrange("p (k t) -> p k t", t=P), func=AF.Copy), reads=pa.all, writes=xT.b[k:k + n])
                        load_rows(xp[ti * T + blk * P:ti * T + (blk + 1) * P, :], P, wr_x, stg_io)
                else:
                    def wr_xs(k, n, pa):
                        S.op("act", lambda e: e.activation(out=xT.t[:, k:k + n, :], in_=pa.t[:, 0:n * NS].rearrange("p (k t) -> p k t", t=NS), func=AF.Copy), reads=pa.all, writes=xT.b[k:k + n])
                    load_rows(xs[:, :], NS, wr_xs)
                for k in range(KC):
                    rms_chunk(k)
                for l in range(DEPTH):
                    layer(l, ti)
                rms_finish()
                for k in range(KC):
                    S.op("dve", lambda e, k=k: e.scalar_tensor_tensor(out=uc32_v[:, k, :], in0=xT.t[:, k, :], scalar=vcol.t[:, 0, k, V_FIN:V_FIN + 1], in1=rstd.t[:], op0=ALU.mult, op1=ALU.mult), reads=[xT.b[k]] + rstd.all + vcol.all, writes=[uc32.b[k]])
                if not sample:
                    for blk in range(N // P):
                        store_rows(lambda k, blk=blk: uc32_v[:, k, blk * P:(blk + 1) * P], uc32.all, P, yp[ti * T + blk * P:ti * T + (blk + 1) * P, :], False, stg_io)
                else:
                    store_rows(lambda k: uc32_v[:, k, :], uc32.all, NS, ys[:, :], False)

        if NS > 0:
            with contextlib.ExitStack() as st1:
                run_group(NS, True, [0], st1)
                barrier()
        with contextlib.ExitStack() as st2:
            run_group(T, False, list(range(NT)), st2)
            S.finish()
            S.emit()
    return nc


_CACHE = {}
DBG = {}


def make_consts():
    c = np.zeros((P, 4 * P + 256), np.float32)
    c[:, 4 * P:] = np.eye(16, dtype=np.float32).reshape(1, 256)
    c[:, 0:P] = np.eye(P, dtype=np.float32)
    tri = np.triu(np.ones((P, P), np.float32))
    c[:, P:2 * P] = tri
    c[:, 2 * P:3 * P] = 1.0
    c[:, 3 * P:4 * P] = -tri
    return c


def run(inputs, SEQ, DEPTH, NS, ncores=8):
    key = (SEQ, DEPTH, NS)
    if key not in _CACHE:
        _CACHE[key] = build(SEQ, DEPTH, NS)
    nc = _CACHE[key]
    f = lambda a: np.ascontiguousarray(np.asarray(a, dtype=np.float32))
    g = {k: f(v) for k, v in inputs.items()}
    vecs = np.zeros((DEPTH, NV, D), np.float32)
    vecs[:, V_NORM] = g["norm_w"]
    vecs[:, V_MCB] = g["mconv_b"]
    vecs[:, V_GN] = g["gn_w"]
    vecs[:, V_SKIP] = g["skip"]
    vecs[:, V_CCB] = g["cconv_b"]
    vecs[:, V_CLNW] = g["cln_w"]
    vecs[:, V_CLNB] = g["cln_b"]
    vecs[:, V_BADA:V_BADA + 3] = g["b_ada"].reshape(DEPTH, 3, D)
    vecs[:, V_MCW:V_MCW + MW] = g["mconv_w"]
    vecs[:, V_CCW:V_CCW + CW] = g["cconv_w"]
    vecs[:, V_FIN] = g["final_norm_w"][None, :]
    consts = make_consts()
    shared = {k: g[k] for k in ("w_ada", "w_in", "w_q", "w_k", "w_v", "w_gate", "b_gate", "w_br_m", "w_br_c", "w_out")}
    shared["vecs"] = vecs
    shared["consts"] = consts
    in_maps = []
    for i in range(ncores):
        sl = slice(i * NS, (i + 1) * NS)
        m = dict(shared)
        m["xp"] = g["x_prompt"][i]
        m["xs"] = g["x_sample"][sl, 0, :]
        m["call"] = np.concatenate([g["c_prompt"][i:i + 1], g["c_sample"][sl]], axis=0)
        m["sC"] = g["state_mlstm_C"][:, sl]
        m["sn"] = g["state_mlstm_n"][:, sl].reshape(DEPTH, NS, H * DH)
        m["sm"] = g["state_mlstm_m"][:, sl]
        m["smc"] = g["state_mlstm_conv"][:, sl].reshape(DEPTH, NS * (MW - 1), D)
        m["scv"] = g["state_conv"][:, sl].reshape(DEPTH, NS * (CW - 1), D)
        in_maps.append({k: np.ascontiguousarray(v) for k, v in m.items()})
    res = run_bass_kernel_spmd(nc, in_maps, core_ids=list(range(ncores)))
    R = res.results
    cat = lambda name, ax: np.concatenate([np.asarray(r[name]) for r in R], axis=ax)
    stk = lambda name: np.stack([np.asarray(r[name]) for r in R], axis=1)
    y_prompt = np.stack([np.asarray(r["yp"]) for r in R], axis=0)
    y_sample = cat("ys", 0).reshape(ncores * NS, 1, D)
    Cp = stk("oCp")
    np_ = stk("onp").reshape(DEPTH, ncores, H, DH)
    mp = stk("omp")
    mcp = stk("omcp")
    cvp = stk("ocvp")
    Cs = cat("oCs", 1)
    ns = cat("ons", 1).reshape(DEPTH, ncores * NS, H, DH)
    ms = cat("oms", 1)
    mcs = cat("omcs", 1).reshape(DEPTH, ncores * NS, MW - 1, D)
    cvs = cat("ocvs", 1).reshape(DEPTH, ncores * NS, CW - 1, D)
    outs = (y_prompt, y_sample, Cp, np_, mp, mcp, cvp, Cs, ns, ms, mcs, cvs)
    return tuple(np.ascontiguousarray(o, dtype=np.float32) for o in outs)


def kernel(**inputs):
    return run(inputs, SEQ=2048, DEPTH=4, NS=16)
```
